# Optimizing a Trainium2 kernel written in Bass

```python
import math
import jax, jax.numpy as jnp
from jax import lax
import numpy as np

D_MODEL = 1024
BATCH = 32
SEQ = 2048
DEPTH = 2

GRID_W = 64
CTX_LEN = 256
REC_W = 512
REC_BLOCKS = 8
REC_BLK = REC_W // REC_BLOCKS
REC_CONV = 4
LRU_C = 8.0
SC_W = 512
SC_CONV = 3
NA_HEADS = 16
NA_HEAD_DIM = 64
NA_W = NA_HEADS * NA_HEAD_DIM
NA_WIN_R = 8
NA_WIN_C = 16
D_FF = 2816
FFN_CONV = 3
N_EVEN = (DEPTH + 1) // 2
N_ODD = DEPTH // 2
EV_IN = 2 * REC_W + 3 * SC_W
ALPHA = (2.0 * DEPTH) ** 0.25
BETA = (8.0 * DEPTH) ** -0.25
LN_EPS = 1e-6
NEG_INF = -1e30

kernel_name = 'hybrid_rglru_shortconv_natten_convffn_deepnorm'


def layer_norm(x, g, b):
    xf = x.astype(jnp.float32)
    mu = jnp.mean(xf, axis=-1, keepdims=True)
    var = jnp.mean(jnp.square(xf - mu), axis=-1, keepdims=True)
    y = (xf - mu) * lax.rsqrt(var + LN_EPS)
    return (y * g.astype(jnp.float32) + b.astype(jnp.float32)).astype(x.dtype)


def modulate(x, shift, scale):
    return x * (1 + scale) + shift


def dwconv(x, w, b, pad_l, pad_r):
    out = lax.conv_general_dilated(
        x, w[:, None, :].astype(x.dtype), window_strides=(1,), padding=[(pad_l, pad_r)],
        dimension_numbers=('NWC', 'WIO', 'NWC'), feature_group_count=x.shape[-1])
    return out + b.astype(x.dtype)


def rglru_coeffs(xc, wa, ba, wx, bx, lam):
    B, L, _ = xc.shape
    xf = xc.astype(jnp.float32)
    xb = xf.reshape(B, L, REC_BLOCKS, REC_BLK)
    r = jax.nn.sigmoid(jnp.einsum('blni,nij->blnj', xb, wa.astype(jnp.float32)).reshape(B, L, REC_W) + ba)
    i = jax.nn.sigmoid(jnp.einsum('blni,nij->blnj', xb, wx.astype(jnp.float32)).reshape(B, L, REC_W) + bx)
    log_a = -LRU_C * r * jax.nn.softplus(-lam.astype(jnp.float32))
    a = jnp.exp(log_a)
    mult = jnp.sqrt(-jnp.expm1(2.0 * log_a))
    return a, mult * (i * xf)


def linear_scan(a, b, h0, reverse):
    if reverse:
        a, b = jnp.flip(a, 1), jnp.flip(b, 1)
    def combine(e1, e2):
        return e1[0] * e2[0], e2[0] * e1[1] + e2[1]
    a_cum, h = lax.associative_scan(combine, (a, b), axis=1)
    h = h + a_cum * h0[:, None, :]
    h_last = h[:, -1]
    if reverse:
        h = jnp.flip(h, 1)
    return h, h_last


def rec_branch(xr, conv_w, conv_b, wa, ba, wx, bx, lam, h0_f, h0_b):
    xc = dwconv(xr, conv_w, conv_b, REC_CONV // 2, REC_CONV - 1 - REC_CONV // 2)
    a_f, b_f = rglru_coeffs(xc, wa[0], ba[0], wx[0], bx[0], lam[0])
    a_b, b_b = rglru_coeffs(xc, wa[1], ba[1], wx[1], bx[1], lam[1])
    h_f, last_f = linear_scan(a_f, b_f, h0_f, False)
    h_b, last_b = linear_scan(a_b, b_b, h0_b, True)
    return (h_f + h_b).astype(xr.dtype), last_f, last_b


def even_mixer(hx, hc, w_in, w_out, rc_w, rc_b, wa, ba, wx, bx, lam, sc_w, sc_b, ctx_out):
    cuts = [REC_W, 2 * REC_W, 2 * REC_W + SC_W, 2 * REC_W + 2 * SC_W]
    xr_x, gr_x, sb_x, sg_x, sx_x = jnp.split(hx @ w_in, cuts, axis=-1)
    xr_c, gr_c, sb_c, sg_c, sx_c = jnp.split(hc @ w_in, cuts, axis=-1)
    B = hc.shape[0]
    zeros = jnp.zeros((B, REC_W), jnp.float32)
    rec_c, last_f, last_b = rec_branch(xr_c, rc_w, rc_b, wa, ba, wx, bx, lam, zeros, zeros)
    rec_x, _, _ = rec_branch(xr_x, rc_w, rc_b, wa, ba, wx, bx, lam, last_f, last_b)

    def merge(rec, gr, sb, sg, sx):
        y_rec = rec * jax.nn.gelu(gr)
        y_sc = sb * dwconv(sg * sx, sc_w, sc_b, SC_CONV // 2, SC_CONV // 2)
        return jnp.concatenate([y_rec, y_sc], axis=-1) @ w_out

    y_x = merge(rec_x, gr_x, sb_x, sg_x, sx_x)
    y_c = merge(rec_c, gr_c, sb_c, sg_c, sx_c) if ctx_out else None
    return y_x, y_c


def odd_mixer(hx, hc, w_qkv, w_out, rpb, ctx_out):
    B, S, _ = hx.shape
    L_c = hc.shape[1]
    rows = S // GRID_W
    win_r = min(NA_WIN_R, rows)
    scale = NA_HEAD_DIM ** -0.5
    qx, kx, vx = [t.reshape(B, S, NA_HEADS, NA_HEAD_DIM) for t in jnp.split(hx @ w_qkv, 3, axis=-1)]
    qc, kc, vc = [t.reshape(B, L_c, NA_HEADS, NA_HEAD_DIM) for t in jnp.split(hc @ w_qkv, 3, axis=-1)]
    kc_h = kc.transpose(0, 2, 1, 3)
    vc_h = vc.transpose(0, 2, 1, 3)
    k_g = kx.reshape(B, rows, GRID_W, NA_HEADS, NA_HEAD_DIM).transpose(0, 3, 1, 2, 4)
    v_g = vx.reshape(B, rows, GRID_W, NA_HEADS, NA_HEAD_DIM).transpose(0, 3, 1, 2, 4)
    q_rows = qx.reshape(B, rows, GRID_W, NA_HEADS, NA_HEAD_DIM).transpose(1, 0, 3, 2, 4)
    cols = jnp.arange(GRID_W)
    col_start = jnp.clip(cols - NA_WIN_C // 2, 0, GRID_W - NA_WIN_C)
    in_win = (cols[None, :] >= col_start[:, None]) & (cols[None, :] < col_start[:, None] + NA_WIN_C)
    dc = jnp.clip(cols[None, :] - cols[:, None], -(NA_WIN_C - 1), NA_WIN_C - 1) + NA_WIN_C - 1
    rpb32 = rpb.astype(jnp.float32)

    def row_block(args):
        q_r, r = args
        rs = jnp.clip(r - win_r // 2, 0, rows - win_r)
        k_b = lax.dynamic_slice_in_dim(k_g, rs, win_r, axis=2)
        v_b = lax.dynamic_slice_in_dim(v_g, rs, win_r, axis=2)
        dr = rs + jnp.arange(win_r) - r + NA_WIN_R - 1
        bias = rpb32[:, dr[None, :, None], dc[:, None, :]]
        s_loc = jnp.einsum('bhqd,bhrkd->bhqrk', q_r, k_b).astype(jnp.float32) * scale + bias[None]
        s_loc = jnp.where(in_win[:, None, :], s_loc, NEG_INF)
        s_ctx = jnp.einsum('bhqd,bhkd->bhqk', q_r, kc_h).astype(jnp.float32) * scale
        s_all = jnp.concatenate([s_loc.reshape(B, NA_HEADS, GRID_W, win_r * GRID_W), s_ctx], axis=-1)
        p = jax.nn.softmax(s_all, axis=-1).astype(v_b.dtype)
        p_loc = p[..., :win_r * GRID_W].reshape(B, NA_HEADS, GRID_W, win_r, GRID_W)
        p_ctx = p[..., win_r * GRID_W:]
        return jnp.einsum('bhqrk,bhrkd->bhqd', p_loc, v_b) + jnp.einsum('bhqk,bhkd->bhqd', p_ctx, vc_h)

    o = lax.map(row_block, (q_rows, jnp.arange(rows)))
    y_x = o.transpose(1, 0, 3, 2, 4).reshape(B, S, NA_W) @ w_out
    y_c = None
    if ctx_out:
        s = jnp.einsum('bqhd,bkhd->bhqk', qc, kc).astype(jnp.float32) * scale
        p = jax.nn.softmax(s, axis=-1).astype(vc.dtype)
        y_c = jnp.einsum('bhqk,bkhd->bqhd', p, vc).reshape(B, L_c, NA_W) @ w_out
    return y_x, y_c


def conv_ffn(h, w_gate, w_up, conv_w, conv_b, w_down):
    g = dwconv(h @ w_gate, conv_w, conv_b, FFN_CONV // 2, FFN_CONV // 2)
    return (jax.nn.silu(g) * (h @ w_up)) @ w_down


def setup_inputs(seed: int = 0) -> dict:
    key = jax.random.key(seed)
    ks = iter(jax.random.split(key, 40))
    f32 = jnp.float32

    def nrm(shape, scale):
        return jax.random.normal(next(ks), shape, f32) * scale

    u = jax.random.uniform(next(ks), (N_EVEN, 2, REC_W), f32, minval=0.9, maxval=0.999)
    a0 = u ** (1.0 / LRU_C)
    lam = jnp.log(a0) - jnp.log1p(-a0)
    return {
        'x': nrm((BATCH, SEQ, D_MODEL), 1.0),
        'c': nrm((BATCH, D_MODEL), 1.0),
        'ctx': nrm((BATCH, CTX_LEN, D_MODEL), 1.0),
        'c_ctx': nrm((D_MODEL,), 1.0),
        'mod_w': nrm((DEPTH, D_MODEL, 6 * D_MODEL), D_MODEL ** -0.5),
        'mod_b': nrm((DEPTH, 6 * D_MODEL), 0.01),
        'ln1_g': 1.0 + nrm((DEPTH, D_MODEL), 0.01),
        'ln1_b': nrm((DEPTH, D_MODEL), 0.01),
        'ln2_g': 1.0 + nrm((DEPTH, D_MODEL), 0.01),
        'ln2_b': nrm((DEPTH, D_MODEL), 0.01),
        'ev_w_in': nrm((N_EVEN, D_MODEL, EV_IN), D_MODEL ** -0.5),
        'ev_w_out': nrm((N_EVEN, REC_W + SC_W, D_MODEL), BETA * (REC_W + SC_W) ** -0.5),
        'rec_conv_w': nrm((N_EVEN, REC_CONV, REC_W), REC_CONV ** -0.5),
        'rec_conv_b': nrm((N_EVEN, REC_W), 0.01),
        'rec_wa': nrm((N_EVEN, 2, REC_BLOCKS, REC_BLK, REC_BLK), REC_BLK ** -0.5),
        'rec_ba': nrm((N_EVEN, 2, REC_W), 0.01),
        'rec_wx': nrm((N_EVEN, 2, REC_BLOCKS, REC_BLK, REC_BLK), REC_BLK ** -0.5),
        'rec_bx': nrm((N_EVEN, 2, REC_W), 0.01),
        'rec_lam': lam,
        'sc_conv_w': nrm((N_EVEN, SC_CONV, SC_W), SC_CONV ** -0.5),
        'sc_conv_b': nrm((N_EVEN, SC_W), 0.01),
        'na_w_qkv': nrm((N_ODD, D_MODEL, 3 * NA_W), D_MODEL ** -0.5),
        'na_w_out': nrm((N_ODD, NA_W, D_MODEL), BETA * NA_W ** -0.5),
        'na_rpb': nrm((N_ODD, NA_HEADS, 2 * NA_WIN_R - 1, 2 * NA_WIN_C - 1), 0.1),
        'ffn_w_gate': nrm((DEPTH, D_MODEL, D_FF), D_MODEL ** -0.5),
        'ffn_w_up': nrm((DEPTH, D_MODEL, D_FF), D_MODEL ** -0.5),
        'ffn_conv_w': nrm((DEPTH, FFN_CONV, D_FF), FFN_CONV ** -0.5),
        'ffn_conv_b': nrm((DEPTH, D_FF), 0.01),
        'ffn_w_down': nrm((DEPTH, D_FF, D_MODEL), BETA * D_FF ** -0.5),
    }


def reference(x, c, ctx, c_ctx, mod_w, mod_b, ln1_g, ln1_b, ln2_g, ln2_b, ev_w_in, ev_w_out,
              rec_conv_w, rec_conv_b, rec_wa, rec_ba, rec_wx, rec_bx, rec_lam, sc_conv_w, sc_conv_b,
              na_w_qkv, na_w_out, na_rpb, ffn_w_gate, ffn_w_up, ffn_conv_w, ffn_conv_b, ffn_w_down):
    sc_c = jax.nn.silu(c)
    sc_ctx = jax.nn.silu(c_ctx)[None, :]
    for i in range(DEPTH):
        last = i == DEPTH - 1
        j = i // 2
        m_x = [t[:, None, :] for t in jnp.split(sc_c @ mod_w[i] + mod_b[i], 6, axis=-1)]
        m_c = [t[:, None, :] for t in jnp.split(sc_ctx @ mod_w[i] + mod_b[i], 6, axis=-1)]
        hx = modulate(x, m_x[0], m_x[1])
        hc = modulate(ctx, m_c[0], m_c[1])
        if i % 2 == 0:
            y_x, y_c = even_mixer(hx, hc, ev_w_in[j], ev_w_out[j], rec_conv_w[j], rec_conv_b[j],
                                  rec_wa[j], rec_ba[j], rec_wx[j], rec_bx[j], rec_lam[j],
                                  sc_conv_w[j], sc_conv_b[j], not last)
        else:
            y_x, y_c = odd_mixer(hx, hc, na_w_qkv[j], na_w_out[j], na_rpb[j], not last)
        x = layer_norm(ALPHA * x + m_x[2] * y_x, ln1_g[i], ln1_b[i])
        f_x = conv_ffn(modulate(x, m_x[3], m_x[4]), ffn_w_gate[i], ffn_w_up[i], ffn_conv_w[i], ffn_conv_b[i], ffn_w_down[i])
        x = layer_norm(ALPHA * x + m_x[5] * f_x, ln2_g[i], ln2_b[i])
        if not last:
            ctx = layer_norm(ALPHA * ctx + m_c[2] * y_c, ln1_g[i], ln1_b[i])
            f_c = conv_ffn(modulate(ctx, m_c[3], m_c[4]), ffn_w_gate[i], ffn_w_up[i], ffn_conv_w[i], ffn_conv_b[i], ffn_w_down[i])
            ctx = layer_norm(ALPHA * ctx + m_c[5] * f_c, ln2_g[i], ln2_b[i])
    return x
```

```python
import numpy as np
import concourse.bass as bass
import concourse.mybir as mybir
from concourse.bass_utils import run_bass_kernel_spmd
from concourse.ap import AP

F32 = mybir.dt.float32
BF16 = mybir.dt.bfloat16
ALU = mybir.AluOpType
AF = mybir.ActivationFunctionType

NCORES = 8
D = 1024
KC = 8
SEQ = 2048
CT = 256
NT = SEQ + CT
DFF = 2816
FC = 22
ALPHA = 4.0 ** 0.25
EPS2 = 1e-6 / (ALPHA * ALPHA)
NEG = -30000.0


class T:
    __slots__ = ("ap", "w", "r")

    def __init__(self, ap):
        self.ap = ap
        self.w = None
        self.r = {}


class Eng:
    def __init__(self, name, h, sem):
        self.name = name
        self.h = h
        self.sem = sem
        self.count = 0
        self.waited = {}
        self.q = []


class Prog:
    def __init__(self, nc, n_dma_sems=8):
        self.nc = nc
        self._ctx = []
        self.engs = {}
        for nm, h in (("pe", nc.tensor), ("act", nc.scalar), ("dve", nc.vector), ("pool", nc.gpsimd), ("sp", nc.sync)):
            self.engs[nm] = Eng(nm, h, self._sem("e_" + nm))
        self.n_dma_sems = n_dma_sems
        self.dma_pool = {}
        for nm in ("sp", "pool"):
            self.dma_pool[nm] = dict(sems=[self._sem("d_%s%d" % (nm, i)) for i in range(n_dma_sems)], j=0)
        self.ninst = 0
        self.force_own = False

    def _sem(self, name):
        cm = self.nc.semaphore(name)
        s = cm.__enter__()
        self._ctx.append(cm)
        return (name, s)

    def sbuf(self, name, shape, dt):
        cm = self.nc.sbuf_tensor(name, shape, dt)
        t = cm.__enter__()
        self._ctx.append(cm)
        return t

    def psum(self, name, shape, dt):
        cm = self.nc.psum_tensor(name, shape, dt)
        t = cm.__enter__()
        self._ctx.append(cm)
        return t

    @staticmethod
    def _deps(reads, writes):
        deps = {}

        def add(k, v):
            if deps.get(k, 0) < v:
                deps[k] = v
        for t in reads:
            if t.w is not None:
                add(*t.w)
        for t in writes:
            if t.w is not None:
                add(*t.w)
            for k, v in t.r.items():
                add(k, v)
        return deps

    def _waits(self, e, deps, own_too):
        for (name, sh), v in deps.items():
            if (not own_too) and name == e.sem[0]:
                continue
            if e.waited.get(name, 0) >= v:
                continue
            e.waited[name] = v
            e.q.append(lambda h, sh=sh, v=v: h.wait_ge(sh, v))
            self.ninst += 1

    def op(self, eng, fn, reads=(), writes=(), sig=True, own=False):
        e = self.engs[eng]
        self._waits(e, self._deps(reads, writes), own or self.force_own)
        if sig:
            e.count += 1
            val = e.count
            sh = e.sem[1]
            e.q.append(lambda h, fn=fn, sh=sh: fn(h).then_inc(sh, 1))
        else:
            val = e.count + 1
            e.q.append(lambda h, fn=fn: fn(h))
        self.ninst += 1
        for t in reads:
            if t.r.get(e.sem, 0) < val:
                t.r[e.sem] = val
        for t in writes:
            t.w = (e.sem, val)
            t.r = {}

    def dma(self, q, out_ap, in_ap, reads=(), writes=()):
        e = self.engs[q]
        pool = self.dma_pool[q]
        j = pool["j"]
        pool["j"] += 1
        s = pool["sems"][j % self.n_dma_sems]
        gen = j // self.n_dma_sems
        deps = self._deps(reads, writes)
        if gen > 0 and deps.get(s, 0) < 16 * gen:
            deps[s] = 16 * gen
        self._waits(e, deps, True)
        val = 16 * (gen + 1)
        sh = s[1]
        e.q.append(lambda h, o=out_ap, i=in_ap, sh=sh: h.dma_start(out=o, in_=i).then_inc(sh, 16))
        self.ninst += 1
        for t in reads:
            if t.r.get(s, 0) < val:
                t.r[s] = val
        for t in writes:
            t.w = (s, val)
            t.r = {}

    def realias(self, new_ts, old_ts):
        merged = {}
        for t in old_ts:
            if t.w is not None and merged.get(t.w[0], 0) < t.w[1]:
                merged[t.w[0]] = t.w[1]
            for k, v in t.r.items():
                if merged.get(k, 0) < v:
                    merged[k] = v
        for t in new_ts:
            t.w = None
            t.r = dict(merged)

    def wait_all(self, eng, ts):
        e = self.engs[eng]
        self._waits(e, self._deps(ts, ts), True)

    def finish(self):
        nc = self.nc
        with nc.Block() as block:
            def mk(e):
                def f(h):
                    for c in e.q:
                        c(h)
                return f
            block.tensor(mk(self.engs["pe"]))
            block.scalar(mk(self.engs["act"]))
            block.vector(mk(self.engs["dve"]))
            block.gpsimd(mk(self.engs["pool"]))
            block.sync(mk(self.engs["sp"]))
        for cm in reversed(self._ctx):
            cm.__exit__(None, None, None)
        self._ctx = []


def MM(P, psT, out, lhsT, rhs, start, stop, reads):
    P.op("pe", lambda h: h.matmul(out, lhsT, rhs, start=start, stop=stop), reads=reads, writes=[psT], sig=True)


def ACT(P, out, in_, func, reads, writes, bias=None, scale=None):
    kw = {}
    if bias is not None:
        kw["bias"] = bias
    if scale is not None:
        kw["scale"] = scale
    P.op("act", lambda h: h.activation(out, in_, func, **kw), reads=reads, writes=writes)


def TT(P, eng, out, in0, in1, op, reads, writes):
    P.op(eng, lambda h: h.tensor_tensor(out, in0, in1, op), reads=reads, writes=writes)


def TS(P, eng, out, in0, s1, s2, op0, op1, reads, writes):
    if s2 is None:
        P.op(eng, lambda h: h.tensor_scalar(out, in0, s1, None, op0), reads=reads, writes=writes)
    else:
        P.op(eng, lambda h: h.tensor_scalar(out, in0, s1, s2, op0, op1), reads=reads, writes=writes)


def STT(P, eng, out, in0, sc, in1, op0, op1, reads, writes):
    P.op(eng, lambda h: h.scalar_tensor_tensor(out, in0, sc, in1, op0, op1), reads=reads, writes=writes)


def CP(P, eng, out, in_, reads, writes):
    if eng == "act":
        P.op("act", lambda h: h.copy(out, in_), reads=reads, writes=writes)
    else:
        P.op(eng, lambda h: h.tensor_copy(out, in_), reads=reads, writes=writes)


def SCAN(P, out, d0, d1, init, reads, writes):
    P.op("dve", lambda h: h.tensor_tensor_scan(out, d0, d1, init, ALU.mult, ALU.add), reads=reads, writes=writes)


def rev(ap2d):
    n = ap2d.shape[1]
    last = ap2d[:, n - 1:n]
    return AP(last.tensor, last.offset, [list(last.ap[0]), [-1, n]])


def pieces(lo, hi, step=512):
    n = -(-(hi - lo) // step)
    size = -(-(hi - lo) // n)
    size += size % 2
    out = []
    c = lo
    while c < hi:
        out.append((c, min(hi, c + size)))
        c += size
    return out


def small_layout():
    off = {}
    o = 0
    for nm, n in (("modb", 96), ("ln1g", 16), ("ln1b", 16), ("ln2g", 16), ("ln2b", 16), ("rcw", 16), ("rcb", 4),
                  ("rba", 8), ("rbx", 8), ("rlam", 8), ("scw", 12), ("scb", 4), ("fcw", 132), ("fcb", 44)):
        off[nm] = o
        o += n
    return off, o


SOFF, NS = small_layout()


def build_program(NB, layers=2, stop=None, order=None):
    nc = bass.Bass("TRN2", target_bir_lowering=False)
    NJ = NB + 1

    def din(name, shape, dt=F32):
        return nc.dram_tensor(name, list(shape), dt, kind="ExternalInput").ap()

    xT = din("xT", [NB, KC, 128, SEQ])
    ctxT = din("ctxT", [NB, KC, 128, CT])
    cT = din("cT", [128, KC * NJ])
    smallp = din("smallp", [128, NS])
    modw = din("modw", [2 * 48 * 128, 1024])
    gw_in = din("gw", [128, 16 * 128])
    tmid = din("tmid", [16 * 128, 896])
    tfull = din("tfull", [16 * 128, 896])
    wspec = [("win", 20, 1024), ("wout0", 8, 1024), ("wg0", FC, 1024), ("wu0", FC, 1024), ("wd0", 8, DFF),
             ("wqkv", 24, 1024), ("wout1", 8, 1024), ("wg1", FC, 1024), ("wu1", FC, 1024), ("wd1", 8, DFF)]
    wsrc = {}
    wscr = {}
    wscrT = {}
    for nm, nmc, ncol in wspec:
        wsrc[nm] = din(nm, [nmc * 128, ncol])
        wscr[nm] = nc.dram_tensor(nm + "_s", [nmc * 128, ncol], BF16, kind="Internal").ap()
        wscrT[nm] = [T(wscr[nm][m * 128:(m + 1) * 128, :]) for m in range(nmc)]
    yT = nc.dram_tensor("yT", [NB, KC, 128, SEQ], F32, kind="ExternalOutput").ap()

    P = Prog(nc)
    Xs = P.sbuf("X", [128, KC * NT], F32)
    Hs = P.sbuf("H", [128, KC * NT], BF16)
    UBYTES = 81920
    Us = P.sbuf("U", [128, UBYTES // 4], F32)
    X3 = Xs[:].rearrange("p (c t) -> p c t", c=KC)
    H3 = Hs[:].rearrange("p (c t) -> p c t", c=KC)
    REG = [(0, CT), (CT, CT + 1024), (CT + 1024, NT)]
    XR = [[T(X3[:, c, lo:hi]) for (lo, hi) in REG] for c in range(KC)]
    HR = [[T(H3[:, c, lo:hi]) for (lo, hi) in REG] for c in range(KC)]

    def xt(mc, c0, c1):
        return [XR[mc][r] for r, (lo, hi) in enumerate(REG) if c0 < hi and c1 > lo]

    def ht1(mc, c0, c1):
        return [HR[mc][r] for r, (lo, hi) in enumerate(REG) if c0 < hi and c1 > lo]

    def ht(c0, c1):
        return [t for mc in range(KC) for t in ht1(mc, c0, c1)]
    WA = [T(P.sbuf("wa%d" % i, [128, 1024], BF16)[:]) for i in range(3)]
    wa_i = [0]
    smallT = T(P.sbuf("smallp_sb", [128, NS], F32)[:])
    cvec = T(P.sbuf("cvec", [128, 4], F32)[:])
    sttA = T(P.sbuf("sttA", [128, 2], F32)[:])
    sttB = T(P.sbuf("sttB", [128, 2], F32)[:])
    onesD = T(P.sbuf("onesD", [128, 128], F32)[:])
    onesB = T(P.sbuf("onesB", [128, 128], BF16)[:])
    scT = T(P.sbuf("scT", [128, KC * NJ], F32)[:])
    MODS = T(P.sbuf("MODS", [128, 2 * NJ * 48], F32)[:])
    DER = T(P.sbuf("DER", [128, 12 * NJ * 8], F32)[:])
    CLAM = T(P.sbuf("CLAM", [128, 8], F32)[:])
    GW = T(P.sbuf("GW", [128, 16 * 128], BF16)[:])
    banks = [T(P.psum("ps%d" % i, [128, 512], F32)[:]) for i in range(8)]
    bank_i = [0]

    def nb_():
        t = banks[bank_i[0] % 8]
        bank_i[0] += 1
        return t

    def ucarve(off_bytes, n, dt):
        if dt == F32:
            return Us[:, off_bytes // 4: off_bytes // 4 + n]
        return Us[:, off_bytes // 4: off_bytes // 4 + (n + 1) // 2].bitcast(BF16)[:, :n]

    u_live = []

    def uphase(specs):
        new = [T(ucarve(o, n, dt)) for (o, n, dt) in specs]
        P.realias(new, u_live)
        u_live[:] = new
        return new

    sm = smallT.ap

    def scol(nm, i):
        o = SOFF[nm] + i
        return sm[:, o:o + 1]

    def mods(l, j, q):
        o = (l * NJ + j) * 48 + q
        return MODS.ap[:, o:o + 1]

    def der(k, j, c):
        o = (k * NJ + j) * 8 + c
        return DER.ap[:, o:o + 1]

    worder = list(order) if order is not None else []
    wpos = {k: i for i, k in enumerate(worder)}
    wstate = dict(done=0, seen=[])
    LOOK = 12

    def prepass_upto(k):
        while wstate["done"] < min(k, len(worder)):
            nm, m = worder[wstate["done"]]
            P.dma("pool", wscr[nm][m * 128:(m + 1) * 128, :], wsrc[nm][m * 128:(m + 1) * 128, :], writes=[wscrT[nm][m]])
            wstate["done"] += 1
    if order is None:
        for nm, nmc, ncol in wspec:
            if layers == 1 and nm in ("wqkv", "wout1", "wg1", "wu1", "wd1"):
                continue
            for m in range(nmc):
                P.dma("pool", wscr[nm][m * 128:(m + 1) * 128, :], wsrc[nm][m * 128:(m + 1) * 128, :], writes=[wscrT[nm][m]])
    P.dma("sp", smallT.ap, smallp, writes=[smallT])
    P.dma("sp", scT.ap, cT, writes=[scT])
    P.dma("pool", GW.ap, gw_in, writes=[GW])
    prepass_upto(LOOK)
    P.op("dve", lambda h: h.memset(cvec.ap[:, 0:1], 1.0), writes=[cvec])
    P.op("dve", lambda h: h.memset(cvec.ap[:, 1:2], EPS2), writes=[cvec])
    P.op("dve", lambda h: h.memset(cvec.ap[:, 2:3], 0.0), writes=[cvec])
    P.op("dve", lambda h: h.memset(onesD.ap, 1.0 / D), writes=[onesD])
    P.op("dve", lambda h: h.memset(onesB.ap, 1.0), writes=[onesB])
    ONE = cvec.ap[:, 0:1]
    EPSc = cvec.ap[:, 1:2]
    ACT(P, scT.ap, scT.ap, AF.Silu, [scT], [scT])
    ACT(P, CLAM.ap, sm[:, SOFF["rlam"]:SOFF["rlam"] + 8], AF.Exp, [smallT], [CLAM], scale=-1.0)
    TS(P, "dve", CLAM.ap, CLAM.ap, 1.0, None, ALU.add, None, [CLAM], [CLAM])
    ACT(P, CLAM.ap, CLAM.ap, AF.Ln, [CLAM], [CLAM])
    TS(P, "dve", CLAM.ap, CLAM.ap, -8.0, None, ALU.mult, None, [CLAM], [CLAM])
    mwb = uphase([(i * 4096, 1024, F32) for i in range(3)])
    sc3 = scT.ap.rearrange("p (c j) -> p c j", c=KC)
    for l in range(layers):
        for q in range(48):
            wt = mwb[(l * 48 + q) % 3]
            r0 = (l * 48 + q) * 128
            P.dma("sp", wt.ap, modw[r0:r0 + 128, :], writes=[wt])
            ps = nb_()
            w3 = wt.ap.rearrange("p (k m) -> p k m", k=KC)
            for kc in range(KC):
                MM(P, ps, ps.ap[:, 0:NJ], w3[:, kc, :], sc3[:, kc, :], kc == 0, kc == KC - 1, [wt, scT])
            o0 = l * NJ * 48 + q
            outap = AP(MODS.ap.tensor, MODS.ap[:, o0:o0 + 1].offset, [list(MODS.ap.ap[0]), [48, NJ]])
            TS(P, "dve", outap, ps.ap[:, 0:NJ], scol("modb", l * 48 + q), None, ALU.add, None, [ps, smallT], [MODS])
    for l in range(layers):
        for j in range(NJ):
            def M8(which):
                o = (l * NJ + j) * 48 + which * 8
                return MODS.ap[:, o:o + 8]

            def D8(k):
                o = (k * NJ + j) * 8
                return DER.ap[:, o:o + 8]
            ln1g = sm[:, SOFF["ln1g"] + l * 8: SOFF["ln1g"] + l * 8 + 8]
            ln1b = sm[:, SOFF["ln1b"] + l * 8: SOFF["ln1b"] + l * 8 + 8]
            TS(P, "dve", D8(0 + l), M8(1), 1.0, None, ALU.add, None, [MODS], [DER])
            TS(P, "dve", D8(2 + l), M8(2), 1.0 / ALPHA, None, ALU.mult, None, [MODS], [DER])
            TS(P, "dve", D8(4 + l), M8(5), 1.0 / ALPHA, None, ALU.mult, None, [MODS], [DER])
            TS(P, "dve", D8(8 + l), M8(4), 1.0, None, ALU.add, None, [MODS], [DER])
            TT(P, "pool", D8(6 + l), D8(8 + l), ln1g, ALU.mult, [DER, smallT], [DER])
            TT(P, "pool", D8(8 + l), D8(8 + l), ln1b, ALU.mult, [DER, smallT], [DER])
            TT(P, "dve", D8(8 + l), D8(8 + l), M8(3), ALU.add, [DER, MODS], [DER])
    if layers == 2:
        for j in range(NJ):
            ln2g = sm[:, SOFF["ln2g"]: SOFF["ln2g"] + 8]
            ln2b = sm[:, SOFF["ln2b"]: SOFF["ln2b"] + 8]
            A1n = DER.ap[:, (1 * NJ + j) * 8:(1 * NJ + j) * 8 + 8]
            SH1n = MODS.ap[:, (1 * NJ + j) * 48: (1 * NJ + j) * 48 + 8]
            g1p = DER.ap[:, (10 * NJ + j) * 8:(10 * NJ + j) * 8 + 8]
            b1p = DER.ap[:, (11 * NJ + j) * 8:(11 * NJ + j) * 8 + 8]
            TT(P, "pool", g1p, A1n, ln2g, ALU.mult, [DER, smallT], [DER])
            TT(P, "pool", b1p, A1n, ln2b, ALU.mult, [DER, smallT], [DER])
            TT(P, "dve", b1p, b1p, SH1n, ALU.add, [DER, MODS], [DER])

    def load_w(nm, m, ncol=1024, buf=None):
        if (nm, m) not in wstate["seen"]:
            wstate["seen"].append((nm, m))
        if order is not None:
            prepass_upto(wpos[(nm, m)] + 1 + LOOK)
        if buf is None:
            buf = WA[wa_i[0] % 3]
            wa_i[0] += 1
        P.dma("sp", buf.ap[:, 0:ncol], wscr[nm][m * 128:(m + 1) * 128, :], reads=[wscrT[nm][m]], writes=[buf])
        return buf

    def proj_fm(wt, rhs3, rhsT, c0, c1, nk=KC):
        ps = nb_()
        w3 = wt.ap[:, 0:nk * 128].rearrange("p (k m) -> p k m", k=nk)
        for kc in range(nk):
            MM(P, ps, ps.ap[:, 0:c1 - c0], w3[:, kc, :], rhs3[:, kc, c0:c1], kc == 0, kc == nk - 1, [wt] + rhsT)
        return ps

    def seqs_of(b, with_ctx):
        s = []
        if with_ctx:
            s.append((0, CT, NB))
        s.append((CT, NT, b))
        return s

    def layer_norm_piece(c0, c1, j, l, which, lnT, nextmod, hskip=0):
        n = c1 - c0
        SQ, MS, RS = lnT
        psm = nb_()
        pse = nb_()
        for mc in range(KC):
            sq = SQ[mc % 2]
            ACT(P, sq.ap[:, 0:n], X3[:, mc, c0:c1], AF.Square, xt(mc, c0, c1), [sq])
            MM(P, psm, psm.ap[:, 0:n], onesD.ap, X3[:, mc, c0:c1], mc == 0, mc == KC - 1, [onesD] + xt(mc, c0, c1))
            MM(P, pse, pse.ap[:, 0:n], onesD.ap, sq.ap[:, 0:n], mc == 0, mc == KC - 1, [onesD, sq])
        ACT(P, MS.ap[:, 0:n], psm.ap[:, 0:n], AF.Square, [psm], [MS])
        TT(P, "dve", MS.ap[:, 0:n], pse.ap[:, 0:n], MS.ap[:, 0:n], ALU.subtract, [pse, MS], [MS])
        ACT(P, MS.ap[:, 0:n], MS.ap[:, 0:n], AF.Sqrt, [MS, cvec], [MS], bias=EPSc, scale=1.0)
        P.op("dve", lambda h: h.reciprocal(RS.ap[:, 0:n], MS.ap[:, 0:n]), reads=[MS], writes=[RS])
        gname = "ln1g" if which == 1 else "ln2g"
        bname = "ln1b" if which == 1 else "ln2b"
        for mc in range(KC):
            xs = X3[:, mc, c0:c1]
            TT(P, "dve", xs, xs, psm.ap[:, 0:n], ALU.subtract, xt(mc, c0, c1) + [psm], xt(mc, c0, c1))
            TT(P, "pool", xs, xs, RS.ap[:, 0:n], ALU.mult, xt(mc, c0, c1) + [RS], xt(mc, c0, c1))
            if nextmod is not None:
                gk, bk = nextmod
                ACT(P, H3[:, mc, c0:c1 - hskip], X3[:, mc, c0:c1 - hskip], AF.Identity, xt(mc, c0, c1) + [DER], ht1(mc, c0, c1), bias=der(bk, j, mc), scale=der(gk, j, mc))
            ACT(P, xs, xs, AF.Identity, xt(mc, c0, c1) + [smallT], xt(mc, c0, c1), bias=scol(bname, l * 8 + mc), scale=scol(gname, l * 8 + mc))

    def proj_resid_ln(b, l, which, wname, nk, rhs3, rhsT, seqs, col_shift, lnT, nextmod, wbufs=None, mc_order=None):
        gk = (2 if which == 1 else 4) + l
        for (lo, hi, j) in seqs:
            for (c0, c1) in pieces(lo, hi):
                for mc in (mc_order or range(KC)):
                    if wbufs is None:
                        wt = load_w(wname, mc)
                    else:
                        wt = load_w(wname, mc, ncol=nk * 128, buf=wbufs[mc % 2])
                    ps = nb_()
                    w3 = wt.ap[:, 0:nk * 128].rearrange("p (k m) -> p k m", k=nk)
                    for kc in range(nk):
                        MM(P, ps, ps.ap[:, 0:c1 - c0], w3[:, kc, :], rhs3[:, kc, c0 - col_shift:c1 - col_shift],
                           kc == 0, kc == nk - 1, [wt] + rhsT)
                    xs = X3[:, mc, c0:c1]
                    STT(P, "dve", xs, ps.ap[:, 0:c1 - c0], der(gk, j, mc), xs, ALU.mult, ALU.add, [ps, DER] + xt(mc, c0, c1), xt(mc, c0, c1))
                layer_norm_piece(c0, c1, j, l, which, lnT, nextmod)

    def conv_seq(dst, src, dstT, srcT, lo, hi, dlo, slo, wname, wbase, K, padl, bias_ap, seq_lo, seq_hi, eng="dve", center="dve"):
        ctr = padl
        if center == "act":
            ACT(P, dst[:, lo - dlo:hi - dlo], src[:, lo - slo:hi - slo], AF.Identity, [srcT, smallT], [dstT], bias=bias_ap, scale=scol(wname, wbase + ctr))
        else:
            TS(P, eng, dst[:, lo - dlo:hi - dlo], src[:, lo - slo:hi - slo], scol(wname, wbase + ctr), bias_ap, ALU.mult, ALU.add,
               [srcT, smallT], [dstT])
        for k in range(K):
            if k == ctr:
                continue
            o = k - padl
            t0 = max(lo, seq_lo - o)
            t1 = min(hi, seq_hi - o)
            if t1 <= t0:
                continue
            STT(P, eng, dst[:, t0 - dlo:t1 - dlo], src[:, t0 + o - slo:t1 + o - slo], scol(wname, wbase + k), dst[:, t0 - dlo:t1 - dlo],
                ALU.mult, ALU.add, [srcT, smallT, dstT], [dstT])

    def ffn_stage(b, l, seqs_super, nextmod):
        specs = [(fc * 2048, 1024, BF16) for fc in range(FC)]
        specs += [(45056, 1056, F32), (49280, 1024, F32), (53376, DFF, BF16), (59008, DFF, BF16)]
        specs += [(64640 + i * 2048, 512, F32) for i in range(4)]
        ts = uphase(specs)
        A = ts[0:FC]
        G, CV = ts[FC], ts[FC + 1]
        WD = ts[FC + 2:FC + 4]
        lnT = (ts[FC + 4:FC + 6], ts[FC + 6], ts[FC + 7])
        wg, wu, wd = "wg%d" % l, "wu%d" % l, "wd%d" % l
        patch = []
        for (lo, hi, j, seq_lo, seq_hi) in seqs_super:
            glo = max(seq_lo, lo - 1)
            ghi = min(seq_hi, hi + 1)
            for fc in range(FC):
                wtg = load_w(wg, fc)
                for (c0, c1) in pieces(glo, ghi):
                    ps = proj_fm(wtg, H3, ht(c0, c1), c0, c1)
                    CP(P, "act", G.ap[:, c0 - glo:c1 - glo], ps.ap[:, 0:c1 - c0], [ps], [G])
                conv_seq(CV.ap, G.ap, CV, G, lo, hi, lo, glo, "fcw", (l * FC + fc) * 3, 3, 1, scol("fcb", l * FC + fc), seq_lo, seq_hi)
                ACT(P, CV.ap[:, 0:hi - lo], CV.ap[:, 0:hi - lo], AF.Silu, [CV], [CV])
                wtu = load_w(wu, fc)
                for (c0, c1) in pieces(lo, hi):
                    ps = proj_fm(wtu, H3, ht(c0, c1), c0, c1)
                    TT(P, "dve", A[fc].ap[:, c0 - lo:c1 - lo], ps.ap[:, 0:c1 - c0], CV.ap[:, c0 - lo:c1 - lo], ALU.mult, [ps, CV], [A[fc]])
            A3 = AP(A[0].ap.tensor, A[0].ap.offset, [list(A[0].ap.ap[0]), [1024, FC], [1, 1024]])
            gk = 4 + l
            for (c0, c1) in pieces(lo, hi):
                for mc in range(KC):
                    wt = load_w(wd, mc, ncol=DFF, buf=WD[mc % 2])
                    ps = nb_()
                    w3 = wt.ap.rearrange("p (k m) -> p k m", k=FC)
                    for kc in range(FC):
                        MM(P, ps, ps.ap[:, 0:c1 - c0], w3[:, kc, :], A3[:, kc, c0 - lo:c1 - lo], kc == 0, kc == FC - 1, [wt, A[kc]])
                    xs = X3[:, mc, c0:c1]
                    STT(P, "dve", xs, ps.ap[:, 0:c1 - c0], der(gk, j, mc), xs, ALU.mult, ALU.add, [ps, DER] + xt(mc, c0, c1), xt(mc, c0, c1))
                more = any((s2[3] == seq_lo and s2[0] == hi) for s2 in seqs_super)
                hs = 1 if (nextmod is not None and c1 == hi and more) else 0
                layer_norm_piece(c0, c1, j, l, 2, lnT, nextmod, hskip=hs)
                if l == 1:
                    for mc in range(KC):
                        P.dma("sp" if mc % 2 == 0 else "pool", yT[b, mc][:, c0 - CT:c1 - CT], X3[:, mc, c0:c1], reads=xt(mc, c0, c1))
                if hs:
                    patch.append((hi - 1, j))
        if patch:
            patch_h(patch, l + 1)

    def patch_h(patch, lnext):
        for (col, j) in patch:
            for mc in range(KC):
                ACT(P, H3[:, mc, col:col + 1], X3[:, mc, col:col + 1], AF.Identity, xt(mc, col, col + 1) + [DER, MODS], ht1(mc, col, col + 1),
                    bias=mods(lnext, j, mc), scale=der(0 + lnext, j, mc))

    def even_mixer(b):
        specs = [(c * 4608, NT, BF16) for c in range(KC)]
        specs += [(36864 + i * 9216, NT, F32) for i in range(3)]
        specs += [(64512, NT, BF16)]
        specs += [(69120 + i * 2048, 512, F32) for i in range(4)]
        ts = uphase(specs)
        YC = ts[0:KC]
        setA = (ts[KC], ts[KC + 1], ts[KC + 2], ts[KC + 3], ts[KC + 4:KC + 7], sttA)
        TM = ts[KC + 4:KC + 8]
        def xcarve(off, n, dt):
            if dt == F32:
                return Xs[:, off // 4: off // 4 + n]
            return Xs[:, off // 4: off // 4 + (n + 1) // 2].bitcast(BF16)[:, :n]
        xs_specs = [(i * 9216, NT, F32) for i in range(3)] + [(27648, NT, BF16)] + [(32256 + i * 2048, 512, F32) for i in range(2)]
        xs_specs += [(36864, NT, F32), (46080, NT, F32)]
        xsT = [T(xcarve(o, n, dt)) for (o, n, dt) in xs_specs]
        borrowB = [t for c in range(4) for t in XR[c]]
        borrowS = [t for c in (4, 5) for t in XR[c]]
        P.realias(xsT[0:6], borrowB)
        P.realias(xsT[6:8], borrowS)
        setB = (xsT[0], xsT[1], xsT[2], xsT[3], xsT[4:6], sttB)
        S0, S1 = xsT[6], xsT[7]

        def reload_x(chunks):
            for r, (lo, hi) in enumerate(REG):
                for c in chunks:
                    q = "sp" if c % 2 == 0 else "pool"
                    if r == 0:
                        P.dma(q, X3[:, c, 0:CT], ctxT[b, c], writes=[XR[c][0]])
                    else:
                        P.dma(q, X3[:, c, lo:hi], xT[b, c][:, lo - CT:hi - CT], writes=[XR[c][r]])
        seqs = seqs_of(b, True)
        allp = [pc for (lo, hi, j) in seqs for pc in pieces(lo, hi)]

        def rec_chunk(c, bufs, wbuf):
            R0, R1, R2, XCb, TMs, st = bufs
            tm_i = [0]

            def ntmp():
                t = TMs[tm_i[0] % len(TMs)]
                tm_i[0] += 1
                return t
            wt = load_w("win", c, buf=wbuf)
            for (c0, c1) in allp:
                ps = proj_fm(wt, H3, ht(c0, c1), c0, c1)
                CP(P, "act", R0.ap[:, c0:c1], ps.ap[:, 0:c1 - c0], [ps], [R0])
                yield
            for (lo, hi, j) in seqs:
                conv_seq(R1.ap, R0.ap, R1, R0, lo, hi, 0, 0, "rcw", c * 4, 4, 2, scol("rcb", c), lo, hi, center="act")
            yield
            CP(P, "pool", XCb.ap, R1.ap, [R1], [XCb])
            yield
            for d in range(2):
                Bd = R2 if d == 0 else R1
                for (c0, c1) in allp:
                    n = c1 - c0
                    psr = nb_()
                    MM(P, psr, psr.ap[:, 0:n], GW.ap[:, ((d * 2 + 0) * 4 + c) * 128:((d * 2 + 0) * 4 + c + 1) * 128], XCb.ap[:, c0:c1], True, True, [GW, XCb])
                    psi = nb_()
                    MM(P, psi, psi.ap[:, 0:n], GW.ap[:, ((d * 2 + 1) * 4 + c) * 128:((d * 2 + 1) * 4 + c + 1) * 128], XCb.ap[:, c0:c1], True, True, [GW, XCb])
                    ACT(P, R0.ap[:, c0:c1], psr.ap[:, 0:n], AF.Sigmoid, [psr, smallT], [R0], bias=scol("rba", d * 4 + c))
                    t1 = ntmp()
                    ACT(P, t1.ap[:, 0:n], psi.ap[:, 0:n], AF.Sigmoid, [psi, smallT], [t1], bias=scol("rbx", d * 4 + c))
                    if d == 0:
                        TT(P, "dve", Bd.ap[:, c0:c1], t1.ap[:, 0:n], R1.ap[:, c0:c1], ALU.mult, [t1, R1], [Bd])
                    else:
                        TT(P, "dve", Bd.ap[:, c0:c1], R1.ap[:, c0:c1], t1.ap[:, 0:n], ALU.mult, [t1, R1], [Bd])
                    yield
                ACT(P, R0.ap, R0.ap, AF.Exp, [R0, CLAM], [R0], scale=CLAM.ap[:, d * 4 + c:d * 4 + c + 1])
                yield
                for (c0, c1) in allp:
                    n = c1 - c0
                    a_ = R0.ap[:, c0:c1]
                    t2 = ntmp()
                    TT(P, "pool", t2.ap[:, 0:n], a_, a_, ALU.mult, [R0], [t2])
                    ACT(P, t2.ap[:, 0:n], t2.ap[:, 0:n], AF.Sqrt, [t2, cvec], [t2], bias=ONE, scale=-1.0)
                    TT(P, "pool", Bd.ap[:, c0:c1], Bd.ap[:, c0:c1], t2.ap[:, 0:n], ALU.mult, [Bd, t2], [Bd])
                    yield
                if d == 0:
                    SCAN(P, Bd.ap[:, 0:CT], R0.ap[:, 0:CT], Bd.ap[:, 0:CT], 0.0, [R0, Bd], [Bd])
                    CP(P, "act", st.ap[:, 0:1], Bd.ap[:, CT - 1:CT], [Bd], [st])
                    yield
                    SCAN(P, Bd.ap[:, CT:NT], R0.ap[:, CT:NT], Bd.ap[:, CT:NT], st.ap[:, 0:1], [R0, Bd, st], [Bd])
                else:
                    SCAN(P, rev(Bd.ap[:, 0:CT]), rev(R0.ap[:, 0:CT]), rev(Bd.ap[:, 0:CT]), 0.0, [R0, Bd], [Bd])
                    CP(P, "act", st.ap[:, 1:2], Bd.ap[:, 0:1], [Bd], [st])
                    yield
                    SCAN(P, rev(Bd.ap[:, CT:NT]), rev(R0.ap[:, CT:NT]), rev(Bd.ap[:, CT:NT]), st.ap[:, 1:2], [R0, Bd, st], [Bd])
                yield
            TT(P, "pool", R2.ap, R2.ap, R1.ap, ALU.add, [R2, R1], [R2])
            yield
            wt = load_w("win", 4 + c, buf=wbuf)
            for (c0, c1) in allp:
                n = c1 - c0
                ps = proj_fm(wt, H3, ht(c0, c1), c0, c1)
                t1 = ntmp()
                ACT(P, t1.ap[:, 0:n], ps.ap[:, 0:n], AF.Gelu_apprx_tanh, [ps], [t1])
                TT(P, "pool", YC[c].ap[:, c0:c1], t1.ap[:, 0:n], R2.ap[:, c0:c1], ALU.mult, [t1, R2], [YC[c]])
                yield

        def sc_chunk(c, wbuf):
            wt = load_w("win", 12 + c, buf=wbuf)
            for (c0, c1) in allp:
                ps = proj_fm(wt, H3, ht(c0, c1), c0, c1)
                CP(P, "act", S0.ap[:, c0:c1], ps.ap[:, 0:c1 - c0], [ps], [S0])
                yield
            wt = load_w("win", 16 + c, buf=wbuf)
            for (c0, c1) in allp:
                ps = proj_fm(wt, H3, ht(c0, c1), c0, c1)
                TT(P, "dve", S0.ap[:, c0:c1], ps.ap[:, 0:c1 - c0], S0.ap[:, c0:c1], ALU.mult, [ps, S0], [S0])
                yield
            for (lo, hi, j) in seqs:
                conv_seq(S1.ap, S0.ap, S1, S0, lo, hi, 0, 0, "scw", c * 3, 3, 1, scol("scb", c), lo, hi, center="act")
            yield
            wt = load_w("win", 8 + c, buf=wbuf)
            for (c0, c1) in allp:
                ps = proj_fm(wt, H3, ht(c0, c1), c0, c1)
                TT(P, "dve", YC[4 + c].ap[:, c0:c1], ps.ap[:, 0:c1 - c0], S1.ap[:, c0:c1], ALU.mult, [ps, S1], [YC[4 + c]])
                yield

        def chain(gs):
            for g in gs:
                yield from g
        gens = [("a", chain([rec_chunk(0, setA, WA[0]), rec_chunk(2, setA, WA[0])])),
                ("b", chain([rec_chunk(1, setB, WA[1]), rec_chunk(3, setB, WA[1])])),
                ("s", chain([sc_chunk(c, WA[2]) for c in range(4)]))]
        while gens:
            for item in list(gens):
                try:
                    next(item[1])
                except StopIteration:
                    gens.remove(item)
                    if item[0] == "s":
                        P.realias(borrowS, xsT[6:8])
                        reload_x([4, 5])
                    elif item[0] == "b":
                        P.realias(borrowB, xsT[0:6])
                        reload_x([0, 1, 2, 3])
        YC3 = AP(YC[0].ap.tensor, YC[0].ap.offset, [list(YC[0].ap.ap[0]), [NT, KC], [1, NT]])
        lnT = ([TM[0], TM[1]], TM[2], TM[3])
        proj_resid_ln(b, 0, 1, "wout0", KC, YC3, YC, seqs, 0, lnT, (6, 8), mc_order=[7, 6, 5, 4, 3, 2, 1, 0])

    def odd_mixer(b):
        specs = [(c * 4096, SEQ, BF16) for c in range(KC)]
        specs += [(32768, SEQ, BF16), (36864, NT, BF16), (41472, 18 * 192, BF16)]
        specs += [(48384 + i * 3584, 896, F32) for i in range(4)]
        specs += [(62720 + i * 2048, 512, F32) for i in range(3)]
        specs += [(68864 + i * 1024, 512, BF16) for i in range(8)]
        specs += [(77056 + i * 1024, 256, F32) for i in range(3)]
        ts = uphase(specs)
        OT = ts[0:KC]
        QT, KT, V = ts[KC:KC + 3]
        TB = ts[KC + 3:KC + 7]
        E = ts[KC + 7:KC + 10]
        PT = ts[KC + 10:KC + 18]
        RC = ts[KC + 18:KC + 20]
        OS = ts[KC + 20]
        V3 = V.ap.rearrange("p (t m) -> p t m", t=18)
        P.op("pool", lambda h: h.memset(V3[:, :, 64:128], 1.0), writes=[V])
        cnt = dict(e=0, p=0, r=0)
        j = b
        for hp in range(KC):
            wq = load_w("wqkv", hp)
            for (c0, c1) in pieces(CT, NT):
                ps = proj_fm(wq, H3, ht(c0, c1), c0, c1)
                CP(P, "act", QT.ap[:, c0 - CT:c1 - CT], ps.ap[:, 0:c1 - c0], [ps], [QT])
            wk = load_w("wqkv", 8 + hp)
            for (c0, c1) in pieces(0, NT):
                ps = proj_fm(wk, H3, ht(c0, c1), c0, c1)
                CP(P, "dve", KT.ap[:, c0:c1], ps.ap[:, 0:c1 - c0], [ps], [KT])
            wv = load_w("wqkv", 16 + hp)
            wv3 = wv.ap.rearrange("p (k m) -> p k m", k=KC)
            for t0 in range(0, 18, 4):
                nt_ = min(4, 18 - t0)
                ps = nb_()
                for ti in range(nt_):
                    tt = t0 + ti
                    for kc in range(KC):
                        MM(P, ps, ps.ap[:, ti * 128:(ti + 1) * 128], H3[:, kc, tt * 128:(tt + 1) * 128], wv3[:, kc, :],
                           kc == 0, kc == KC - 1, [wv] + ht(tt * 128, (tt + 1) * 128))
                ps3 = ps.ap[:, 0:nt_ * 128].rearrange("p (t m) -> p t m", t=nt_)
                CP(P, "act", V3[:, t0:t0 + nt_, 0:64], ps3[:, :, 0:64], [ps], [V])
                CP(P, "dve", V3[:, t0:t0 + nt_, 128:192], ps3[:, :, 64:128], [ps], [V])
            for par in range(2):
                h_ = 2 * hp + par
                P.dma("sp", TB[par * 2].ap, tmid[h_ * 128:(h_ + 1) * 128, :], writes=[TB[par * 2]])
                P.dma("sp", TB[par * 2 + 1].ap, tfull[h_ * 128:(h_ + 1) * 128, :], writes=[TB[par * 2 + 1]])
                ACT(P, TB[par * 2].ap, TB[par * 2].ap, AF.Exp, [TB[par * 2]], [TB[par * 2]])
                ACT(P, TB[par * 2 + 1].ap, TB[par * 2 + 1].ap, AF.Exp, [TB[par * 2 + 1]], [TB[par * 2 + 1]])
            items = [(par, r0) for par in range(2) for r0 in range(0, 32, 4)]

            def s_phase(par, r0):
                pb = par * 64
                q0 = r0 * 64
                if r0 == 0:
                    chunks, kind = [0, 1, 2, 3], 1
                elif r0 == 28:
                    chunks, kind = [12, 13, 14, 15], 1
                else:
                    a0 = (r0 - 4) // 2
                    chunks, kind = list(range(a0, a0 + 6)), 0
                tb = TB[par * 2 + kind]
                pts = []
                for i in range(0, len(chunks), 2):
                    ps = nb_()
                    for k in range(2):
                        a = chunks[i + k]
                        MM(P, ps, ps.ap[:, k * 256:(k + 1) * 256], KT.ap[pb:pb + 64, CT + a * 128:CT + (a + 1) * 128],
                           QT.ap[pb:pb + 64, q0:q0 + 256], True, True, [KT, QT])
                    e = E[cnt["e"] % 3]
                    cnt["e"] += 1
                    ACT(P, e.ap, ps.ap, AF.Exp, [ps], [e], scale=0.125)
                    pt = PT[cnt["p"] % 8]
                    cnt["p"] += 1
                    ei0 = r0 - 2 * chunks[i] + 6
                    tap = AP(tb.ap.tensor, tb.ap[:, ei0 * 64:ei0 * 64 + 1].offset, [list(tb.ap.ap[0]), [-128, 2], [1, 256]])
                    TT(P, "pool" if (i // 2) == 1 else "dve", pt.ap.rearrange("p (a b) -> p a b", a=2), e.ap.rearrange("p (a b) -> p a b", a=2), tap, ALU.mult, [e, tb], [pt])
                    pts.append((pt, chunks[i], chunks[i + 1]))
                ps = nb_()
                for k in range(2):
                    MM(P, ps, ps.ap[:, k * 256:(k + 1) * 256], KT.ap[pb:pb + 64, k * 128:(k + 1) * 128],
                       QT.ap[pb:pb + 64, q0:q0 + 256], True, True, [KT, QT])
                ptc = PT[cnt["p"] % 8]
                cnt["p"] += 1
                ACT(P, ptc.ap, ps.ap, AF.Exp, [ps], [ptc], scale=0.125)
                return (par, r0, pts, ptc)

            def pv_phase(st):
                par, r0, pts, ptc = st
                pb = par * 64
                q0 = r0 * 64
                pso = nb_()
                ob = 64 - pb
                mms = []
                for (pt, a, a2) in pts:
                    mms.append((V3[:, 2 + a, pb:pb + 128], pt.ap[:, 0:256], pt))
                    mms.append((V3[:, 2 + a2, pb:pb + 128], pt.ap[:, 256:512], pt))
                mms.append((V3[:, 0, pb:pb + 128], ptc.ap[:, 0:256], ptc))
                mms.append((V3[:, 1, pb:pb + 128], ptc.ap[:, 256:512], ptc))
                nm_ = len(mms)
                for i, (l_, r_, pt) in enumerate(mms):
                    MM(P, pso, pso.ap[:, 0:256], l_, r_, i == 0, i == nm_ - 1, [V, pt])
                rc = RC[cnt["r"] % 2]
                cnt["r"] += 1
                rl = OS
                ACT(P, rl.ap[pb:pb + 64, :], pso.ap[ob:ob + 64, 0:256], AF.Ln, [pso], [rl])
                CP(P, "dve", rc.ap[pb:pb + 64, :], rl.ap[pb:pb + 64, :], [rl], [rc])
                ACT(P, rc.ap[pb:pb + 64, :], rc.ap[pb:pb + 64, :], AF.Exp, [rc], [rc], scale=-1.0)
                TT(P, "dve", OT[hp].ap[pb:pb + 64, q0:q0 + 256], pso.ap[pb:pb + 64, 0:256], rc.ap[pb:pb + 64, :], ALU.mult, [pso, rc], [OT[hp]])

            prev = None
            for (par, r0) in items:
                st = s_phase(par, r0)
                if prev is not None:
                    pv_phase(prev)
                prev = st
            pv_phase(prev)
        OT3 = AP(OT[0].ap.tensor, OT[0].ap.offset, [list(OT[0].ap.ap[0]), [SEQ, KC], [1, SEQ]])
        lnT = ([E[0], E[1]], E[2], T(ucarve(68864, 512, F32)))
        P.realias([lnT[2]], PT)
        u_live.append(lnT[2])
        proj_resid_ln(b, 1, 1, "wout1", KC, OT3, OT, seqs_of(b, False), CT, lnT, (6 + 1, 8 + 1))

    for b in range(NB):
        for c in range(KC):
            q = "sp" if c % 2 == 0 else "pool"
            P.dma(q, X3[:, c, 0:CT], ctxT[b, c], writes=[XR[c][0]])
            P.dma(q, X3[:, c, CT:CT + 1024], xT[b, c][:, 0:1024], writes=[XR[c][1]])
            P.dma(q, X3[:, c, CT + 1024:NT], xT[b, c][:, 1024:2048], writes=[XR[c][2]])
        for c in range(KC):
            for r, (lo, hi) in enumerate(REG):
                j = NB if r == 0 else b
                ACT(P, H3[:, c, lo:hi], X3[:, c, lo:hi], AF.Identity, [XR[c][r], DER, MODS], [HR[c][r]], bias=mods(0, j, c), scale=der(0, j, c))
        if stop != "load":
            even_mixer(b)
        nm0 = (10, 11) if layers == 2 else None
        if stop is None:
            ffn_stage(b, 0, [(0, CT, NB, 0, CT), (CT, CT + 1024, b, CT, NT), (CT + 1024, NT, b, CT, NT)], nm0)

        if layers == 2:
            odd_mixer(b)
            ffn_stage(b, 1, [(CT, CT + 1024, b, CT, NT), (CT + 1024, NT, b, CT, NT)], None)
        if not (layers == 2 and stop is None):
            for c in range(KC):
                q = "sp" if c % 2 == 0 else "pool"
                P.dma(q, yT[b, c], X3[:, c, CT:NT], reads=[XR[c][1], XR[c][2]])
    allx = [t for c in range(KC) for t in XR[c]]
    P.wait_all("sp", allx)
    P.worder = wstate["seen"]
    P.finish()
    return nc, P


def _wt(W):
    K, M = W.shape
    return np.ascontiguousarray(W.reshape(K // 128, 128, M // 128, 128).transpose(2, 1, 0, 3)).reshape(M, K)


def _pc(v, nchunk):
    v = np.asarray(v, np.float32)
    lead = v.shape[:-1]
    return np.moveaxis(v.reshape(lead + (nchunk, 128)), -1, 0)


def _tables(rpb):
    kr2 = np.arange(2)[:, None, None, None]
    kcol = np.arange(64)[None, :, None, None]
    e = (np.arange(14) - 6)[None, None, :, None]
    qcol = np.arange(64)[None, None, None, :]
    dr = kr2 - e + 7 + 0 * kcol + 0 * qcol
    dc = kcol - qcol + 15 + 0 * kr2 + 0 * e
    cs = np.clip(qcol - 8, 0, 48)
    colok = (kcol >= cs) & (kcol < cs + 16)
    drok = (dr >= 0) & (dr <= 14)
    rowmid = (kr2 - e >= -4) & (kr2 - e <= 3)
    g = rpb[:, np.clip(dr, 0, 14), np.clip(dc, 0, 30)]
    negs = np.full(g.shape, NEG, np.float32)
    tfull = np.where((colok & drok)[None], g, negs).reshape(16 * 128, 896)
    tmid = np.where((colok & drok & rowmid)[None], g, negs).reshape(16 * 128, 896)
    return np.ascontiguousarray(tmid, np.float32), np.ascontiguousarray(tfull, np.float32)


def prep_shared(mod_w, mod_b, ln1_g, ln1_b, ln2_g, ln2_b, ev_w_in, ev_w_out, rec_conv_w, rec_conv_b, rec_wa, rec_ba,
                rec_wx, rec_bx, rec_lam, sc_conv_w, sc_conv_b, na_w_qkv, na_w_out, na_rpb, ffn_w_gate, ffn_w_up,
                ffn_conv_w, ffn_conv_b, ffn_w_down):
    sh = {}
    sh["win"] = _wt(ev_w_in[0])
    sh["wout0"] = _wt(ev_w_out[0])
    sh["wqkv"] = _wt(na_w_qkv[0])
    sh["wout1"] = _wt(na_w_out[0])
    for l in range(2):
        sh["wg%d" % l] = _wt(ffn_w_gate[l])
        sh["wu%d" % l] = _wt(ffn_w_up[l])
        sh["wd%d" % l] = _wt(ffn_w_down[l])
    sh["modw"] = np.concatenate([_wt(mod_w[l]) for l in range(2)], axis=0)
    small = np.zeros((128, NS), np.float32)

    def put(nm, arr):
        arr = np.asarray(arr, np.float32).reshape(128, -1)
        small[:, SOFF[nm]:SOFF[nm] + arr.shape[1]] = arr
    put("modb", _pc(mod_b, 48))
    put("ln1g", _pc(ln1_g, 8)); put("ln1b", _pc(ln1_b, 8)); put("ln2g", _pc(ln2_g, 8)); put("ln2b", _pc(ln2_b, 8))
    put("rcw", np.transpose(_pc(rec_conv_w[0], 4), (0, 2, 1)))
    put("rcb", _pc(rec_conv_b[0], 4))
    put("rba", _pc(rec_ba[0], 4)); put("rbx", _pc(rec_bx[0], 4)); put("rlam", _pc(rec_lam[0], 4))
    put("scw", np.transpose(_pc(sc_conv_w[0], 4), (0, 2, 1)))
    put("scb", _pc(sc_conv_b[0], 4))
    put("fcw", np.transpose(_pc(ffn_conv_w, FC), (0, 1, 3, 2)))
    put("fcb", _pc(ffn_conv_b, FC))
    sh["smallp"] = small
    gw = np.zeros((128, 16, 128), np.float32)
    for d in range(2):
        for wi, W in enumerate((rec_wa[0], rec_wx[0])):
            for c in range(4):
                g = (d * 2 + wi) * 4 + c
                gw[0:64, g, 0:64] = W[d, 2 * c]
                gw[64:128, g, 64:128] = W[d, 2 * c + 1]
    sh["gw"] = gw.reshape(128, 16 * 128)
    sh["tmid"], sh["tfull"] = _tables(np.asarray(na_rpb[0], np.float32))
    return sh


def prep_core(x, c, ctx, c_ctx, b0, NB):
    m = {}
    m["xT"] = np.ascontiguousarray(x[b0:b0 + NB].transpose(0, 2, 1)).reshape(NB, KC, 128, SEQ)
    m["ctxT"] = np.ascontiguousarray(ctx[b0:b0 + NB].transpose(0, 2, 1)).reshape(NB, KC, 128, CT)
    cc = np.concatenate([c[b0:b0 + NB], c_ctx[None, :]], axis=0)
    m["cT"] = np.ascontiguousarray(np.transpose(cc.reshape(NB + 1, KC, 128), (2, 1, 0))).reshape(128, KC * (NB + 1))
    return m


_CACHE = {}


def kernel(x, c, ctx, c_ctx, mod_w, mod_b, ln1_g, ln1_b, ln2_g, ln2_b, ev_w_in, ev_w_out, rec_conv_w, rec_conv_b,
           rec_wa, rec_ba, rec_wx, rec_bx, rec_lam, sc_conv_w, sc_conv_b, na_w_qkv, na_w_out, na_rpb, ffn_w_gate,
           ffn_w_up, ffn_conv_w, ffn_conv_b, ffn_w_down):
    f = lambda a: np.asarray(a, np.float32)
    x, c, ctx, c_ctx = f(x), f(c), f(ctx), f(c_ctx)
    B = x.shape[0]
    NB = B // NCORES
    sh = prep_shared(f(mod_w), f(mod_b), f(ln1_g), f(ln1_b), f(ln2_g), f(ln2_b), f(ev_w_in), f(ev_w_out), f(rec_conv_w),
                     f(rec_conv_b), f(rec_wa), f(rec_ba), f(rec_wx), f(rec_bx), f(rec_lam), f(sc_conv_w), f(sc_conv_b),
                     f(na_w_qkv), f(na_w_out), f(na_rpb), f(ffn_w_gate), f(ffn_w_up), f(ffn_conv_w), f(ffn_conv_b), f(ffn_w_down))
    if NB not in _CACHE:
        _, p0 = build_program(1)
        _CACHE[NB] = build_program(NB, order=p0.worder)[0]
    nc = _CACHE[NB]
    in_maps = []
    for i in range(NCORES):
        m = dict(sh)
        m.update(prep_core(x, c, ctx, c_ctx, i * NB, NB))
        in_maps.append(m)
    res = run_bass_kernel_spmd(nc, in_maps, core_ids=list(range(NCORES)))
    out = np.empty((B, SEQ, D), np.float32)
    for i in range(NCORES):
        y = np.asarray(res.results[i]["yT"]).reshape(NB, D, SEQ)
        out[i * NB:(i + 1) * NB] = y.transpose(0, 2, 1)
    return out
```

```python
import numpy as np
import concourse.bass as bass
import concourse.mybir as mybir
from concourse.bass_utils import run_bass_kernel_spmd
from concourse.ap import AP

F32 = mybir.dt.float32
BF16 = mybir.dt.bfloat16
ALU = mybir.AluOpType
AF = mybir.ActivationFunctionType

NCORES = 8
D = 1024
KC = 8
SEQ = 2048
CT = 256
NT = SEQ + CT
DFF = 2816
FC = 22
ALPHA = 4.0 ** 0.25
EPS2 = 1e-6 / (ALPHA * ALPHA)
NEG = -30000.0


class T:
    __slots__ = ("ap", "w", "r")

    def __init__(self, ap):
        self.ap = ap
        self.w = None
        self.r = {}


class Eng:
    def __init__(self, name, h, sem):
        self.name = name
        self.h = h
        self.sem = sem
        self.count = 0
        self.waited = {}
        self.q = []


class Prog:
    def __init__(self, nc, n_dma_sems=8):
        self.nc = nc
        self._ctx = []
        self.engs = {}
        for nm, h in (("pe", nc.tensor), ("act", nc.scalar), ("dve", nc.vector), ("pool", nc.gpsimd), ("sp", nc.sync)):
            self.engs[nm] = Eng(nm, h, self._sem("e_" + nm))
        self.n_dma_sems = n_dma_sems
        self.dma_pool = {}
        for nm in ("sp", "pool"):
            self.dma_pool[nm] = dict(sems=[self._sem("d_%s%d" % (nm, i)) for i in range(n_dma_sems)], j=0)
        self.ninst = 0
        self.force_own = False

    def _sem(self, name):
        cm = self.nc.semaphore(name)
        s = cm.__enter__()
        self._ctx.append(cm)
        return (name, s)

    def sbuf(self, name, shape, dt):
        cm = self.nc.sbuf_tensor(name, shape, dt)
        t = cm.__enter__()
        self._ctx.append(cm)
        return t

    def psum(self, name, shape, dt):
        cm = self.nc.psum_tensor(name, shape, dt)
        t = cm.__enter__()
        self._ctx.append(cm)
        return t

    @staticmethod
    def _deps(reads, writes):
        deps = {}

        def add(k, v):
            if deps.get(k, 0) < v:
                deps[k] = v
        for t in reads:
            if t.w is not None:
                add(*t.w)
        for t in writes:
            if t.w is not None:
                add(*t.w)
            for k, v in t.r.items():
                add(k, v)
        return deps

    def _waits(self, e, deps, own_too):
        for (name, sh), v in deps.items():
            if (not own_too) and name == e.sem[0]:
                continue
            if e.waited.get(name, 0) >= v:
                continue
            e.waited[name] = v
            e.q.append(lambda h, sh=sh, v=v: h.wait_ge(sh, v))
            self.ninst += 1

    def op(self, eng, fn, reads=(), writes=(), sig=True, own=False):
        e = self.engs[eng]
        self._waits(e, self._deps(reads, writes), own or self.force_own)
        if sig:
            e.count += 1
            val = e.count
            sh = e.sem[1]
            e.q.append(lambda h, fn=fn, sh=sh: fn(h).then_inc(sh, 1))
        else:
            val = e.count + 1
            e.q.append(lambda h, fn=fn: fn(h))
        self.ninst += 1
        for t in reads:
            if t.r.get(e.sem, 0) < val:
                t.r[e.sem] = val
        for t in writes:
            t.w = (e.sem, val)
            t.r = {}

    def dma(self, q, out_ap, in_ap, reads=(), writes=()):
        e = self.engs[q]
        pool = self.dma_pool[q]
        j = pool["j"]
        pool["j"] += 1
        s = pool["sems"][j % self.n_dma_sems]
        gen = j // self.n_dma_sems
        deps = self._deps(reads, writes)
        if gen > 0 and deps.get(s, 0) < 16 * gen:
            deps[s] = 16 * gen
        self._waits(e, deps, True)
        val = 16 * (gen + 1)
        sh = s[1]
        e.q.append(lambda h, o=out_ap, i=in_ap, sh=sh: h.dma_start(out=o, in_=i).then_inc(sh, 16))
        self.ninst += 1
        for t in reads:
            if t.r.get(s, 0) < val:
                t.r[s] = val
        for t in writes:
            t.w = (s, val)
            t.r = {}

    def realias(self, new_ts, old_ts):
        merged = {}
        for t in old_ts:
            if t.w is not None and merged.get(t.w[0], 0) < t.w[1]:
                merged[t.w[0]] = t.w[1]
            for k, v in t.r.items():
                if merged.get(k, 0) < v:
                    merged[k] = v
        for t in new_ts:
            t.w = None
            t.r = dict(merged)

    def wait_all(self, eng, ts):
        e = self.engs[eng]
        self._waits(e, self._deps(ts, ts), True)

    def finish(self):
        nc = self.nc
        with nc.Block() as block:
            def mk(e):
                def f(h):
                    for c in e.q:
                        c(h)
                return f
            block.tensor(mk(self.engs["pe"]))
            block.scalar(mk(self.engs["act"]))
            block.vector(mk(self.engs["dve"]))
            block.gpsimd(mk(self.engs["pool"]))
            block.sync(mk(self.engs["sp"]))
        for cm in reversed(self._ctx):
            cm.__exit__(None, None, None)
        self._ctx = []


def MM(P, psT, out, lhsT, rhs, start, stop, reads):
    P.op("pe", lambda h: h.matmul(out, lhsT, rhs, start=start, stop=stop), reads=reads, writes=[psT], sig=True)


def ACT(P, out, in_, func, reads, writes, bias=None, scale=None):
    kw = {}
    if bias is not None:
        kw["bias"] = bias
    if scale is not None:
        kw["scale"] = scale
    P.op("act", lambda h: h.activation(out, in_, func, **kw), reads=reads, writes=writes)


def TT(P, eng, out, in0, in1, op, reads, writes):
    P.op(eng, lambda h: h.tensor_tensor(out, in0, in1, op), reads=reads, writes=writes)


def TS(P, eng, out, in0, s1, s2, op0, op1, reads, writes):
    if s2 is None:
        P.op(eng, lambda h: h.tensor_scalar(out, in0, s1, None, op0), reads=reads, writes=writes)
    else:
        P.op(eng, lambda h: h.tensor_scalar(out, in0, s1, s2, op0, op1), reads=reads, writes=writes)


def STT(P, eng, out, in0, sc, in1, op0, op1, reads, writes):
    P.op(eng, lambda h: h.scalar_tensor_tensor(out, in0, sc, in1, op0, op1), reads=reads, writes=writes)


def CP(P, eng, out, in_, reads, writes):
    if eng == "act":
        P.op("act", lambda h: h.copy(out, in_), reads=reads, writes=writes)
    else:
        P.op(eng, lambda h: h.tensor_copy(out, in_), reads=reads, writes=writes)


def SCAN(P, out, d0, d1, init, reads, writes):
    P.op("dve", lambda h: h.tensor_tensor_scan(out, d0, d1, init, ALU.mult, ALU.add), reads=reads, writes=writes)


def rev(ap2d):
    n = ap2d.shape[1]
    last = ap2d[:, n - 1:n]
    return AP(last.tensor, last.offset, [list(last.ap[0]), [-1, n]])


def pieces(lo, hi, step=512):
    n = -(-(hi - lo) // step)
    size = -(-(hi - lo) // n)
    size += size % 2
    out = []
    c = lo
    while c < hi:
        out.append((c, min(hi, c + size)))
        c += size
    return out


def small_layout():
    off = {}
    o = 0
    for nm, n in (("modb", 96), ("ln1g", 16), ("ln1b", 16), ("ln2g", 16), ("ln2b", 16), ("rcw", 16), ("rcb", 4),
                  ("rba", 8), ("rbx", 8), ("rlam", 8), ("scw", 12), ("scb", 4), ("fcw", 132), ("fcb", 44)):
        off[nm] = o
        o += n
    return off, o


SOFF, NS = small_layout()


def build_program(NB, layers=2, stop=None, order=None):
    nc = bass.Bass("TRN2", target_bir_lowering=False)
    NJ = NB + 1

    def din(name, shape, dt=F32):
        return nc.dram_tensor(name, list(shape), dt, kind="ExternalInput").ap()

    xT = din("xT", [NB, KC, 128, SEQ])
    ctxT = din("ctxT", [NB, KC, 128, CT])
    cT = din("cT", [128, KC * NJ])
    smallp = din("smallp", [128, NS])
    modw = din("modw", [2 * 48 * 128, 1024])
    gw_in = din("gw", [128, 16 * 128])
    tmid = din("tmid", [16 * 128, 896])
    tfull = din("tfull", [16 * 128, 896])
    wspec = [("win", 20, 1024), ("wout0", 8, 1024), ("wg0", FC, 1024), ("wu0", FC, 1024), ("wd0", 8, DFF),
             ("wqkv", 24, 1024), ("wout1", 8, 1024), ("wg1", FC, 1024), ("wu1", FC, 1024), ("wd1", 8, DFF)]
    wsrc = {}
    wscr = {}
    wscrT = {}
    for nm, nmc, ncol in wspec:
        wsrc[nm] = din(nm, [nmc * 128, ncol])
        wscr[nm] = nc.dram_tensor(nm + "_s", [nmc * 128, ncol], BF16, kind="Internal").ap()
        wscrT[nm] = [T(wscr[nm][m * 128:(m + 1) * 128, :]) for m in range(nmc)]
    yT = nc.dram_tensor("yT", [NB, KC, 128, SEQ], F32, kind="ExternalOutput").ap()

    P = Prog(nc)
    Xs = P.sbuf("X", [128, KC * NT], F32)
    Hs = P.sbuf("H", [128, KC * NT], BF16)
    UBYTES = 81920
    Us = P.sbuf("U", [128, UBYTES // 4], F32)
    X3 = Xs[:].rearrange("p (c t) -> p c t", c=KC)
    H3 = Hs[:].rearrange("p (c t) -> p c t", c=KC)
    REG = [(0, CT), (CT, CT + 1024), (CT + 1024, NT)]
    XR = [[T(X3[:, c, lo:hi]) for (lo, hi) in REG] for c in range(KC)]
    HR = [[T(H3[:, c, lo:hi]) for (lo, hi) in REG] for c in range(KC)]

    def xt(mc, c0, c1):
        return [XR[mc][r] for r, (lo, hi) in enumerate(REG) if c0 < hi and c1 > lo]

    def ht1(mc, c0, c1):
        return [HR[mc][r] for r, (lo, hi) in enumerate(REG) if c0 < hi and c1 > lo]

    def ht(c0, c1):
        return [t for mc in range(KC) for t in ht1(mc, c0, c1)]
    WA = [T(P.sbuf("wa%d" % i, [128, 1024], BF16)[:]) for i in range(3)]
    wa_i = [0]
    smallT = T(P.sbuf("smallp_sb", [128, NS], F32)[:])
    cvec = T(P.sbuf("cvec", [128, 4], F32)[:])
    sttA = T(P.sbuf("sttA", [128, 2], F32)[:])
    sttB = T(P.sbuf("sttB", [128, 2], F32)[:])
    onesD = T(P.sbuf("onesD", [128, 128], F32)[:])
    onesB = T(P.sbuf("onesB", [128, 128], BF16)[:])
    scT = T(P.sbuf("scT", [128, KC * NJ], F32)[:])
    MODS = T(P.sbuf("MODS", [128, 2 * NJ * 48], F32)[:])
    DER = T(P.sbuf("DER", [128, 12 * NJ * 8], F32)[:])
    CLAM = T(P.sbuf("CLAM", [128, 8], F32)[:])
    GW = T(P.sbuf("GW", [128, 16 * 128], BF16)[:])
    banks = [T(P.psum("ps%d" % i, [128, 512], F32)[:]) for i in range(8)]
    bank_i = [0]

    def nb_():
        t = banks[bank_i[0] % 8]
        bank_i[0] += 1
        return t

    def ucarve(off_bytes, n, dt):
        if dt == F32:
            return Us[:, off_bytes // 4: off_bytes // 4 + n]
        return Us[:, off_bytes // 4: off_bytes // 4 + (n + 1) // 2].bitcast(BF16)[:, :n]

    u_live = []

    def uphase(specs):
        new = [T(ucarve(o, n, dt)) for (o, n, dt) in specs]
        P.realias(new, u_live)
        u_live[:] = new
        return new

    sm = smallT.ap

    def scol(nm, i):
        o = SOFF[nm] + i
        return sm[:, o:o + 1]

    def mods(l, j, q):
        o = (l * NJ + j) * 48 + q
        return MODS.ap[:, o:o + 1]

    def der(k, j, c):
        o = (k * NJ + j) * 8 + c
        return DER.ap[:, o:o + 1]

    worder = list(order) if order is not None else []
    wpos = {k: i for i, k in enumerate(worder)}
    wstate = dict(done=0, seen=[])
    LOOK = 12

    def prepass_upto(k):
        while wstate["done"] < min(k, len(worder)):
            nm, m = worder[wstate["done"]]
            P.dma("pool", wscr[nm][m * 128:(m + 1) * 128, :], wsrc[nm][m * 128:(m + 1) * 128, :], writes=[wscrT[nm][m]])
            wstate["done"] += 1
    if order is None:
        for nm, nmc, ncol in wspec:
            if layers == 1 and nm in ("wqkv", "wout1", "wg1", "wu1", "wd1"):
                continue
            for m in range(nmc):
                P.dma("pool", wscr[nm][m * 128:(m + 1) * 128, :], wsrc[nm][m * 128:(m + 1) * 128, :], writes=[wscrT[nm][m]])
    P.dma("sp", smallT.ap, smallp, writes=[smallT])
    P.dma("sp", scT.ap, cT, writes=[scT])
    P.dma("pool", GW.ap, gw_in, writes=[GW])
    prepass_upto(LOOK)
    P.op("dve", lambda h: h.memset(cvec.ap[:, 0:1], 1.0), writes=[cvec])
    P.op("dve", lambda h: h.memset(cvec.ap[:, 1:2], EPS2), writes=[cvec])
    P.op("dve", lambda h: h.memset(cvec.ap[:, 2:3], 0.0), writes=[cvec])
    P.op("dve", lambda h: h.memset(onesD.ap, 1.0 / D), writes=[onesD])
    P.op("dve", lambda h: h.memset(onesB.ap, 1.0), writes=[onesB])
    ONE = cvec.ap[:, 0:1]
    EPSc = cvec.ap[:, 1:2]
    ACT(P, scT.ap, scT.ap, AF.Silu, [scT], [scT])
    ACT(P, CLAM.ap, sm[:, SOFF["rlam"]:SOFF["rlam"] + 8], AF.Exp, [smallT], [CLAM], scale=-1.0)
    TS(P, "dve", CLAM.ap, CLAM.ap, 1.0, None, ALU.add, None, [CLAM], [CLAM])
    ACT(P, CLAM.ap, CLAM.ap, AF.Ln, [CLAM], [CLAM])
    TS(P, "dve", CLAM.ap, CLAM.ap, -8.0, None, ALU.mult, None, [CLAM], [CLAM])
    mwb = uphase([(i * 4096, 1024, F32) for i in range(3)])
    sc3 = scT.ap.rearrange("p (c j) -> p c j", c=KC)
    for l in range(layers):
        for q in range(48):
            wt = mwb[(l * 48 + q) % 3]
            r0 = (l * 48 + q) * 128
            P.dma("sp", wt.ap, modw[r0:r0 + 128, :], writes=[wt])
            ps = nb_()
            w3 = wt.ap.rearrange("p (k m) -> p k m", k=KC)
            for kc in range(KC):
                MM(P, ps, ps.ap[:, 0:NJ], w3[:, kc, :], sc3[:, kc, :], kc == 0, kc == KC - 1, [wt, scT])
            o0 = l * NJ * 48 + q
            outap = AP(MODS.ap.tensor, MODS.ap[:, o0:o0 + 1].offset, [list(MODS.ap.ap[0]), [48, NJ]])
            TS(P, "dve", outap, ps.ap[:, 0:NJ], scol("modb", l * 48 + q), None, ALU.add, None, [ps, smallT], [MODS])
    for l in range(layers):
        for j in range(NJ):
            def M8(which):
                o = (l * NJ + j) * 48 + which * 8
                return MODS.ap[:, o:o + 8]

            def D8(k):
                o = (k * NJ + j) * 8
                return DER.ap[:, o:o + 8]
            ln1g = sm[:, SOFF["ln1g"] + l * 8: SOFF["ln1g"] + l * 8 + 8]
            ln1b = sm[:, SOFF["ln1b"] + l * 8: SOFF["ln1b"] + l * 8 + 8]
            TS(P, "dve", D8(0 + l), M8(1), 1.0, None, ALU.add, None, [MODS], [DER])
            TS(P, "dve", D8(2 + l), M8(2), 1.0 / ALPHA, None, ALU.mult, None, [MODS], [DER])
            TS(P, "dve", D8(4 + l), M8(5), 1.0 / ALPHA, None, ALU.mult, None, [MODS], [DER])
            TS(P, "dve", D8(8 + l), M8(4), 1.0, None, ALU.add, None, [MODS], [DER])
            TT(P, "pool", D8(6 + l), D8(8 + l), ln1g, ALU.mult, [DER, smallT], [DER])
            TT(P, "pool", D8(8 + l), D8(8 + l), ln1b, ALU.mult, [DER, smallT], [DER])
            TT(P, "dve", D8(8 + l), D8(8 + l), M8(3), ALU.add, [DER, MODS], [DER])
    if layers == 2:
        for j in range(NJ):
            ln2g = sm[:, SOFF["ln2g"]: SOFF["ln2g"] + 8]
            ln2b = sm[:, SOFF["ln2b"]: SOFF["ln2b"] + 8]
            A1n = DER.ap[:, (1 * NJ + j) * 8:(1 * NJ + j) * 8 + 8]
            SH1n = MODS.ap[:, (1 * NJ + j) * 48: (1 * NJ + j) * 48 + 8]
            g1p = DER.ap[:, (10 * NJ + j) * 8:(10 * NJ + j) * 8 + 8]
            b1p = DER.ap[:, (11 * NJ + j) * 8:(11 * NJ + j) * 8 + 8]
            TT(P, "pool", g1p, A1n, ln2g, ALU.mult, [DER, smallT], [DER])
            TT(P, "pool", b1p, A1n, ln2b, ALU.mult, [DER, smallT], [DER])
            TT(P, "dve", b1p, b1p, SH1n, ALU.add, [DER, MODS], [DER])

    def load_w(nm, m, ncol=1024, buf=None):
        if (nm, m) not in wstate["seen"]:
            wstate["seen"].append((nm, m))
        if order is not None:
            prepass_upto(wpos[(nm, m)] + 1 + LOOK)
        if buf is None:
            buf = WA[wa_i[0] % 3]
            wa_i[0] += 1
        P.dma("sp", buf.ap[:, 0:ncol], wscr[nm][m * 128:(m + 1) * 128, :], reads=[wscrT[nm][m]], writes=[buf])
        return buf

    def proj_fm(wt, rhs3, rhsT, c0, c1, nk=KC):
        ps = nb_()
        w3 = wt.ap[:, 0:nk * 128].rearrange("p (k m) -> p k m", k=nk)
        for kc in range(nk):
            MM(P, ps, ps.ap[:, 0:c1 - c0], w3[:, kc, :], rhs3[:, kc, c0:c1], kc == 0, kc == nk - 1, [wt] + rhsT)
        return ps

    def seqs_of(b, with_ctx):
        s = []
        if with_ctx:
            s.append((0, CT, NB))
        s.append((CT, NT, b))
        return s

    def layer_norm_piece(c0, c1, j, l, which, lnT, nextmod, hskip=0):
        n = c1 - c0
        SQ, MS, RS = lnT
        psm = nb_()
        pse = nb_()
        for mc in range(KC):
            sq = SQ[mc % 2]
            ACT(P, sq.ap[:, 0:n], X3[:, mc, c0:c1], AF.Square, xt(mc, c0, c1), [sq])
            MM(P, psm, psm.ap[:, 0:n], onesD.ap, X3[:, mc, c0:c1], mc == 0, mc == KC - 1, [onesD] + xt(mc, c0, c1))
            MM(P, pse, pse.ap[:, 0:n], onesD.ap, sq.ap[:, 0:n], mc == 0, mc == KC - 1, [onesD, sq])
        ACT(P, MS.ap[:, 0:n], psm.ap[:, 0:n], AF.Square, [psm], [MS])
        TT(P, "dve", MS.ap[:, 0:n], pse.ap[:, 0:n], MS.ap[:, 0:n], ALU.subtract, [pse, MS], [MS])
        ACT(P, MS.ap[:, 0:n], MS.ap[:, 0:n], AF.Sqrt, [MS, cvec], [MS], bias=EPSc, scale=1.0)
        P.op("dve", lambda h: h.reciprocal(RS.ap[:, 0:n], MS.ap[:, 0:n]), reads=[MS], writes=[RS])
        gname = "ln1g" if which == 1 else "ln2g"
        bname = "ln1b" if which == 1 else "ln2b"
        for mc in range(KC):
            xs = X3[:, mc, c0:c1]
            TT(P, "dve", xs, xs, psm.ap[:, 0:n], ALU.subtract, xt(mc, c0, c1) + [psm], xt(mc, c0, c1))
            TT(P, "pool", xs, xs, RS.ap[:, 0:n], ALU.mult, xt(mc, c0, c1) + [RS], xt(mc, c0, c1))
            if nextmod is not None:
                gk, bk = nextmod
                ACT(P, H3[:, mc, c0:c1 - hskip], X3[:, mc, c0:c1 - hskip], AF.Identity, xt(mc, c0, c1) + [DER], ht1(mc, c0, c1), bias=der(bk, j, mc), scale=der(gk, j, mc))
            ACT(P, xs, xs, AF.Identity, xt(mc, c0, c1) + [smallT], xt(mc, c0, c1), bias=scol(bname, l * 8 + mc), scale=scol(gname, l * 8 + mc))

    def proj_resid_ln(b, l, which, wname, nk, rhs3, rhsT, seqs, col_shift, lnT, nextmod, wbufs=None, mc_order=None):
        gk = (2 if which == 1 else 4) + l
        for (lo, hi, j) in seqs:
            for (c0, c1) in pieces(lo, hi):
                for mc in (mc_order or range(KC)):
                    if wbufs is None:
                        wt = load_w(wname, mc)
                    else:
                        wt = load_w(wname, mc, ncol=nk * 128, buf=wbufs[mc % 2])
                    ps = nb_()
                    w3 = wt.ap[:, 0:nk * 128].rearrange("p (k m) -> p k m", k=nk)
                    for kc in range(nk):
                        MM(P, ps, ps.ap[:, 0:c1 - c0], w3[:, kc, :], rhs3[:, kc, c0 - col_shift:c1 - col_shift],
                           kc == 0, kc == nk - 1, [wt] + rhsT)
                    xs = X3[:, mc, c0:c1]
                    STT(P, "dve", xs, ps.ap[:, 0:c1 - c0], der(gk, j, mc), xs, ALU.mult, ALU.add, [ps, DER] + xt(mc, c0, c1), xt(mc, c0, c1))
                layer_norm_piece(c0, c1, j, l, which, lnT, nextmod)

    def conv_seq(dst, src, dstT, srcT, lo, hi, dlo, slo, wname, wbase, K, padl, bias_ap, seq_lo, seq_hi, eng="dve", center="dve"):
        ctr = padl
        if center == "act":
            ACT(P, dst[:, lo - dlo:hi - dlo], src[:, lo - slo:hi - slo], AF.Identity, [srcT, smallT], [dstT], bias=bias_ap, scale=scol(wname, wbase + ctr))
        else:
            TS(P, eng, dst[:, lo - dlo:hi - dlo], src[:, lo - slo:hi - slo], scol(wname, wbase + ctr), bias_ap, ALU.mult, ALU.add,
               [srcT, smallT], [dstT])
        for k in range(K):
            if k == ctr:
                continue
            o = k - padl
            t0 = max(lo, seq_lo - o)
            t1 = min(hi, seq_hi - o)
            if t1 <= t0:
                continue
            STT(P, eng, dst[:, t0 - dlo:t1 - dlo], src[:, t0 + o - slo:t1 + o - slo], scol(wname, wbase + k), dst[:, t0 - dlo:t1 - dlo],
                ALU.mult, ALU.add, [srcT, smallT, dstT], [dstT])

    def ffn_stage(b, l, seqs_super, nextmod):
        specs = [(fc * 2048, 1024, BF16) for fc in range(FC)]
        specs += [(45056, 1056, F32), (49280, 1024, F32), (53376, DFF, BF16), (59008, DFF, BF16)]
        specs += [(64640 + i * 2048, 512, F32) for i in range(4)]
        ts = uphase(specs)
        A = ts[0:FC]
        G, CV = ts[FC], ts[FC + 1]
        WD = ts[FC + 2:FC + 4]
        lnT = (ts[FC + 4:FC + 6], ts[FC + 6], ts[FC + 7])
        wg, wu, wd = "wg%d" % l, "wu%d" % l, "wd%d" % l
        patch = []
        for (lo, hi, j, seq_lo, seq_hi) in seqs_super:
            glo = max(seq_lo, lo - 1)
            ghi = min(seq_hi, hi + 1)
            for fc in range(FC):
                wtg = load_w(wg, fc)
                for (c0, c1) in pieces(glo, ghi):
                    ps = proj_fm(wtg, H3, ht(c0, c1), c0, c1)
                    CP(P, "act", G.ap[:, c0 - glo:c1 - glo], ps.ap[:, 0:c1 - c0], [ps], [G])
                conv_seq(CV.ap, G.ap, CV, G, lo, hi, lo, glo, "fcw", (l * FC + fc) * 3, 3, 1, scol("fcb", l * FC + fc), seq_lo, seq_hi)
                ACT(P, CV.ap[:, 0:hi - lo], CV.ap[:, 0:hi - lo], AF.Silu, [CV], [CV])
                wtu = load_w(wu, fc)
                for (c0, c1) in pieces(lo, hi):
                    ps = proj_fm(wtu, H3, ht(c0, c1), c0, c1)
                    TT(P, "dve", A[fc].ap[:, c0 - lo:c1 - lo], ps.ap[:, 0:c1 - c0], CV.ap[:, c0 - lo:c1 - lo], ALU.mult, [ps, CV], [A[fc]])
            A3 = AP(A[0].ap.tensor, A[0].ap.offset, [list(A[0].ap.ap[0]), [1024, FC], [1, 1024]])
            gk = 4 + l
            for (c0, c1) in pieces(lo, hi):
                for mc in range(KC):
                    wt = load_w(wd, mc, ncol=DFF, buf=WD[mc % 2])
                    ps = nb_()
                    w3 = wt.ap.rearrange("p (k m) -> p k m", k=FC)
                    for kc in range(FC):
                        MM(P, ps, ps.ap[:, 0:c1 - c0], w3[:, kc, :], A3[:, kc, c0 - lo:c1 - lo], kc == 0, kc == FC - 1, [wt, A[kc]])
                    xs = X3[:, mc, c0:c1]
                    STT(P, "dve", xs, ps.ap[:, 0:c1 - c0], der(gk, j, mc), xs, ALU.mult, ALU.add, [ps, DER] + xt(mc, c0, c1), xt(mc, c0, c1))
                more = any((s2[3] == seq_lo and s2[0] == hi) for s2 in seqs_super)
                hs = 1 if (nextmod is not None and c1 == hi and more) else 0
                layer_norm_piece(c0, c1, j, l, 2, lnT, nextmod, hskip=hs)
                if l == 1:
                    for mc in range(KC):
                        P.dma("sp" if mc % 2 == 0 else "pool", yT[b, mc][:, c0 - CT:c1 - CT], X3[:, mc, c0:c1], reads=xt(mc, c0, c1))
                if hs:
                    patch.append((hi - 1, j))
        if patch:
            patch_h(patch, l + 1)

    def patch_h(patch, lnext):
        for (col, j) in patch:
            for mc in range(KC):
                ACT(P, H3[:, mc, col:col + 1], X3[:, mc, col:col + 1], AF.Identity, xt(mc, col, col + 1) + [DER, MODS], ht1(mc, col, col + 1),
                    bias=mods(lnext, j, mc), scale=der(0 + lnext, j, mc))

    def even_mixer(b):
        specs = [(c * 4608, NT, BF16) for c in range(KC)]
        specs += [(36864 + i * 9216, NT, F32) for i in range(3)]
        specs += [(64512, NT, BF16)]
        specs += [(69120 + i * 2048, 512, F32) for i in range(4)]
        ts = uphase(specs)
        YC = ts[0:KC]
        setA = (ts[KC], ts[KC + 1], ts[KC + 2], ts[KC + 3], ts[KC + 4:KC + 7], sttA)
        TM = ts[KC + 4:KC + 8]
        def xcarve(off, n, dt):
            if dt == F32:
                return Xs[:, off // 4: off // 4 + n]
            return Xs[:, off // 4: off // 4 + (n + 1) // 2].bitcast(BF16)[:, :n]
        xs_specs = [(i * 9216, NT, F32) for i in range(3)] + [(27648, NT, BF16)] + [(32256 + i * 2048, 512, F32) for i in range(2)]
        xs_specs += [(36864, NT, F32), (46080, NT, F32)]
        xsT = [T(xcarve(o, n, dt)) for (o, n, dt) in xs_specs]
        borrowB = [t for c in range(4) for t in XR[c]]
        borrowS = [t for c in (4, 5) for t in XR[c]]
        P.realias(xsT[0:6], borrowB)
        P.realias(xsT[6:8], borrowS)
        setB = (xsT[0], xsT[1], xsT[2], xsT[3], xsT[4:6], sttB)
        S0, S1 = xsT[6], xsT[7]

        def reload_x(chunks):
            for r, (lo, hi) in enumerate(REG):
                for c in chunks:
                    q = "sp" if c % 2 == 0 else "pool"
                    if r == 0:
                        P.dma(q, X3[:, c, 0:CT], ctxT[b, c], writes=[XR[c][0]])
                    else:
                        P.dma(q, X3[:, c, lo:hi], xT[b, c][:, lo - CT:hi - CT], writes=[XR[c][r]])
        seqs = seqs_of(b, True)
        allp = [pc for (lo, hi, j) in seqs for pc in pieces(lo, hi)]

        def rec_chunk(c, bufs, wbuf):
            R0, R1, R2, XCb, TMs, st = bufs
            tm_i = [0]

            def ntmp():
                t = TMs[tm_i[0] % len(TMs)]
                tm_i[0] += 1
                return t
            wt = load_w("win", c, buf=wbuf)
            for (c0, c1) in allp:
                ps = proj_fm(wt, H3, ht(c0, c1), c0, c1)
                CP(P, "act", R0.ap[:, c0:c1], ps.ap[:, 0:c1 - c0], [ps], [R0])
                yield
            for (lo, hi, j) in seqs:
                conv_seq(R1.ap, R0.ap, R1, R0, lo, hi, 0, 0, "rcw", c * 4, 4, 2, scol("rcb", c), lo, hi, center="act")
            yield
            CP(P, "pool", XCb.ap, R1.ap, [R1], [XCb])
            yield
            for d in range(2):
                Bd = R2 if d == 0 else R1
                for (c0, c1) in allp:
                    n = c1 - c0
                    psr = nb_()
                    MM(P, psr, psr.ap[:, 0:n], GW.ap[:, ((d * 2 + 0) * 4 + c) * 128:((d * 2 + 0) * 4 + c + 1) * 128], XCb.ap[:, c0:c1], True, True, [GW, XCb])
                    psi = nb_()
                    MM(P, psi, psi.ap[:, 0:n], GW.ap[:, ((d * 2 + 1) * 4 + c) * 128:((d * 2 + 1) * 4 + c + 1) * 128], XCb.ap[:, c0:c1], True, True, [GW, XCb])
                    ACT(P, R0.ap[:, c0:c1], psr.ap[:, 0:n], AF.Sigmoid, [psr, smallT], [R0], bias=scol("rba", d * 4 + c))
                    t1 = ntmp()
                    ACT(P, t1.ap[:, 0:n], psi.ap[:, 0:n], AF.Sigmoid, [psi, smallT], [t1], bias=scol("rbx", d * 4 + c))
                    if d == 0:
                        TT(P, "pool", Bd.ap[:, c0:c1], t1.ap[:, 0:n], R1.ap[:, c0:c1], ALU.mult, [t1, R1], [Bd])
                    else:
                        TT(P, "pool", Bd.ap[:, c0:c1], R1.ap[:, c0:c1], t1.ap[:, 0:n], ALU.mult, [t1, R1], [Bd])
                    yield
                ACT(P, R0.ap, R0.ap, AF.Exp, [R0, CLAM], [R0], scale=CLAM.ap[:, d * 4 + c:d * 4 + c + 1])
                yield
                for (c0, c1) in allp:
                    n = c1 - c0
                    a_ = R0.ap[:, c0:c1]
                    t2 = ntmp()
                    TT(P, "pool", t2.ap[:, 0:n], a_, a_, ALU.mult, [R0], [t2])
                    ACT(P, t2.ap[:, 0:n], t2.ap[:, 0:n], AF.Sqrt, [t2, cvec], [t2], bias=ONE, scale=-1.0)
                    TT(P, "dve", Bd.ap[:, c0:c1], Bd.ap[:, c0:c1], t2.ap[:, 0:n], ALU.mult, [Bd, t2], [Bd])
                    yield
                if d == 0:
                    SCAN(P, Bd.ap[:, 0:CT], R0.ap[:, 0:CT], Bd.ap[:, 0:CT], 0.0, [R0, Bd], [Bd])
                    CP(P, "act", st.ap[:, 0:1], Bd.ap[:, CT - 1:CT], [Bd], [st])
                    yield
                    SCAN(P, Bd.ap[:, CT:NT], R0.ap[:, CT:NT], Bd.ap[:, CT:NT], st.ap[:, 0:1], [R0, Bd, st], [Bd])
                else:
                    SCAN(P, rev(Bd.ap[:, 0:CT]), rev(R0.ap[:, 0:CT]), rev(Bd.ap[:, 0:CT]), 0.0, [R0, Bd], [Bd])
                    CP(P, "act", st.ap[:, 1:2], Bd.ap[:, 0:1], [Bd], [st])
                    yield
                    SCAN(P, rev(Bd.ap[:, CT:NT]), rev(R0.ap[:, CT:NT]), rev(Bd.ap[:, CT:NT]), st.ap[:, 1:2], [R0, Bd, st], [Bd])
                yield
            TT(P, "pool", R2.ap, R2.ap, R1.ap, ALU.add, [R2, R1], [R2])
            yield
            wt = load_w("win", 4 + c, buf=wbuf)
            for (c0, c1) in allp:
                n = c1 - c0
                ps = proj_fm(wt, H3, ht(c0, c1), c0, c1)
                t1 = ntmp()
                ACT(P, t1.ap[:, 0:n], ps.ap[:, 0:n], AF.Gelu_apprx_tanh, [ps], [t1])
                TT(P, "pool", YC[c].ap[:, c0:c1], t1.ap[:, 0:n], R2.ap[:, c0:c1], ALU.mult, [t1, R2], [YC[c]])
                yield

        def sc_chunk(c, wbuf):
            wt = load_w("win", 12 + c, buf=wbuf)
            for (c0, c1) in allp:
                ps = proj_fm(wt, H3, ht(c0, c1), c0, c1)
                CP(P, "act", S0.ap[:, c0:c1], ps.ap[:, 0:c1 - c0], [ps], [S0])
                yield
            wt = load_w("win", 16 + c, buf=wbuf)
            for (c0, c1) in allp:
                ps = proj_fm(wt, H3, ht(c0, c1), c0, c1)
                TT(P, "dve", S0.ap[:, c0:c1], ps.ap[:, 0:c1 - c0], S0.ap[:, c0:c1], ALU.mult, [ps, S0], [S0])
                yield
            for (lo, hi, j) in seqs:
                conv_seq(S1.ap, S0.ap, S1, S0, lo, hi, 0, 0, "scw", c * 3, 3, 1, scol("scb", c), lo, hi, center="act")
            yield
            wt = load_w("win", 8 + c, buf=wbuf)
            for (c0, c1) in allp:
                ps = proj_fm(wt, H3, ht(c0, c1), c0, c1)
                TT(P, "dve", YC[4 + c].ap[:, c0:c1], ps.ap[:, 0:c1 - c0], S1.ap[:, c0:c1], ALU.mult, [ps, S1], [YC[4 + c]])
                yield

        def chain(gs):
            for g in gs:
                yield from g
        gens = [("a", chain([rec_chunk(0, setA, WA[0]), rec_chunk(2, setA, WA[0])])),
                ("b", chain([rec_chunk(1, setB, WA[1]), rec_chunk(3, setB, WA[1])])),
                ("s", chain([sc_chunk(c, WA[2]) for c in range(4)]))]
        while gens:
            for item in list(gens):
                try:
                    next(item[1])
                except StopIteration:
                    gens.remove(item)
                    if item[0] == "s":
                        P.realias(borrowS, xsT[6:8])
                        reload_x([4, 5])
                    elif item[0] == "b":
                        P.realias(borrowB, xsT[0:6])
                        reload_x([0, 1, 2, 3])
        YC3 = AP(YC[0].ap.tensor, YC[0].ap.offset, [list(YC[0].ap.ap[0]), [NT, KC], [1, NT]])
        lnT = ([TM[0], TM[1]], TM[2], TM[3])
        proj_resid_ln(b, 0, 1, "wout0", KC, YC3, YC, seqs, 0, lnT, (6, 8), mc_order=[7, 6, 5, 4, 3, 2, 1, 0])

    def odd_mixer(b):
        specs = [(c * 4096, SEQ, BF16) for c in range(KC)]
        specs += [(32768, SEQ, BF16), (36864, NT, BF16), (41472, 18 * 192, BF16)]
        specs += [(48384 + i * 3584, 896, F32) for i in range(4)]
        specs += [(62720 + i * 2048, 512, F32) for i in range(3)]
        specs += [(68864 + i * 1024, 512, BF16) for i in range(8)]
        specs += [(77056 + i * 1024, 256, F32) for i in range(3)]
        ts = uphase(specs)
        OT = ts[0:KC]
        QT, KT, V = ts[KC:KC + 3]
        TB = ts[KC + 3:KC + 7]
        E = ts[KC + 7:KC + 10]
        PT = ts[KC + 10:KC + 18]
        RC = ts[KC + 18:KC + 20]
        OS = ts[KC + 20]
        V3 = V.ap.rearrange("p (t m) -> p t m", t=18)
        P.op("pool", lambda h: h.memset(V3[:, :, 64:128], 1.0), writes=[V])
        cnt = dict(e=0, p=0, r=0)
        j = b
        for hp in range(KC):
            wq = load_w("wqkv", hp)
            for (c0, c1) in pieces(CT, NT):
                ps = proj_fm(wq, H3, ht(c0, c1), c0, c1)
                CP(P, "act", QT.ap[:, c0 - CT:c1 - CT], ps.ap[:, 0:c1 - c0], [ps], [QT])
            wk = load_w("wqkv", 8 + hp)
            for (c0, c1) in pieces(0, NT):
                ps = proj_fm(wk, H3, ht(c0, c1), c0, c1)
                CP(P, "dve", KT.ap[:, c0:c1], ps.ap[:, 0:c1 - c0], [ps], [KT])
            wv = load_w("wqkv", 16 + hp)
            wv3 = wv.ap.rearrange("p (k m) -> p k m", k=KC)
            for t0 in range(0, 18, 4):
                nt_ = min(4, 18 - t0)
                ps = nb_()
                for ti in range(nt_):
                    tt = t0 + ti
                    for kc in range(KC):
                        MM(P, ps, ps.ap[:, ti * 128:(ti + 1) * 128], H3[:, kc, tt * 128:(tt + 1) * 128], wv3[:, kc, :],
                           kc == 0, kc == KC - 1, [wv] + ht(tt * 128, (tt + 1) * 128))
                ps3 = ps.ap[:, 0:nt_ * 128].rearrange("p (t m) -> p t m", t=nt_)
                CP(P, "act", V3[:, t0:t0 + nt_, 0:64], ps3[:, :, 0:64], [ps], [V])
                CP(P, "dve", V3[:, t0:t0 + nt_, 128:192], ps3[:, :, 64:128], [ps], [V])
            for par in range(2):
                h_ = 2 * hp + par
                P.dma("sp", TB[par * 2].ap, tmid[h_ * 128:(h_ + 1) * 128, :], writes=[TB[par * 2]])
                P.dma("sp", TB[par * 2 + 1].ap, tfull[h_ * 128:(h_ + 1) * 128, :], writes=[TB[par * 2 + 1]])
                ACT(P, TB[par * 2].ap, TB[par * 2].ap, AF.Exp, [TB[par * 2]], [TB[par * 2]])
                ACT(P, TB[par * 2 + 1].ap, TB[par * 2 + 1].ap, AF.Exp, [TB[par * 2 + 1]], [TB[par * 2 + 1]])
            items = [(par, r0) for par in range(2) for r0 in range(0, 32, 4)]

            def s_phase(par, r0):
                pb = par * 64
                q0 = r0 * 64
                if r0 == 0:
                    chunks, kind = [0, 1, 2, 3], 1
                elif r0 == 28:
                    chunks, kind = [12, 13, 14, 15], 1
                else:
                    a0 = (r0 - 4) // 2
                    chunks, kind = list(range(a0, a0 + 6)), 0
                tb = TB[par * 2 + kind]
                pts = []
                for i in range(0, len(chunks), 2):
                    ps = nb_()
                    for k in range(2):
                        a = chunks[i + k]
                        MM(P, ps, ps.ap[:, k * 256:(k + 1) * 256], KT.ap[pb:pb + 64, CT + a * 128:CT + (a + 1) * 128],
                           QT.ap[pb:pb + 64, q0:q0 + 256], True, True, [KT, QT])
                    e = E[cnt["e"] % 3]
                    cnt["e"] += 1
                    ACT(P, e.ap, ps.ap, AF.Exp, [ps], [e], scale=0.125)
                    pt = PT[cnt["p"] % 8]
                    cnt["p"] += 1
                    ei0 = r0 - 2 * chunks[i] + 6
                    tap = AP(tb.ap.tensor, tb.ap[:, ei0 * 64:ei0 * 64 + 1].offset, [list(tb.ap.ap[0]), [-128, 2], [1, 256]])
                    TT(P, "pool" if (i // 2) == 1 else "dve", pt.ap.rearrange("p (a b) -> p a b", a=2), e.ap.rearrange("p (a b) -> p a b", a=2), tap, ALU.mult, [e, tb], [pt])
                    pts.append((pt, chunks[i], chunks[i + 1]))
                ps = nb_()
                for k in range(2):
                    MM(P, ps, ps.ap[:, k * 256:(k + 1) * 256], KT.ap[pb:pb + 64, k * 128:(k + 1) * 128],
                       QT.ap[pb:pb + 64, q0:q0 + 256], True, True, [KT, QT])
                ptc = PT[cnt["p"] % 8]
                cnt["p"] += 1
                ACT(P, ptc.ap, ps.ap, AF.Exp, [ps], [ptc], scale=0.125)
                return (par, r0, pts, ptc)

            def pv_phase(st):
                par, r0, pts, ptc = st
                pb = par * 64
                q0 = r0 * 64
                pso = nb_()
                ob = 64 - pb
                mms = []
                for (pt, a, a2) in pts:
                    mms.append((V3[:, 2 + a, pb:pb + 128], pt.ap[:, 0:256], pt))
                    mms.append((V3[:, 2 + a2, pb:pb + 128], pt.ap[:, 256:512], pt))
                mms.append((V3[:, 0, pb:pb + 128], ptc.ap[:, 0:256], ptc))
                mms.append((V3[:, 1, pb:pb + 128], ptc.ap[:, 256:512], ptc))
                nm_ = len(mms)
                for i, (l_, r_, pt) in enumerate(mms):
                    MM(P, pso, pso.ap[:, 0:256], l_, r_, i == 0, i == nm_ - 1, [V, pt])
                rc = RC[cnt["r"] % 2]
                cnt["r"] += 1
                rl = OS
                ACT(P, rl.ap[pb:pb + 64, :], pso.ap[ob:ob + 64, 0:256], AF.Ln, [pso], [rl])
                CP(P, "dve", rc.ap[pb:pb + 64, :], rl.ap[pb:pb + 64, :], [rl], [rc])
                ACT(P, rc.ap[pb:pb + 64, :], rc.ap[pb:pb + 64, :], AF.Exp, [rc], [rc], scale=-1.0)
                TT(P, "dve", OT[hp].ap[pb:pb + 64, q0:q0 + 256], pso.ap[pb:pb + 64, 0:256], rc.ap[pb:pb + 64, :], ALU.mult, [pso, rc], [OT[hp]])

            prev = None
            for (par, r0) in items:
                st = s_phase(par, r0)
                if prev is not None:
                    pv_phase(prev)
                prev = st
            pv_phase(prev)
        OT3 = AP(OT[0].ap.tensor, OT[0].ap.offset, [list(OT[0].ap.ap[0]), [SEQ, KC], [1, SEQ]])
        lnT = ([E[0], E[1]], E[2], T(ucarve(68864, 512, F32)))
        P.realias([lnT[2]], PT)
        u_live.append(lnT[2])
        proj_resid_ln(b, 1, 1, "wout1", KC, OT3, OT, seqs_of(b, False), CT, lnT, (6 + 1, 8 + 1))

    for b in range(NB):
        for c in range(KC):
            q = "sp" if c % 2 == 0 else "pool"
            P.dma(q, X3[:, c, 0:CT], ctxT[b, c], writes=[XR[c][0]])
            P.dma(q, X3[:, c, CT:CT + 1024], xT[b, c][:, 0:1024], writes=[XR[c][1]])
            P.dma(q, X3[:, c, CT + 1024:NT], xT[b, c][:, 1024:2048], writes=[XR[c][2]])
        for c in range(KC):
            for r, (lo, hi) in enumerate(REG):
                j = NB if r == 0 else b
                ACT(P, H3[:, c, lo:hi], X3[:, c, lo:hi], AF.Identity, [XR[c][r], DER, MODS], [HR[c][r]], bias=mods(0, j, c), scale=der(0, j, c))
        if stop != "load":
            even_mixer(b)
        nm0 = (10, 11) if layers == 2 else None
        if stop is None:
            ffn_stage(b, 0, [(0, CT, NB, 0, CT), (CT, CT + 1024, b, CT, NT), (CT + 1024, NT, b, CT, NT)], nm0)

        if layers == 2:
            odd_mixer(b)
            ffn_stage(b, 1, [(CT, CT + 1024, b, CT, NT), (CT + 1024, NT, b, CT, NT)], None)
        if not (layers == 2 and stop is None):
            for c in range(KC):
                q = "sp" if c % 2 == 0 else "pool"
                P.dma(q, yT[b, c], X3[:, c, CT:NT], reads=[XR[c][1], XR[c][2]])
    allx = [t for c in range(KC) for t in XR[c]]
    P.wait_all("sp", allx)
    P.worder = wstate["seen"]
    P.finish()
    return nc, P


def _wt(W):
    K, M = W.shape
    return np.ascontiguousarray(W.reshape(K // 128, 128, M // 128, 128).transpose(2, 1, 0, 3)).reshape(M, K)


def _pc(v, nchunk):
    v = np.asarray(v, np.float32)
    lead = v.shape[:-1]
    return np.moveaxis(v.reshape(lead + (nchunk, 128)), -1, 0)


def _tables(rpb):
    kr2 = np.arange(2)[:, None, None, None]
    kcol = np.arange(64)[None, :, None, None]
    e = (np.arange(14) - 6)[None, None, :, None]
    qcol = np.arange(64)[None, None, None, :]
    dr = kr2 - e + 7 + 0 * kcol + 0 * qcol
    dc = kcol - qcol + 15 + 0 * kr2 + 0 * e
    cs = np.clip(qcol - 8, 0, 48)
    colok = (kcol >= cs) & (kcol < cs + 16)
    drok = (dr >= 0) & (dr <= 14)
    rowmid = (kr2 - e >= -4) & (kr2 - e <= 3)
    g = rpb[:, np.clip(dr, 0, 14), np.clip(dc, 0, 30)]
    negs = np.full(g.shape, NEG, np.float32)
    tfull = np.where((colok & drok)[None], g, negs).reshape(16 * 128, 896)
    tmid = np.where((colok & drok & rowmid)[None], g, negs).reshape(16 * 128, 896)
    return np.ascontiguousarray(tmid, np.float32), np.ascontiguousarray(tfull, np.float32)


def prep_shared(mod_w, mod_b, ln1_g, ln1_b, ln2_g, ln2_b, ev_w_in, ev_w_out, rec_conv_w, rec_conv_b, rec_wa, rec_ba,
                rec_wx, rec_bx, rec_lam, sc_conv_w, sc_conv_b, na_w_qkv, na_w_out, na_rpb, ffn_w_gate, ffn_w_up,
                ffn_conv_w, ffn_conv_b, ffn_w_down):
    sh = {}
    sh["win"] = _wt(ev_w_in[0])
    sh["wout0"] = _wt(ev_w_out[0])
    sh["wqkv"] = _wt(na_w_qkv[0])
    sh["wout1"] = _wt(na_w_out[0])
    for l in range(2):
        sh["wg%d" % l] = _wt(ffn_w_gate[l])
        sh["wu%d" % l] = _wt(ffn_w_up[l])
        sh["wd%d" % l] = _wt(ffn_w_down[l])
    sh["modw"] = np.concatenate([_wt(mod_w[l]) for l in range(2)], axis=0)
    small = np.zeros((128, NS), np.float32)

    def put(nm, arr):
        arr = np.asarray(arr, np.float32).reshape(128, -1)
        small[:, SOFF[nm]:SOFF[nm] + arr.shape[1]] = arr
    put("modb", _pc(mod_b, 48))
    put("ln1g", _pc(ln1_g, 8)); put("ln1b", _pc(ln1_b, 8)); put("ln2g", _pc(ln2_g, 8)); put("ln2b", _pc(ln2_b, 8))
    put("rcw", np.transpose(_pc(rec_conv_w[0], 4), (0, 2, 1)))
    put("rcb", _pc(rec_conv_b[0], 4))
    put("rba", _pc(rec_ba[0], 4)); put("rbx", _pc(rec_bx[0], 4)); put("rlam", _pc(rec_lam[0], 4))
    put("scw", np.transpose(_pc(sc_conv_w[0], 4), (0, 2, 1)))
    put("scb", _pc(sc_conv_b[0], 4))
    put("fcw", np.transpose(_pc(ffn_conv_w, FC), (0, 1, 3, 2)))
    put("fcb", _pc(ffn_conv_b, FC))
    sh["smallp"] = small
    gw = np.zeros((128, 16, 128), np.float32)
    for d in range(2):
        for wi, W in enumerate((rec_wa[0], rec_wx[0])):
            for c in range(4):
                g = (d * 2 + wi) * 4 + c
                gw[0:64, g, 0:64] = W[d, 2 * c]
                gw[64:128, g, 64:128] = W[d, 2 * c + 1]
    sh["gw"] = gw.reshape(128, 16 * 128)
    sh["tmid"], sh["tfull"] = _tables(np.asarray(na_rpb[0], np.float32))
    return sh


def prep_core(x, c, ctx, c_ctx, b0, NB):
    m = {}
    m["xT"] = np.ascontiguousarray(x[b0:b0 + NB].transpose(0, 2, 1)).reshape(NB, KC, 128, SEQ)
    m["ctxT"] = np.ascontiguousarray(ctx[b0:b0 + NB].transpose(0, 2, 1)).reshape(NB, KC, 128, CT)
    cc = np.concatenate([c[b0:b0 + NB], c_ctx[None, :]], axis=0)
    m["cT"] = np.ascontiguousarray(np.transpose(cc.reshape(NB + 1, KC, 128), (2, 1, 0))).reshape(128, KC * (NB + 1))
    return m


_CACHE = {}


def kernel(x, c, ctx, c_ctx, mod_w, mod_b, ln1_g, ln1_b, ln2_g, ln2_b, ev_w_in, ev_w_out, rec_conv_w, rec_conv_b,
           rec_wa, rec_ba, rec_wx, rec_bx, rec_lam, sc_conv_w, sc_conv_b, na_w_qkv, na_w_out, na_rpb, ffn_w_gate,
           ffn_w_up, ffn_conv_w, ffn_conv_b, ffn_w_down):
    f = lambda a: np.asarray(a, np.float32)
    x, c, ctx, c_ctx = f(x), f(c), f(ctx), f(c_ctx)
    B = x.shape[0]
    NB = B // NCORES
    sh = prep_shared(f(mod_w), f(mod_b), f(ln1_g), f(ln1_b), f(ln2_g), f(ln2_b), f(ev_w_in), f(ev_w_out), f(rec_conv_w),
                     f(rec_conv_b), f(rec_wa), f(rec_ba), f(rec_wx), f(rec_bx), f(rec_lam), f(sc_conv_w), f(sc_conv_b),
                     f(na_w_qkv), f(na_w_out), f(na_rpb), f(ffn_w_gate), f(ffn_w_up), f(ffn_conv_w), f(ffn_conv_b), f(ffn_w_down))
    if NB not in _CACHE:
        _, p0 = build_program(1)
        _CACHE[NB] = build_program(NB, order=p0.worder)[0]
    nc = _CACHE[NB]
    in_maps = []
    for i in range(NCORES):
        m = dict(sh)
        m.update(prep_core(x, c, ctx, c_ctx, i * NB, NB))
        in_maps.append(m)
    res = run_bass_kernel_spmd(nc, in_maps, core_ids=list(range(NCORES)))
    out = np.empty((B, SEQ, D), np.float32)
    for i in range(NCORES):
        y = np.asarray(res.results[i]["yT"]).reshape(NB, D, SEQ)
        out[i * NB:(i + 1) * NB] = y.transpose(0, 2, 1)
    return out
```

```python
import numpy as np
import concourse.bass as bass
import concourse.mybir as mybir
from concourse.bass_utils import run_bass_kernel_spmd
from concourse.ap import AP

F32 = mybir.dt.float32
BF16 = mybir.dt.bfloat16
ALU = mybir.AluOpType
AF = mybir.ActivationFunctionType

NCORES = 8
D = 1024
KC = 8
SEQ = 2048
CT = 256
NT = SEQ + CT
DFF = 2816
FC = 22
ALPHA = 4.0 ** 0.25
EPS2 = 1e-6 / (ALPHA * ALPHA)
NEG = -30000.0


class T:
    __slots__ = ("ap", "w", "r")

    def __init__(self, ap):
        self.ap = ap
        self.w = None
        self.r = {}


class Eng:
    def __init__(self, name, h, sem):
        self.name = name
        self.h = h
        self.sem = sem
        self.count = 0
        self.waited = {}
        self.q = []


class Prog:
    def __init__(self, nc, n_dma_sems=8):
        self.nc = nc
        self._ctx = []
        self.engs = {}
        for nm, h in (("pe", nc.tensor), ("act", nc.scalar), ("dve", nc.vector), ("pool", nc.gpsimd), ("sp", nc.sync)):
            self.engs[nm] = Eng(nm, h, self._sem("e_" + nm))
        self.n_dma_sems = n_dma_sems
        self.dma_pool = {}
        for nm in ("sp", "pool"):
            self.dma_pool[nm] = dict(sems=[self._sem("d_%s%d" % (nm, i)) for i in range(n_dma_sems)], j=0)
        self.ninst = 0
        self.force_own = False

    def _sem(self, name):
        cm = self.nc.semaphore(name)
        s = cm.__enter__()
        self._ctx.append(cm)
        return (name, s)

    def sbuf(self, name, shape, dt):
        cm = self.nc.sbuf_tensor(name, shape, dt)
        t = cm.__enter__()
        self._ctx.append(cm)
        return t

    def psum(self, name, shape, dt):
        cm = self.nc.psum_tensor(name, shape, dt)
        t = cm.__enter__()
        self._ctx.append(cm)
        return t

    @staticmethod
    def _deps(reads, writes):
        deps = {}

        def add(k, v):
            if deps.get(k, 0) < v:
                deps[k] = v
        for t in reads:
            if t.w is not None:
                add(*t.w)
        for t in writes:
            if t.w is not None:
                add(*t.w)
            for k, v in t.r.items():
                add(k, v)
        return deps

    def _waits(self, e, deps, own_too):
        for (name, sh), v in deps.items():
            if (not own_too) and name == e.sem[0]:
                continue
            if e.waited.get(name, 0) >= v:
                continue
            e.waited[name] = v
            e.q.append(lambda h, sh=sh, v=v: h.wait_ge(sh, v))
            self.ninst += 1

    def op(self, eng, fn, reads=(), writes=(), sig=True, own=False):
        e = self.engs[eng]
        self._waits(e, self._deps(reads, writes), own or self.force_own)
        if sig:
            e.count += 1
            val = e.count
            sh = e.sem[1]
            e.q.append(lambda h, fn=fn, sh=sh: fn(h).then_inc(sh, 1))
        else:
            val = e.count + 1
            e.q.append(lambda h, fn=fn: fn(h))
        self.ninst += 1
        for t in reads:
            if t.r.get(e.sem, 0) < val:
                t.r[e.sem] = val
        for t in writes:
            t.w = (e.sem, val)
            t.r = {}

    def dma(self, q, out_ap, in_ap, reads=(), writes=()):
        e = self.engs[q]
        pool = self.dma_pool[q]
        j = pool["j"]
        pool["j"] += 1
        s = pool["sems"][j % self.n_dma_sems]
        gen = j // self.n_dma_sems
        deps = self._deps(reads, writes)
        if gen > 0 and deps.get(s, 0) < 16 * gen:
            deps[s] = 16 * gen
        self._waits(e, deps, True)
        val = 16 * (gen + 1)
        sh = s[1]
        e.q.append(lambda h, o=out_ap, i=in_ap, sh=sh: h.dma_start(out=o, in_=i).then_inc(sh, 16))
        self.ninst += 1
        for t in reads:
            if t.r.get(s, 0) < val:
                t.r[s] = val
        for t in writes:
            t.w = (s, val)
            t.r = {}

    def realias(self, new_ts, old_ts):
        merged = {}
        for t in old_ts:
            if t.w is not None and merged.get(t.w[0], 0) < t.w[1]:
                merged[t.w[0]] = t.w[1]
            for k, v in t.r.items():
                if merged.get(k, 0) < v:
                    merged[k] = v
        for t in new_ts:
            t.w = None
            t.r = dict(merged)

    def wait_all(self, eng, ts):
        e = self.engs[eng]
        self._waits(e, self._deps(ts, ts), True)

    def finish(self):
        nc = self.nc
        with nc.Block() as block:
            def mk(e):
                def f(h):
                    for c in e.q:
                        c(h)
                return f
            block.tensor(mk(self.engs["pe"]))
            block.scalar(mk(self.engs["act"]))
            block.vector(mk(self.engs["dve"]))
            block.gpsimd(mk(self.engs["pool"]))
            block.sync(mk(self.engs["sp"]))
        for cm in reversed(self._ctx):
            cm.__exit__(None, None, None)
        self._ctx = []


def MM(P, psT, out, lhsT, rhs, start, stop, reads):
    P.op("pe", lambda h: h.matmul(out, lhsT, rhs, start=start, stop=stop), reads=reads, writes=[psT], sig=True)


def ACT(P, out, in_, func, reads, writes, bias=None, scale=None):
    kw = {}
    if bias is not None:
        kw["bias"] = bias
    if scale is not None:
        kw["scale"] = scale
    P.op("act", lambda h: h.activation(out, in_, func, **kw), reads=reads, writes=writes)


def TT(P, eng, out, in0, in1, op, reads, writes):
    P.op(eng, lambda h: h.tensor_tensor(out, in0, in1, op), reads=reads, writes=writes)


def TS(P, eng, out, in0, s1, s2, op0, op1, reads, writes):
    if s2 is None:
        P.op(eng, lambda h: h.tensor_scalar(out, in0, s1, None, op0), reads=reads, writes=writes)
    else:
        P.op(eng, lambda h: h.tensor_scalar(out, in0, s1, s2, op0, op1), reads=reads, writes=writes)


def STT(P, eng, out, in0, sc, in1, op0, op1, reads, writes):
    P.op(eng, lambda h: h.scalar_tensor_tensor(out, in0, sc, in1, op0, op1), reads=reads, writes=writes)


def CP(P, eng, out, in_, reads, writes):
    if eng == "act":
        P.op("act", lambda h: h.copy(out, in_), reads=reads, writes=writes)
    else:
        P.op(eng, lambda h: h.tensor_copy(out, in_), reads=reads, writes=writes)


def SCAN(P, out, d0, d1, init, reads, writes):
    P.op("dve", lambda h: h.tensor_tensor_scan(out, d0, d1, init, ALU.mult, ALU.add), reads=reads, writes=writes)


def rev(ap2d):
    n = ap2d.shape[1]
    last = ap2d[:, n - 1:n]
    return AP(last.tensor, last.offset, [list(last.ap[0]), [-1, n]])


def pieces(lo, hi, step=512):
    n = -(-(hi - lo) // step)
    size = -(-(hi - lo) // n)
    size += size % 2
    out = []
    c = lo
    while c < hi:
        out.append((c, min(hi, c + size)))
        c += size
    return out


def small_layout():
    off = {}
    o = 0
    for nm, n in (("modb", 96), ("ln1g", 16), ("ln1b", 16), ("ln2g", 16), ("ln2b", 16), ("rcw", 16), ("rcb", 4),
                  ("rba", 8), ("rbx", 8), ("rlam", 8), ("scw", 12), ("scb", 4), ("fcw", 132), ("fcb", 44)):
        off[nm] = o
        o += n
    return off, o


SOFF, NS = small_layout()


def build_program(NB, layers=2, stop=None, order=None):
    nc = bass.Bass("TRN2", target_bir_lowering=False)
    NJ = NB + 1

    def din(name, shape, dt=F32):
        return nc.dram_tensor(name, list(shape), dt, kind="ExternalInput").ap()

    xT = din("xT", [NB, KC, 128, SEQ])
    ctxT = din("ctxT", [NB, KC, 128, CT])
    cT = din("cT", [128, KC * NJ])
    smallp = din("smallp", [128, NS])
    modw = din("modw", [2 * 48 * 128, 1024])
    gw_in = din("gw", [128, 16 * 128])
    tmid = din("tmid", [16 * 128, 896])
    tfull = din("tfull", [16 * 128, 896])
    wspec = [("win", 20, 1024), ("wout0", 8, 1024), ("wg0", FC, 1024), ("wu0", FC, 1024), ("wd0", 8, DFF),
             ("wqkv", 24, 1024), ("wout1", 8, 1024), ("wg1", FC, 1024), ("wu1", FC, 1024), ("wd1", 8, DFF)]
    wsrc = {}
    wscr = {}
    wscrT = {}
    for nm, nmc, ncol in wspec:
        wsrc[nm] = din(nm, [nmc * 128, ncol])
        wscr[nm] = nc.dram_tensor(nm + "_s", [nmc * 128, ncol], BF16, kind="Internal").ap()
        wscrT[nm] = [T(wscr[nm][m * 128:(m + 1) * 128, :]) for m in range(nmc)]
    yT = nc.dram_tensor("yT", [NB, KC, 128, SEQ], F32, kind="ExternalOutput").ap()

    P = Prog(nc)
    Xs = P.sbuf("X", [128, KC * NT], F32)
    Hs = P.sbuf("H", [128, KC * NT], BF16)
    UBYTES = 81920
    Us = P.sbuf("U", [128, UBYTES // 4], F32)
    X3 = Xs[:].rearrange("p (c t) -> p c t", c=KC)
    H3 = Hs[:].rearrange("p (c t) -> p c t", c=KC)
    REG = [(0, CT), (CT, CT + 1024), (CT + 1024, NT)]
    XR = [[T(X3[:, c, lo:hi]) for (lo, hi) in REG] for c in range(KC)]
    HR = [[T(H3[:, c, lo:hi]) for (lo, hi) in REG] for c in range(KC)]

    def xt(mc, c0, c1):
        return [XR[mc][r] for r, (lo, hi) in enumerate(REG) if c0 < hi and c1 > lo]

    def ht1(mc, c0, c1):
        return [HR[mc][r] for r, (lo, hi) in enumerate(REG) if c0 < hi and c1 > lo]

    def ht(c0, c1):
        return [t for mc in range(KC) for t in ht1(mc, c0, c1)]
    WA = [T(P.sbuf("wa%d" % i, [128, 1024], BF16)[:]) for i in range(3)]
    wa_i = [0]
    smallT = T(P.sbuf("smallp_sb", [128, NS], F32)[:])
    cvec = T(P.sbuf("cvec", [128, 4], F32)[:])
    sttA = T(P.sbuf("sttA", [128, 2], F32)[:])
    sttB = T(P.sbuf("sttB", [128, 2], F32)[:])
    onesD = T(P.sbuf("onesD", [128, 128], F32)[:])
    onesB = T(P.sbuf("onesB", [128, 128], BF16)[:])
    scT = T(P.sbuf("scT", [128, KC * NJ], F32)[:])
    MODS = T(P.sbuf("MODS", [128, 2 * NJ * 48], F32)[:])
    DER = T(P.sbuf("DER", [128, 12 * NJ * 8], F32)[:])
    CLAM = T(P.sbuf("CLAM", [128, 8], F32)[:])
    GW = T(P.sbuf("GW", [128, 16 * 128], BF16)[:])
    banks = [T(P.psum("ps%d" % i, [128, 512], F32)[:]) for i in range(8)]
    bank_i = [0]

    def nb_():
        t = banks[bank_i[0] % 8]
        bank_i[0] += 1
        return t

    def ucarve(off_bytes, n, dt):
        if dt == F32:
            return Us[:, off_bytes // 4: off_bytes // 4 + n]
        return Us[:, off_bytes // 4: off_bytes // 4 + (n + 1) // 2].bitcast(BF16)[:, :n]

    u_live = []

    def uphase(specs):
        new = [T(ucarve(o, n, dt)) for (o, n, dt) in specs]
        P.realias(new, u_live)
        u_live[:] = new
        return new

    sm = smallT.ap

    def scol(nm, i):
        o = SOFF[nm] + i
        return sm[:, o:o + 1]

    def mods(l, j, q):
        o = (l * NJ + j) * 48 + q
        return MODS.ap[:, o:o + 1]

    def der(k, j, c):
        o = (k * NJ + j) * 8 + c
        return DER.ap[:, o:o + 1]

    worder = list(order) if order is not None else []
    wpos = {k: i for i, k in enumerate(worder)}
    wstate = dict(done=0, seen=[])
    LOOK = 12

    def prepass_upto(k):
        while wstate["done"] < min(k, len(worder)):
            nm, m = worder[wstate["done"]]
            P.dma("pool", wscr[nm][m * 128:(m + 1) * 128, :], wsrc[nm][m * 128:(m + 1) * 128, :], writes=[wscrT[nm][m]])
            wstate["done"] += 1
    if order is None:
        for nm, nmc, ncol in wspec:
            if layers == 1 and nm in ("wqkv", "wout1", "wg1", "wu1", "wd1"):
                continue
            for m in range(nmc):
                P.dma("pool", wscr[nm][m * 128:(m + 1) * 128, :], wsrc[nm][m * 128:(m + 1) * 128, :], writes=[wscrT[nm][m]])
    P.dma("sp", smallT.ap, smallp, writes=[smallT])
    P.dma("sp", scT.ap, cT, writes=[scT])
    P.dma("pool", GW.ap, gw_in, writes=[GW])
    prepass_upto(LOOK)
    P.op("dve", lambda h: h.memset(cvec.ap[:, 0:1], 1.0), writes=[cvec])
    P.op("dve", lambda h: h.memset(cvec.ap[:, 1:2], EPS2), writes=[cvec])
    P.op("dve", lambda h: h.memset(cvec.ap[:, 2:3], 0.0), writes=[cvec])
    P.op("dve", lambda h: h.memset(onesD.ap, 1.0 / D), writes=[onesD])
    P.op("dve", lambda h: h.memset(onesB.ap, 1.0), writes=[onesB])
    ONE = cvec.ap[:, 0:1]
    EPSc = cvec.ap[:, 1:2]
    ACT(P, scT.ap, scT.ap, AF.Silu, [scT], [scT])
    ACT(P, CLAM.ap, sm[:, SOFF["rlam"]:SOFF["rlam"] + 8], AF.Exp, [smallT], [CLAM], scale=-1.0)
    TS(P, "dve", CLAM.ap, CLAM.ap, 1.0, None, ALU.add, None, [CLAM], [CLAM])
    ACT(P, CLAM.ap, CLAM.ap, AF.Ln, [CLAM], [CLAM])
    TS(P, "dve", CLAM.ap, CLAM.ap, -8.0, None, ALU.mult, None, [CLAM], [CLAM])
    mwb = uphase([(i * 2048, 1024, BF16) for i in range(6)])
    scTb = T(P.sbuf("scTb", [128, KC * NJ], BF16)[:])
    CP(P, "dve", scTb.ap, scT.ap, [scT], [scTb])
    sc3 = scTb.ap.rearrange("p (c j) -> p c j", c=KC)
    for l in range(layers):
        for q in range(48):
            wt = mwb[(l * 48 + q) % 6]
            r0 = (l * 48 + q) * 128
            P.dma("pool", wt.ap, modw[r0:r0 + 128, :], writes=[wt])
            ps = nb_()
            w3 = wt.ap.rearrange("p (k m) -> p k m", k=KC)
            for kc in range(KC):
                MM(P, ps, ps.ap[:, 0:NJ], w3[:, kc, :], sc3[:, kc, :], kc == 0, kc == KC - 1, [wt, scTb])
            o0 = l * NJ * 48 + q
            outap = AP(MODS.ap.tensor, MODS.ap[:, o0:o0 + 1].offset, [list(MODS.ap.ap[0]), [48, NJ]])
            TS(P, "dve", outap, ps.ap[:, 0:NJ], scol("modb", l * 48 + q), None, ALU.add, None, [ps, smallT], [MODS])
    for l in range(layers):
        for j in range(NJ):
            def M8(which):
                o = (l * NJ + j) * 48 + which * 8
                return MODS.ap[:, o:o + 8]

            def D8(k):
                o = (k * NJ + j) * 8
                return DER.ap[:, o:o + 8]
            ln1g = sm[:, SOFF["ln1g"] + l * 8: SOFF["ln1g"] + l * 8 + 8]
            ln1b = sm[:, SOFF["ln1b"] + l * 8: SOFF["ln1b"] + l * 8 + 8]
            TS(P, "dve", D8(0 + l), M8(1), 1.0, None, ALU.add, None, [MODS], [DER])
            TS(P, "dve", D8(2 + l), M8(2), 1.0 / ALPHA, None, ALU.mult, None, [MODS], [DER])
            TS(P, "dve", D8(4 + l), M8(5), 1.0 / ALPHA, None, ALU.mult, None, [MODS], [DER])
            TS(P, "dve", D8(8 + l), M8(4), 1.0, None, ALU.add, None, [MODS], [DER])
            TT(P, "pool", D8(6 + l), D8(8 + l), ln1g, ALU.mult, [DER, smallT], [DER])
            TT(P, "pool", D8(8 + l), D8(8 + l), ln1b, ALU.mult, [DER, smallT], [DER])
            TT(P, "dve", D8(8 + l), D8(8 + l), M8(3), ALU.add, [DER, MODS], [DER])
    if layers == 2:
        for j in range(NJ):
            ln2g = sm[:, SOFF["ln2g"]: SOFF["ln2g"] + 8]
            ln2b = sm[:, SOFF["ln2b"]: SOFF["ln2b"] + 8]
            A1n = DER.ap[:, (1 * NJ + j) * 8:(1 * NJ + j) * 8 + 8]
            SH1n = MODS.ap[:, (1 * NJ + j) * 48: (1 * NJ + j) * 48 + 8]
            g1p = DER.ap[:, (10 * NJ + j) * 8:(10 * NJ + j) * 8 + 8]
            b1p = DER.ap[:, (11 * NJ + j) * 8:(11 * NJ + j) * 8 + 8]
            TT(P, "pool", g1p, A1n, ln2g, ALU.mult, [DER, smallT], [DER])
            TT(P, "pool", b1p, A1n, ln2b, ALU.mult, [DER, smallT], [DER])
            TT(P, "dve", b1p, b1p, SH1n, ALU.add, [DER, MODS], [DER])

    def load_w(nm, m, ncol=1024, buf=None):
        if (nm, m) not in wstate["seen"]:
            wstate["seen"].append((nm, m))
        if order is not None:
            prepass_upto(wpos[(nm, m)] + 1 + LOOK)
        if buf is None:
            buf = WA[wa_i[0] % 3]
            wa_i[0] += 1
        P.dma("sp", buf.ap[:, 0:ncol], wscr[nm][m * 128:(m + 1) * 128, :], reads=[wscrT[nm][m]], writes=[buf])
        return buf

    def proj_fm(wt, rhs3, rhsT, c0, c1, nk=KC):
        ps = nb_()
        w3 = wt.ap[:, 0:nk * 128].rearrange("p (k m) -> p k m", k=nk)
        for kc in range(nk):
            MM(P, ps, ps.ap[:, 0:c1 - c0], w3[:, kc, :], rhs3[:, kc, c0:c1], kc == 0, kc == nk - 1, [wt] + rhsT)
        return ps

    def seqs_of(b, with_ctx):
        s = []
        if with_ctx:
            s.append((0, CT, NB))
        s.append((CT, NT, b))
        return s

    def layer_norm_piece(c0, c1, j, l, which, lnT, nextmod, hskip=0):
        n = c1 - c0
        SQ, MS, RS = lnT
        psm = nb_()
        pse = nb_()
        for mc in range(KC):
            sq = SQ[mc % 2]
            ACT(P, sq.ap[:, 0:n], X3[:, mc, c0:c1], AF.Square, xt(mc, c0, c1), [sq])
            MM(P, psm, psm.ap[:, 0:n], onesD.ap, X3[:, mc, c0:c1], mc == 0, mc == KC - 1, [onesD] + xt(mc, c0, c1))
            MM(P, pse, pse.ap[:, 0:n], onesD.ap, sq.ap[:, 0:n], mc == 0, mc == KC - 1, [onesD, sq])
        ACT(P, MS.ap[:, 0:n], psm.ap[:, 0:n], AF.Square, [psm], [MS])
        TT(P, "dve", MS.ap[:, 0:n], pse.ap[:, 0:n], MS.ap[:, 0:n], ALU.subtract, [pse, MS], [MS])
        ACT(P, MS.ap[:, 0:n], MS.ap[:, 0:n], AF.Sqrt, [MS, cvec], [MS], bias=EPSc, scale=1.0)
        P.op("dve", lambda h: h.reciprocal(RS.ap[:, 0:n], MS.ap[:, 0:n]), reads=[MS], writes=[RS])
        gname = "ln1g" if which == 1 else "ln2g"
        bname = "ln1b" if which == 1 else "ln2b"
        for mc in range(KC):
            xs = X3[:, mc, c0:c1]
            TT(P, "dve", xs, xs, psm.ap[:, 0:n], ALU.subtract, xt(mc, c0, c1) + [psm], xt(mc, c0, c1))
            TT(P, "pool", xs, xs, RS.ap[:, 0:n], ALU.mult, xt(mc, c0, c1) + [RS], xt(mc, c0, c1))
            if nextmod is not None:
                gk, bk = nextmod
                ACT(P, H3[:, mc, c0:c1 - hskip], X3[:, mc, c0:c1 - hskip], AF.Identity, xt(mc, c0, c1) + [DER], ht1(mc, c0, c1), bias=der(bk, j, mc), scale=der(gk, j, mc))
            ACT(P, xs, xs, AF.Identity, xt(mc, c0, c1) + [smallT], xt(mc, c0, c1), bias=scol(bname, l * 8 + mc), scale=scol(gname, l * 8 + mc))

    def proj_resid_ln(b, l, which, wname, nk, rhs3, rhsT, seqs, col_shift, lnT, nextmod, wbufs=None):
        gk = (2 if which == 1 else 4) + l
        for (lo, hi, j) in seqs:
            for (c0, c1) in pieces(lo, hi):
                for mc in range(KC):
                    if wbufs is None:
                        wt = load_w(wname, mc)
                    else:
                        wt = load_w(wname, mc, ncol=nk * 128, buf=wbufs[mc % 2])
                    ps = nb_()
                    w3 = wt.ap[:, 0:nk * 128].rearrange("p (k m) -> p k m", k=nk)
                    for kc in range(nk):
                        MM(P, ps, ps.ap[:, 0:c1 - c0], w3[:, kc, :], rhs3[:, kc, c0 - col_shift:c1 - col_shift],
                           kc == 0, kc == nk - 1, [wt] + rhsT)
                    xs = X3[:, mc, c0:c1]
                    STT(P, "dve", xs, ps.ap[:, 0:c1 - c0], der(gk, j, mc), xs, ALU.mult, ALU.add, [ps, DER] + xt(mc, c0, c1), xt(mc, c0, c1))
                layer_norm_piece(c0, c1, j, l, which, lnT, nextmod)

    def conv_seq(dst, src, dstT, srcT, lo, hi, dlo, slo, wname, wbase, K, padl, bias_ap, seq_lo, seq_hi, eng="dve"):
        ctr = padl
        TS(P, eng, dst[:, lo - dlo:hi - dlo], src[:, lo - slo:hi - slo], scol(wname, wbase + ctr), bias_ap, ALU.mult, ALU.add,
           [srcT, smallT], [dstT])
        for k in range(K):
            if k == ctr:
                continue
            o = k - padl
            t0 = max(lo, seq_lo - o)
            t1 = min(hi, seq_hi - o)
            if t1 <= t0:
                continue
            STT(P, eng, dst[:, t0 - dlo:t1 - dlo], src[:, t0 + o - slo:t1 + o - slo], scol(wname, wbase + k), dst[:, t0 - dlo:t1 - dlo],
                ALU.mult, ALU.add, [srcT, smallT, dstT], [dstT])

    def ffn_stage(b, l, seqs_super, nextmod):
        specs = [(fc * 2048, 1024, BF16) for fc in range(FC)]
        specs += [(45056, 1056, F32), (49280, 1024, F32), (53376, DFF, BF16), (59008, DFF, BF16)]
        specs += [(64640 + i * 2048, 512, F32) for i in range(4)]
        ts = uphase(specs)
        A = ts[0:FC]
        G, CV = ts[FC], ts[FC + 1]
        WD = ts[FC + 2:FC + 4]
        lnT = (ts[FC + 4:FC + 6], ts[FC + 6], ts[FC + 7])
        wg, wu, wd = "wg%d" % l, "wu%d" % l, "wd%d" % l
        patch = []
        for (lo, hi, j, seq_lo, seq_hi) in seqs_super:
            glo = max(seq_lo, lo - 1)
            ghi = min(seq_hi, hi + 1)
            for fc in range(FC):
                wtg = load_w(wg, fc)
                for (c0, c1) in pieces(glo, ghi):
                    ps = proj_fm(wtg, H3, ht(c0, c1), c0, c1)
                    CP(P, "act", G.ap[:, c0 - glo:c1 - glo], ps.ap[:, 0:c1 - c0], [ps], [G])
                conv_seq(CV.ap, G.ap, CV, G, lo, hi, lo, glo, "fcw", (l * FC + fc) * 3, 3, 1, scol("fcb", l * FC + fc), seq_lo, seq_hi)
                ACT(P, CV.ap[:, 0:hi - lo], CV.ap[:, 0:hi - lo], AF.Silu, [CV], [CV])
                wtu = load_w(wu, fc)
                for (c0, c1) in pieces(lo, hi):
                    ps = proj_fm(wtu, H3, ht(c0, c1), c0, c1)
                    TT(P, "dve", A[fc].ap[:, c0 - lo:c1 - lo], ps.ap[:, 0:c1 - c0], CV.ap[:, c0 - lo:c1 - lo], ALU.mult, [ps, CV], [A[fc]])
            A3 = AP(A[0].ap.tensor, A[0].ap.offset, [list(A[0].ap.ap[0]), [1024, FC], [1, 1024]])
            gk = 4 + l
            for (c0, c1) in pieces(lo, hi):
                for mc in range(KC):
                    wt = load_w(wd, mc, ncol=DFF, buf=WD[mc % 2])
                    ps = nb_()
                    w3 = wt.ap.rearrange("p (k m) -> p k m", k=FC)
                    for kc in range(FC):
                        MM(P, ps, ps.ap[:, 0:c1 - c0], w3[:, kc, :], A3[:, kc, c0 - lo:c1 - lo], kc == 0, kc == FC - 1, [wt, A[kc]])
                    xs = X3[:, mc, c0:c1]
                    STT(P, "dve", xs, ps.ap[:, 0:c1 - c0], der(gk, j, mc), xs, ALU.mult, ALU.add, [ps, DER] + xt(mc, c0, c1), xt(mc, c0, c1))
                more = any((s2[3] == seq_lo and s2[0] == hi) for s2 in seqs_super)
                hs = 1 if (nextmod is not None and c1 == hi and more) else 0
                layer_norm_piece(c0, c1, j, l, 2, lnT, nextmod, hskip=hs)
                if l == 1:
                    for mc in range(KC):
                        P.dma("sp" if mc % 2 == 0 else "pool", yT[b, mc][:, c0 - CT:c1 - CT], X3[:, mc, c0:c1], reads=xt(mc, c0, c1))
                if hs:
                    patch.append((hi - 1, j))
        if patch:
            patch_h(patch, l + 1)

    def patch_h(patch, lnext):
        for (col, j) in patch:
            for mc in range(KC):
                ACT(P, H3[:, mc, col:col + 1], X3[:, mc, col:col + 1], AF.Identity, xt(mc, col, col + 1) + [DER, MODS], ht1(mc, col, col + 1),
                    bias=mods(lnext, j, mc), scale=der(0 + lnext, j, mc))

    def even_mixer(b):
        specs = [(c * 4608, NT, BF16) for c in range(KC)]
        specs += [(36864 + i * 9216, NT, F32) for i in range(3)]
        specs += [(64512, NT, BF16)]
        specs += [(69120 + i * 2048, 512, F32) for i in range(4)]
        ts = uphase(specs)
        YC = ts[0:KC]
        setA = (ts[KC], ts[KC + 1], ts[KC + 2], ts[KC + 3], ts[KC + 4:KC + 7], sttA)
        TM = ts[KC + 4:KC + 8]
        def xcarve(off, n, dt):
            if dt == F32:
                return Xs[:, off // 4: off // 4 + n]
            return Xs[:, off // 4: off // 4 + (n + 1) // 2].bitcast(BF16)[:, :n]
        xs_specs = [(i * 9216, NT, F32) for i in range(3)] + [(27648, NT, BF16)] + [(32256 + i * 2048, 512, F32) for i in range(2)]
        xs_specs += [(36352, NT, F32), (45568, NT, F32)]
        xsT = [T(xcarve(o, n, dt)) for (o, n, dt) in xs_specs]
        borrowed = [t for c in range(6) for t in XR[c]]
        P.realias(xsT, borrowed)
        setB = (xsT[0], xsT[1], xsT[2], xsT[3], xsT[4:6], sttB)
        S0, S1 = xsT[6], xsT[7]
        seqs = seqs_of(b, True)
        allp = [pc for (lo, hi, j) in seqs for pc in pieces(lo, hi)]

        def rec_chunk(c, bufs, wbuf):
            R0, R1, R2, XCb, TMs, st = bufs
            tm_i = [0]

            def ntmp():
                t = TMs[tm_i[0] % len(TMs)]
                tm_i[0] += 1
                return t
            wt = load_w("win", c, buf=wbuf)
            for (c0, c1) in allp:
                ps = proj_fm(wt, H3, ht(c0, c1), c0, c1)
                CP(P, "act", R0.ap[:, c0:c1], ps.ap[:, 0:c1 - c0], [ps], [R0])
                yield
            for (lo, hi, j) in seqs:
                conv_seq(R1.ap, R0.ap, R1, R0, lo, hi, 0, 0, "rcw", c * 4, 4, 2, scol("rcb", c), lo, hi)
            yield
            CP(P, "pool", XCb.ap, R1.ap, [R1], [XCb])
            yield
            for d in range(2):
                Bd = R2 if d == 0 else R1
                for (c0, c1) in allp:
                    n = c1 - c0
                    psr = nb_()
                    MM(P, psr, psr.ap[:, 0:n], GW.ap[:, ((d * 2 + 0) * 4 + c) * 128:((d * 2 + 0) * 4 + c + 1) * 128], XCb.ap[:, c0:c1], True, True, [GW, XCb])
                    psi = nb_()
                    MM(P, psi, psi.ap[:, 0:n], GW.ap[:, ((d * 2 + 1) * 4 + c) * 128:((d * 2 + 1) * 4 + c + 1) * 128], XCb.ap[:, c0:c1], True, True, [GW, XCb])
                    ACT(P, R0.ap[:, c0:c1], psr.ap[:, 0:n], AF.Sigmoid, [psr, smallT], [R0], bias=scol("rba", d * 4 + c))
                    t1 = ntmp()
                    ACT(P, t1.ap[:, 0:n], psi.ap[:, 0:n], AF.Sigmoid, [psi, smallT], [t1], bias=scol("rbx", d * 4 + c))
                    if d == 0:
                        TT(P, "dve", Bd.ap[:, c0:c1], t1.ap[:, 0:n], R1.ap[:, c0:c1], ALU.mult, [t1, R1], [Bd])
                    else:
                        TT(P, "dve", Bd.ap[:, c0:c1], R1.ap[:, c0:c1], t1.ap[:, 0:n], ALU.mult, [t1, R1], [Bd])
                    yield
                ACT(P, R0.ap, R0.ap, AF.Exp, [R0, CLAM], [R0], scale=CLAM.ap[:, d * 4 + c:d * 4 + c + 1])
                yield
                for (c0, c1) in allp:
                    n = c1 - c0
                    a_ = R0.ap[:, c0:c1]
                    t2 = ntmp()
                    TT(P, "pool", t2.ap[:, 0:n], a_, a_, ALU.mult, [R0], [t2])
                    ACT(P, t2.ap[:, 0:n], t2.ap[:, 0:n], AF.Sqrt, [t2, cvec], [t2], bias=ONE, scale=-1.0)
                    TT(P, "dve", Bd.ap[:, c0:c1], Bd.ap[:, c0:c1], t2.ap[:, 0:n], ALU.mult, [Bd, t2], [Bd])
                    yield
                if d == 0:
                    SCAN(P, Bd.ap[:, 0:CT], R0.ap[:, 0:CT], Bd.ap[:, 0:CT], 0.0, [R0, Bd], [Bd])
                    CP(P, "act", st.ap[:, 0:1], Bd.ap[:, CT - 1:CT], [Bd], [st])
                    yield
                    SCAN(P, Bd.ap[:, CT:NT], R0.ap[:, CT:NT], Bd.ap[:, CT:NT], st.ap[:, 0:1], [R0, Bd, st], [Bd])
                else:
                    SCAN(P, rev(Bd.ap[:, 0:CT]), rev(R0.ap[:, 0:CT]), rev(Bd.ap[:, 0:CT]), 0.0, [R0, Bd], [Bd])
                    CP(P, "act", st.ap[:, 1:2], Bd.ap[:, 0:1], [Bd], [st])
                    yield
                    SCAN(P, rev(Bd.ap[:, CT:NT]), rev(R0.ap[:, CT:NT]), rev(Bd.ap[:, CT:NT]), st.ap[:, 1:2], [R0, Bd, st], [Bd])
                yield
            TT(P, "pool", R2.ap, R2.ap, R1.ap, ALU.add, [R2, R1], [R2])
            yield
            wt = load_w("win", 4 + c, buf=wbuf)
            for (c0, c1) in allp:
                n = c1 - c0
                ps = proj_fm(wt, H3, ht(c0, c1), c0, c1)
                t1 = ntmp()
                ACT(P, t1.ap[:, 0:n], ps.ap[:, 0:n], AF.Gelu_apprx_tanh, [ps], [t1])
                TT(P, "dve", YC[c].ap[:, c0:c1], t1.ap[:, 0:n], R2.ap[:, c0:c1], ALU.mult, [t1, R2], [YC[c]])
                yield

        def sc_chunk(c, wbuf):
            wt = load_w("win", 12 + c, buf=wbuf)
            for (c0, c1) in allp:
                ps = proj_fm(wt, H3, ht(c0, c1), c0, c1)
                CP(P, "act", S0.ap[:, c0:c1], ps.ap[:, 0:c1 - c0], [ps], [S0])
                yield
            wt = load_w("win", 16 + c, buf=wbuf)
            for (c0, c1) in allp:
                ps = proj_fm(wt, H3, ht(c0, c1), c0, c1)
                TT(P, "dve", S0.ap[:, c0:c1], ps.ap[:, 0:c1 - c0], S0.ap[:, c0:c1], ALU.mult, [ps, S0], [S0])
                yield
            for (lo, hi, j) in seqs:
                conv_seq(S1.ap, S0.ap, S1, S0, lo, hi, 0, 0, "scw", c * 3, 3, 1, scol("scb", c), lo, hi)
            yield
            wt = load_w("win", 8 + c, buf=wbuf)
            for (c0, c1) in allp:
                ps = proj_fm(wt, H3, ht(c0, c1), c0, c1)
                TT(P, "dve", YC[4 + c].ap[:, c0:c1], ps.ap[:, 0:c1 - c0], S1.ap[:, c0:c1], ALU.mult, [ps, S1], [YC[4 + c]])
                yield

        def chain(gs):
            for g in gs:
                yield from g
        gens = [chain([rec_chunk(0, setA, WA[0]), rec_chunk(2, setA, WA[0])]),
                chain([rec_chunk(1, setB, WA[1]), rec_chunk(3, setB, WA[1])]),
                chain([sc_chunk(c, WA[2]) for c in range(4)])]
        while gens:
            for g in list(gens):
                try:
                    next(g)
                except StopIteration:
                    gens.remove(g)
        P.realias(borrowed, xsT)
        for r, (lo, hi) in enumerate(REG):
            for c in range(6):
                q = "sp" if c % 2 == 0 else "pool"
                if r == 0:
                    P.dma(q, X3[:, c, 0:CT], ctxT[b, c], writes=[XR[c][0]])
                else:
                    P.dma(q, X3[:, c, lo:hi], xT[b, c][:, lo - CT:hi - CT], writes=[XR[c][r]])
        YC3 = AP(YC[0].ap.tensor, YC[0].ap.offset, [list(YC[0].ap.ap[0]), [NT, KC], [1, NT]])
        lnT = ([TM[0], TM[1]], TM[2], TM[3])
        proj_resid_ln(b, 0, 1, "wout0", KC, YC3, YC, seqs, 0, lnT, (6, 8))

    def odd_mixer(b):
        specs = [(c * 4096, SEQ, BF16) for c in range(KC)]
        specs += [(32768, SEQ, BF16), (36864, NT, BF16), (41472, 18 * 192, BF16)]
        specs += [(48384 + i * 3584, 896, F32) for i in range(4)]
        specs += [(62720 + i * 2048, 512, F32) for i in range(3)]
        specs += [(68864 + i * 1024, 512, BF16) for i in range(8)]
        specs += [(77056 + i * 1024, 256, F32) for i in range(3)]
        ts = uphase(specs)
        OT = ts[0:KC]
        QT, KT, V = ts[KC:KC + 3]
        TB = ts[KC + 3:KC + 7]
        E = ts[KC + 7:KC + 10]
        PT = ts[KC + 10:KC + 18]
        RC = ts[KC + 18:KC + 20]
        OS = ts[KC + 20]
        V3 = V.ap.rearrange("p (t m) -> p t m", t=18)
        P.op("pool", lambda h: h.memset(V3[:, :, 64:128], 1.0), writes=[V])
        cnt = dict(e=0, p=0, r=0)
        j = b
        for hp in range(KC):
            wq = load_w("wqkv", hp)
            for (c0, c1) in pieces(CT, NT):
                ps = proj_fm(wq, H3, ht(c0, c1), c0, c1)
                CP(P, "act", QT.ap[:, c0 - CT:c1 - CT], ps.ap[:, 0:c1 - c0], [ps], [QT])
            wk = load_w("wqkv", 8 + hp)
            for (c0, c1) in pieces(0, NT):
                ps = proj_fm(wk, H3, ht(c0, c1), c0, c1)
                CP(P, "dve", KT.ap[:, c0:c1], ps.ap[:, 0:c1 - c0], [ps], [KT])
            wv = load_w("wqkv", 16 + hp)
            wv3 = wv.ap.rearrange("p (k m) -> p k m", k=KC)
            for t0 in range(0, 18, 4):
                nt_ = min(4, 18 - t0)
                ps = nb_()
                for ti in range(nt_):
                    tt = t0 + ti
                    for kc in range(KC):
                        MM(P, ps, ps.ap[:, ti * 128:(ti + 1) * 128], H3[:, kc, tt * 128:(tt + 1) * 128], wv3[:, kc, :],
                           kc == 0, kc == KC - 1, [wv] + ht(tt * 128, (tt + 1) * 128))
                ps3 = ps.ap[:, 0:nt_ * 128].rearrange("p (t m) -> p t m", t=nt_)
                CP(P, "act", V3[:, t0:t0 + nt_, 0:64], ps3[:, :, 0:64], [ps], [V])
                CP(P, "dve", V3[:, t0:t0 + nt_, 128:192], ps3[:, :, 64:128], [ps], [V])
            for par in range(2):
                h_ = 2 * hp + par
                P.dma("sp", TB[par * 2].ap, tmid[h_ * 128:(h_ + 1) * 128, :], writes=[TB[par * 2]])
                P.dma("sp", TB[par * 2 + 1].ap, tfull[h_ * 128:(h_ + 1) * 128, :], writes=[TB[par * 2 + 1]])
                ACT(P, TB[par * 2].ap, TB[par * 2].ap, AF.Exp, [TB[par * 2]], [TB[par * 2]])
                ACT(P, TB[par * 2 + 1].ap, TB[par * 2 + 1].ap, AF.Exp, [TB[par * 2 + 1]], [TB[par * 2 + 1]])
            items = [(par, r0) for par in range(2) for r0 in range(0, 32, 4)]

            def s_phase(par, r0):
                pb = par * 64
                q0 = r0 * 64
                if r0 == 0:
                    chunks, kind = [0, 1, 2, 3], 1
                elif r0 == 28:
                    chunks, kind = [12, 13, 14, 15], 1
                else:
                    a0 = (r0 - 4) // 2
                    chunks, kind = list(range(a0, a0 + 6)), 0
                tb = TB[par * 2 + kind]
                pts = []
                for i in range(0, len(chunks), 2):
                    ps = nb_()
                    for k in range(2):
                        a = chunks[i + k]
                        MM(P, ps, ps.ap[:, k * 256:(k + 1) * 256], KT.ap[pb:pb + 64, CT + a * 128:CT + (a + 1) * 128],
                           QT.ap[pb:pb + 64, q0:q0 + 256], True, True, [KT, QT])
                    e = E[cnt["e"] % 3]
                    cnt["e"] += 1
                    ACT(P, e.ap, ps.ap, AF.Exp, [ps], [e], scale=0.125)
                    pt = PT[cnt["p"] % 8]
                    cnt["p"] += 1
                    ei0 = r0 - 2 * chunks[i] + 6
                    tap = AP(tb.ap.tensor, tb.ap[:, ei0 * 64:ei0 * 64 + 1].offset, [list(tb.ap.ap[0]), [-128, 2], [1, 256]])
                    TT(P, "pool", pt.ap.rearrange("p (a b) -> p a b", a=2), e.ap.rearrange("p (a b) -> p a b", a=2), tap, ALU.mult, [e, tb], [pt])
                    pts.append((pt, chunks[i], chunks[i + 1]))
                ps = nb_()
                for k in range(2):
                    MM(P, ps, ps.ap[:, k * 256:(k + 1) * 256], KT.ap[pb:pb + 64, k * 128:(k + 1) * 128],
                       QT.ap[pb:pb + 64, q0:q0 + 256], True, True, [KT, QT])
                ptc = PT[cnt["p"] % 8]
                cnt["p"] += 1
                ACT(P, ptc.ap, ps.ap, AF.Exp, [ps], [ptc], scale=0.125)
                return (par, r0, pts, ptc)

            def pv_phase(st):
                par, r0, pts, ptc = st
                pb = par * 64
                q0 = r0 * 64
                pso = nb_()
                ob = 64 - pb
                mms = []
                for (pt, a, a2) in pts:
                    mms.append((V3[:, 2 + a, pb:pb + 128], pt.ap[:, 0:256], pt))
                    mms.append((V3[:, 2 + a2, pb:pb + 128], pt.ap[:, 256:512], pt))
                mms.append((V3[:, 0, pb:pb + 128], ptc.ap[:, 0:256], ptc))
                mms.append((V3[:, 1, pb:pb + 128], ptc.ap[:, 256:512], ptc))
                nm_ = len(mms)
                for i, (l_, r_, pt) in enumerate(mms):
                    MM(P, pso, pso.ap[:, 0:256], l_, r_, i == 0, i == nm_ - 1, [V, pt])
                rc = RC[cnt["r"] % 2]
                cnt["r"] += 1
                P.op("dve", lambda h: h.reciprocal(rc.ap[pb:pb + 64, :], pso.ap[ob:ob + 64, 0:256]), reads=[pso], writes=[rc])
                CP(P, "act", OS.ap[pb:pb + 64, :], pso.ap[pb:pb + 64, 0:256], [pso], [OS])
                TT(P, "pool", OT[hp].ap[pb:pb + 64, q0:q0 + 256], OS.ap[pb:pb + 64, :], rc.ap[pb:pb + 64, :], ALU.mult, [OS, rc], [OT[hp]])

            prev = None
            for (par, r0) in items:
                st = s_phase(par, r0)
                if prev is not None:
                    pv_phase(prev)
                prev = st
            pv_phase(prev)
        OT3 = AP(OT[0].ap.tensor, OT[0].ap.offset, [list(OT[0].ap.ap[0]), [SEQ, KC], [1, SEQ]])
        lnT = ([E[0], E[1]], E[2], T(ucarve(68864, 512, F32)))
        P.realias([lnT[2]], PT)
        u_live.append(lnT[2])
        proj_resid_ln(b, 1, 1, "wout1", KC, OT3, OT, seqs_of(b, False), CT, lnT, (6 + 1, 8 + 1))

    for b in range(NB):
        for c in range(KC):
            q = "sp" if c % 2 == 0 else "pool"
            P.dma(q, X3[:, c, 0:CT], ctxT[b, c], writes=[XR[c][0]])
            P.dma(q, X3[:, c, CT:CT + 1024], xT[b, c][:, 0:1024], writes=[XR[c][1]])
            P.dma(q, X3[:, c, CT + 1024:NT], xT[b, c][:, 1024:2048], writes=[XR[c][2]])
        for c in range(KC):
            for r, (lo, hi) in enumerate(REG):
                j = NB if r == 0 else b
                ACT(P, H3[:, c, lo:hi], X3[:, c, lo:hi], AF.Identity, [XR[c][r], DER, MODS], [HR[c][r]], bias=mods(0, j, c), scale=der(0, j, c))
        if stop != "load":
            even_mixer(b)
        nm0 = (10, 11) if layers == 2 else None
        if stop is None:
            ffn_stage(b, 0, [(0, CT, NB, 0, CT), (CT, CT + 1024, b, CT, NT), (CT + 1024, NT, b, CT, NT)], nm0)

        if layers == 2:
            odd_mixer(b)
            ffn_stage(b, 1, [(CT, CT + 1024, b, CT, NT), (CT + 1024, NT, b, CT, NT)], None)
        if not (layers == 2 and stop is None):
            for c in range(KC):
                q = "sp" if c % 2 == 0 else "pool"
                P.dma(q, yT[b, c], X3[:, c, CT:NT], reads=[XR[c][1], XR[c][2]])
    allx = [t for c in range(KC) for t in XR[c]]
    P.wait_all("sp", allx)
    P.worder = wstate["seen"]
    P.finish()
    return nc, P


def _wt(W):
    K, M = W.shape
    return np.ascontiguousarray(W.reshape(K // 128, 128, M // 128, 128).transpose(2, 1, 0, 3)).reshape(M, K)


def _pc(v, nchunk):
    v = np.asarray(v, np.float32)
    lead = v.shape[:-1]
    return np.moveaxis(v.reshape(lead + (nchunk, 128)), -1, 0)


def _tables(rpb):
    kr2 = np.arange(2)[:, None, None, None]
    kcol = np.arange(64)[None, :, None, None]
    e = (np.arange(14) - 6)[None, None, :, None]
    qcol = np.arange(64)[None, None, None, :]
    dr = kr2 - e + 7 + 0 * kcol + 0 * qcol
    dc = kcol - qcol + 15 + 0 * kr2 + 0 * e
    cs = np.clip(qcol - 8, 0, 48)
    colok = (kcol >= cs) & (kcol < cs + 16)
    drok = (dr >= 0) & (dr <= 14)
    rowmid = (kr2 - e >= -4) & (kr2 - e <= 3)
    g = rpb[:, np.clip(dr, 0, 14), np.clip(dc, 0, 30)]
    negs = np.full(g.shape, NEG, np.float32)
    tfull = np.where((colok & drok)[None], g, negs).reshape(16 * 128, 896)
    tmid = np.where((colok & drok & rowmid)[None], g, negs).reshape(16 * 128, 896)
    return np.ascontiguousarray(tmid, np.float32), np.ascontiguousarray(tfull, np.float32)


def prep_shared(mod_w, mod_b, ln1_g, ln1_b, ln2_g, ln2_b, ev_w_in, ev_w_out, rec_conv_w, rec_conv_b, rec_wa, rec_ba,
                rec_wx, rec_bx, rec_lam, sc_conv_w, sc_conv_b, na_w_qkv, na_w_out, na_rpb, ffn_w_gate, ffn_w_up,
                ffn_conv_w, ffn_conv_b, ffn_w_down):
    sh = {}
    sh["win"] = _wt(ev_w_in[0])
    sh["wout0"] = _wt(ev_w_out[0])
    sh["wqkv"] = _wt(na_w_qkv[0])
    sh["wout1"] = _wt(na_w_out[0])
    for l in range(2):
        sh["wg%d" % l] = _wt(ffn_w_gate[l])
        sh["wu%d" % l] = _wt(ffn_w_up[l])
        sh["wd%d" % l] = _wt(ffn_w_down[l])
    sh["modw"] = np.concatenate([_wt(mod_w[l]) for l in range(2)], axis=0)
    small = np.zeros((128, NS), np.float32)

    def put(nm, arr):
        arr = np.asarray(arr, np.float32).reshape(128, -1)
        small[:, SOFF[nm]:SOFF[nm] + arr.shape[1]] = arr
    put("modb", _pc(mod_b, 48))
    put("ln1g", _pc(ln1_g, 8)); put("ln1b", _pc(ln1_b, 8)); put("ln2g", _pc(ln2_g, 8)); put("ln2b", _pc(ln2_b, 8))
    put("rcw", np.transpose(_pc(rec_conv_w[0], 4), (0, 2, 1)))
    put("rcb", _pc(rec_conv_b[0], 4))
    put("rba", _pc(rec_ba[0], 4)); put("rbx", _pc(rec_bx[0], 4)); put("rlam", _pc(rec_lam[0], 4))
    put("scw", np.transpose(_pc(sc_conv_w[0], 4), (0, 2, 1)))
    put("scb", _pc(sc_conv_b[0], 4))
    put("fcw", np.transpose(_pc(ffn_conv_w, FC), (0, 1, 3, 2)))
    put("fcb", _pc(ffn_conv_b, FC))
    sh["smallp"] = small
    gw = np.zeros((128, 16, 128), np.float32)
    for d in range(2):
        for wi, W in enumerate((rec_wa[0], rec_wx[0])):
            for c in range(4):
                g = (d * 2 + wi) * 4 + c
                gw[0:64, g, 0:64] = W[d, 2 * c]
                gw[64:128, g, 64:128] = W[d, 2 * c + 1]
    sh["gw"] = gw.reshape(128, 16 * 128)
    sh["tmid"], sh["tfull"] = _tables(np.asarray(na_rpb[0], np.float32))
    return sh


def prep_core(x, c, ctx, c_ctx, b0, NB):
    m = {}
    m["xT"] = np.ascontiguousarray(x[b0:b0 + NB].transpose(0, 2, 1)).reshape(NB, KC, 128, SEQ)
    m["ctxT"] = np.ascontiguousarray(ctx[b0:b0 + NB].transpose(0, 2, 1)).reshape(NB, KC, 128, CT)
    cc = np.concatenate([c[b0:b0 + NB], c_ctx[None, :]], axis=0)
    m["cT"] = np.ascontiguousarray(np.transpose(cc.reshape(NB + 1, KC, 128), (2, 1, 0))).reshape(128, KC * (NB + 1))
    return m


_CACHE = {}


def kernel(x, c, ctx, c_ctx, mod_w, mod_b, ln1_g, ln1_b, ln2_g, ln2_b, ev_w_in, ev_w_out, rec_conv_w, rec_conv_b,
           rec_wa, rec_ba, rec_wx, rec_bx, rec_lam, sc_conv_w, sc_conv_b, na_w_qkv, na_w_out, na_rpb, ffn_w_gate,
           ffn_w_up, ffn_conv_w, ffn_conv_b, ffn_w_down):
    f = lambda a: np.asarray(a, np.float32)
    x, c, ctx, c_ctx = f(x), f(c), f(ctx), f(c_ctx)
    B = x.shape[0]
    NB = B // NCORES
    sh = prep_shared(f(mod_w), f(mod_b), f(ln1_g), f(ln1_b), f(ln2_g), f(ln2_b), f(ev_w_in), f(ev_w_out), f(rec_conv_w),
                     f(rec_conv_b), f(rec_wa), f(rec_ba), f(rec_wx), f(rec_bx), f(rec_lam), f(sc_conv_w), f(sc_conv_b),
                     f(na_w_qkv), f(na_w_out), f(na_rpb), f(ffn_w_gate), f(ffn_w_up), f(ffn_conv_w), f(ffn_conv_b), f(ffn_w_down))
    if NB not in _CACHE:
        _, p0 = build_program(1)
        _CACHE[NB] = build_program(NB, order=p0.worder)[0]
    nc = _CACHE[NB]
    in_maps = []
    for i in range(NCORES):
        m = dict(sh)
        m.update(prep_core(x, c, ctx, c_ctx, i * NB, NB))
        in_maps.append(m)
    res = run_bass_kernel_spmd(nc, in_maps, core_ids=list(range(NCORES)))
    out = np.empty((B, SEQ, D), np.float32)
    for i in range(NCORES):
        y = np.asarray(res.results[i]["yT"]).reshape(NB, D, SEQ)
        out[i * NB:(i + 1) * NB] = y.transpose(0, 2, 1)
    return out
```

```python
import numpy as np
import concourse.bass as bass
import concourse.mybir as mybir
from concourse.bass_utils import run_bass_kernel_spmd
from concourse.ap import AP

F32 = mybir.dt.float32
BF16 = mybir.dt.bfloat16
ALU = mybir.AluOpType
AF = mybir.ActivationFunctionType

NCORES = 8
D = 1024
KC = 8
SEQ = 2048
CT = 256
NT = SEQ + CT
DFF = 2816
FC = 22
ALPHA = 4.0 ** 0.25
EPS2 = 1e-6 / (ALPHA * ALPHA)
NEG = -30000.0


class T:
    __slots__ = ("ap", "w", "r")

    def __init__(self, ap):
        self.ap = ap
        self.w = None
        self.r = {}


class Eng:
    def __init__(self, name, h, sem):
        self.name = name
        self.h = h
        self.sem = sem
        self.count = 0
        self.waited = {}
        self.q = []


class Prog:
    def __init__(self, nc, n_dma_sems=8):
        self.nc = nc
        self._ctx = []
        self.engs = {}
        for nm, h in (("pe", nc.tensor), ("act", nc.scalar), ("dve", nc.vector), ("pool", nc.gpsimd), ("sp", nc.sync)):
            self.engs[nm] = Eng(nm, h, self._sem("e_" + nm))
        self.n_dma_sems = n_dma_sems
        self.dma_pool = {}
        for nm in ("sp", "pool"):
            self.dma_pool[nm] = dict(sems=[self._sem("d_%s%d" % (nm, i)) for i in range(n_dma_sems)], j=0)
        self.ninst = 0
        self.force_own = False

    def _sem(self, name):
        cm = self.nc.semaphore(name)
        s = cm.__enter__()
        self._ctx.append(cm)
        return (name, s)

    def sbuf(self, name, shape, dt):
        cm = self.nc.sbuf_tensor(name, shape, dt)
        t = cm.__enter__()
        self._ctx.append(cm)
        return t

    def psum(self, name, shape, dt):
        cm = self.nc.psum_tensor(name, shape, dt)
        t = cm.__enter__()
        self._ctx.append(cm)
        return t

    @staticmethod
    def _deps(reads, writes):
        deps = {}

        def add(k, v):
            if deps.get(k, 0) < v:
                deps[k] = v
        for t in reads:
            if t.w is not None:
                add(*t.w)
        for t in writes:
            if t.w is not None:
                add(*t.w)
            for k, v in t.r.items():
                add(k, v)
        return deps

    def _waits(self, e, deps, own_too):
        for (name, sh), v in deps.items():
            if (not own_too) and name == e.sem[0]:
                continue
            if e.waited.get(name, 0) >= v:
                continue
            e.waited[name] = v
            e.q.append(lambda h, sh=sh, v=v: h.wait_ge(sh, v))
            self.ninst += 1

    def op(self, eng, fn, reads=(), writes=(), sig=True, own=False):
        e = self.engs[eng]
        self._waits(e, self._deps(reads, writes), own or self.force_own)
        if sig:
            e.count += 1
            val = e.count
            sh = e.sem[1]
            e.q.append(lambda h, fn=fn, sh=sh: fn(h).then_inc(sh, 1))
        else:
            val = e.count + 1
            e.q.append(lambda h, fn=fn: fn(h))
        self.ninst += 1
        for t in reads:
            if t.r.get(e.sem, 0) < val:
                t.r[e.sem] = val
        for t in writes:
            t.w = (e.sem, val)
            t.r = {}

    def dma(self, q, out_ap, in_ap, reads=(), writes=()):
        e = self.engs[q]
        pool = self.dma_pool[q]
        j = pool["j"]
        pool["j"] += 1
        s = pool["sems"][j % self.n_dma_sems]
        gen = j // self.n_dma_sems
        deps = self._deps(reads, writes)
        if gen > 0 and deps.get(s, 0) < 16 * gen:
            deps[s] = 16 * gen
        self._waits(e, deps, True)
        val = 16 * (gen + 1)
        sh = s[1]
        e.q.append(lambda h, o=out_ap, i=in_ap, sh=sh: h.dma_start(out=o, in_=i).then_inc(sh, 16))
        self.ninst += 1
        for t in reads:
            if t.r.get(s, 0) < val:
                t.r[s] = val
        for t in writes:
            t.w = (s, val)
            t.r = {}

    def realias(self, new_ts, old_ts):
        merged = {}
        for t in old_ts:
            if t.w is not None and merged.get(t.w[0], 0) < t.w[1]:
                merged[t.w[0]] = t.w[1]
            for k, v in t.r.items():
                if merged.get(k, 0) < v:
                    merged[k] = v
        for t in new_ts:
            t.w = None
            t.r = dict(merged)

    def wait_all(self, eng, ts):
        e = self.engs[eng]
        self._waits(e, self._deps(ts, ts), True)

    def finish(self):
        nc = self.nc
        with nc.Block() as block:
            def mk(e):
                def f(h):
                    for c in e.q:
                        c(h)
                return f
            block.tensor(mk(self.engs["pe"]))
            block.scalar(mk(self.engs["act"]))
            block.vector(mk(self.engs["dve"]))
            block.gpsimd(mk(self.engs["pool"]))
            block.sync(mk(self.engs["sp"]))
        for cm in reversed(self._ctx):
            cm.__exit__(None, None, None)
        self._ctx = []


def MM(P, psT, out, lhsT, rhs, start, stop, reads):
    P.op("pe", lambda h: h.matmul(out, lhsT, rhs, start=start, stop=stop), reads=reads, writes=[psT], sig=True)


def ACT(P, out, in_, func, reads, writes, bias=None, scale=None):
    kw = {}
    if bias is not None:
        kw["bias"] = bias
    if scale is not None:
        kw["scale"] = scale
    P.op("act", lambda h: h.activation(out, in_, func, **kw), reads=reads, writes=writes)


def TT(P, eng, out, in0, in1, op, reads, writes):
    P.op(eng, lambda h: h.tensor_tensor(out, in0, in1, op), reads=reads, writes=writes)


def TS(P, eng, out, in0, s1, s2, op0, op1, reads, writes):
    if s2 is None:
        P.op(eng, lambda h: h.tensor_scalar(out, in0, s1, None, op0), reads=reads, writes=writes)
    else:
        P.op(eng, lambda h: h.tensor_scalar(out, in0, s1, s2, op0, op1), reads=reads, writes=writes)


def STT(P, eng, out, in0, sc, in1, op0, op1, reads, writes):
    P.op(eng, lambda h: h.scalar_tensor_tensor(out, in0, sc, in1, op0, op1), reads=reads, writes=writes)


def CP(P, eng, out, in_, reads, writes):
    if eng == "act":
        P.op("act", lambda h: h.copy(out, in_), reads=reads, writes=writes)
    else:
        P.op(eng, lambda h: h.tensor_copy(out, in_), reads=reads, writes=writes)


def SCAN(P, out, d0, d1, init, reads, writes):
    P.op("dve", lambda h: h.tensor_tensor_scan(out, d0, d1, init, ALU.mult, ALU.add), reads=reads, writes=writes)


def rev(ap2d):
    n = ap2d.shape[1]
    last = ap2d[:, n - 1:n]
    return AP(last.tensor, last.offset, [list(last.ap[0]), [-1, n]])


def pieces(lo, hi, step=512):
    n = -(-(hi - lo) // step)
    size = -(-(hi - lo) // n)
    size += size % 2
    out = []
    c = lo
    while c < hi:
        out.append((c, min(hi, c + size)))
        c += size
    return out


def small_layout():
    off = {}
    o = 0
    for nm, n in (("modb", 96), ("ln1g", 16), ("ln1b", 16), ("ln2g", 16), ("ln2b", 16), ("rcw", 16), ("rcb", 4),
                  ("rba", 8), ("rbx", 8), ("rlam", 8), ("scw", 12), ("scb", 4), ("fcw", 132), ("fcb", 44)):
        off[nm] = o
        o += n
    return off, o


SOFF, NS = small_layout()


def build_program(NB, layers=2, stop=None, order=None):
    nc = bass.Bass("TRN2", target_bir_lowering=False)
    NJ = NB + 1

    def din(name, shape, dt=F32):
        return nc.dram_tensor(name, list(shape), dt, kind="ExternalInput").ap()

    xT = din("xT", [NB, KC, 128, SEQ])
    ctxT = din("ctxT", [NB, KC, 128, CT])
    cT = din("cT", [128, KC * NJ])
    smallp = din("smallp", [128, NS])
    modw = din("modw", [2 * 48 * 128, 1024])
    gw_in = din("gw", [128, 16 * 128])
    tmid = din("tmid", [16 * 128, 896])
    tfull = din("tfull", [16 * 128, 896])
    wspec = [("win", 20, 1024), ("wout0", 8, 1024), ("wg0", FC, 1024), ("wu0", FC, 1024), ("wd0", 8, DFF),
             ("wqkv", 24, 1024), ("wout1", 8, 1024), ("wg1", FC, 1024), ("wu1", FC, 1024), ("wd1", 8, DFF)]
    wsrc = {}
    wscr = {}
    wscrT = {}
    for nm, nmc, ncol in wspec:
        wsrc[nm] = din(nm, [nmc * 128, ncol])
        wscr[nm] = nc.dram_tensor(nm + "_s", [nmc * 128, ncol], BF16, kind="Internal").ap()
        wscrT[nm] = [T(wscr[nm][m * 128:(m + 1) * 128, :]) for m in range(nmc)]
    yT = nc.dram_tensor("yT", [NB, KC, 128, SEQ], F32, kind="ExternalOutput").ap()

    P = Prog(nc)
    Xs = P.sbuf("X", [128, KC * NT], F32)
    Hs = P.sbuf("H", [128, KC * NT], BF16)
    UBYTES = 81920
    Us = P.sbuf("U", [128, UBYTES // 4], F32)
    X3 = Xs[:].rearrange("p (c t) -> p c t", c=KC)
    H3 = Hs[:].rearrange("p (c t) -> p c t", c=KC)
    REG = [(0, CT), (CT, CT + 1024), (CT + 1024, NT)]
    XR = [[T(X3[:, c, lo:hi]) for (lo, hi) in REG] for c in range(KC)]
    HR = [[T(H3[:, c, lo:hi]) for (lo, hi) in REG] for c in range(KC)]

    def xt(mc, c0, c1):
        return [XR[mc][r] for r, (lo, hi) in enumerate(REG) if c0 < hi and c1 > lo]

    def ht1(mc, c0, c1):
        return [HR[mc][r] for r, (lo, hi) in enumerate(REG) if c0 < hi and c1 > lo]

    def ht(c0, c1):
        return [t for mc in range(KC) for t in ht1(mc, c0, c1)]
    WA = [T(P.sbuf("wa%d" % i, [128, 1024], BF16)[:]) for i in range(3)]
    wa_i = [0]
    smallT = T(P.sbuf("smallp_sb", [128, NS], F32)[:])
    cvec = T(P.sbuf("cvec", [128, 4], F32)[:])
    sttA = T(P.sbuf("sttA", [128, 2], F32)[:])
    sttB = T(P.sbuf("sttB", [128, 2], F32)[:])
    onesD = T(P.sbuf("onesD", [128, 128], F32)[:])
    onesB = T(P.sbuf("onesB", [128, 128], BF16)[:])
    scT = T(P.sbuf("scT", [128, KC * NJ], F32)[:])
    MODS = T(P.sbuf("MODS", [128, 2 * NJ * 48], F32)[:])
    DER = T(P.sbuf("DER", [128, 12 * NJ * 8], F32)[:])
    CLAM = T(P.sbuf("CLAM", [128, 8], F32)[:])
    GW = T(P.sbuf("GW", [128, 16 * 128], BF16)[:])
    banks = [T(P.psum("ps%d" % i, [128, 512], F32)[:]) for i in range(8)]
    bank_i = [0]

    def nb_():
        t = banks[bank_i[0] % 8]
        bank_i[0] += 1
        return t

    def ucarve(off_bytes, n, dt):
        if dt == F32:
            return Us[:, off_bytes // 4: off_bytes // 4 + n]
        return Us[:, off_bytes // 4: off_bytes // 4 + (n + 1) // 2].bitcast(BF16)[:, :n]

    u_live = []

    def uphase(specs):
        new = [T(ucarve(o, n, dt)) for (o, n, dt) in specs]
        P.realias(new, u_live)
        u_live[:] = new
        return new

    sm = smallT.ap

    def scol(nm, i):
        o = SOFF[nm] + i
        return sm[:, o:o + 1]

    def mods(l, j, q):
        o = (l * NJ + j) * 48 + q
        return MODS.ap[:, o:o + 1]

    def der(k, j, c):
        o = (k * NJ + j) * 8 + c
        return DER.ap[:, o:o + 1]

    worder = list(order) if order is not None else []
    wpos = {k: i for i, k in enumerate(worder)}
    wstate = dict(done=0, seen=[])
    LOOK = 12

    def prepass_upto(k):
        while wstate["done"] < min(k, len(worder)):
            nm, m = worder[wstate["done"]]
            P.dma("pool", wscr[nm][m * 128:(m + 1) * 128, :], wsrc[nm][m * 128:(m + 1) * 128, :], writes=[wscrT[nm][m]])
            wstate["done"] += 1
    if order is None:
        for nm, nmc, ncol in wspec:
            if layers == 1 and nm in ("wqkv", "wout1", "wg1", "wu1", "wd1"):
                continue
            for m in range(nmc):
                P.dma("pool", wscr[nm][m * 128:(m + 1) * 128, :], wsrc[nm][m * 128:(m + 1) * 128, :], writes=[wscrT[nm][m]])
    P.dma("sp", smallT.ap, smallp, writes=[smallT])
    P.dma("sp", scT.ap, cT, writes=[scT])
    P.dma("pool", GW.ap, gw_in, writes=[GW])
    prepass_upto(LOOK)
    P.op("dve", lambda h: h.memset(cvec.ap[:, 0:1], 1.0), writes=[cvec])
    P.op("dve", lambda h: h.memset(cvec.ap[:, 1:2], EPS2), writes=[cvec])
    P.op("dve", lambda h: h.memset(cvec.ap[:, 2:3], 0.0), writes=[cvec])
    P.op("dve", lambda h: h.memset(onesD.ap, 1.0 / D), writes=[onesD])
    P.op("dve", lambda h: h.memset(onesB.ap, 1.0), writes=[onesB])
    ONE = cvec.ap[:, 0:1]
    EPSc = cvec.ap[:, 1:2]
    ACT(P, scT.ap, scT.ap, AF.Silu, [scT], [scT])
    ACT(P, CLAM.ap, sm[:, SOFF["rlam"]:SOFF["rlam"] + 8], AF.Exp, [smallT], [CLAM], scale=-1.0)
    TS(P, "dve", CLAM.ap, CLAM.ap, 1.0, None, ALU.add, None, [CLAM], [CLAM])
    ACT(P, CLAM.ap, CLAM.ap, AF.Ln, [CLAM], [CLAM])
    TS(P, "dve", CLAM.ap, CLAM.ap, -8.0, None, ALU.mult, None, [CLAM], [CLAM])
    mwb = uphase([(i * 2048, 1024, BF16) for i in range(6)])
    scTb = T(P.sbuf("scTb", [128, KC * NJ], BF16)[:])
    CP(P, "dve", scTb.ap, scT.ap, [scT], [scTb])
    sc3 = scTb.ap.rearrange("p (c j) -> p c j", c=KC)
    for l in range(layers):
        for q in range(48):
            wt = mwb[(l * 48 + q) % 6]
            r0 = (l * 48 + q) * 128
            P.dma("pool", wt.ap, modw[r0:r0 + 128, :], writes=[wt])
            ps = nb_()
            w3 = wt.ap.rearrange("p (k m) -> p k m", k=KC)
            for kc in range(KC):
                MM(P, ps, ps.ap[:, 0:NJ], w3[:, kc, :], sc3[:, kc, :], kc == 0, kc == KC - 1, [wt, scTb])
            o0 = l * NJ * 48 + q
            outap = AP(MODS.ap.tensor, MODS.ap[:, o0:o0 + 1].offset, [list(MODS.ap.ap[0]), [48, NJ]])
            TS(P, "dve", outap, ps.ap[:, 0:NJ], scol("modb", l * 48 + q), None, ALU.add, None, [ps, smallT], [MODS])
    for l in range(layers):
        for j in range(NJ):
            def M8(which):
                o = (l * NJ + j) * 48 + which * 8
                return MODS.ap[:, o:o + 8]

            def D8(k):
                o = (k * NJ + j) * 8
                return DER.ap[:, o:o + 8]
            ln1g = sm[:, SOFF["ln1g"] + l * 8: SOFF["ln1g"] + l * 8 + 8]
            ln1b = sm[:, SOFF["ln1b"] + l * 8: SOFF["ln1b"] + l * 8 + 8]
            TS(P, "dve", D8(0 + l), M8(1), 1.0, None, ALU.add, None, [MODS], [DER])
            TS(P, "dve", D8(2 + l), M8(2), 1.0 / ALPHA, None, ALU.mult, None, [MODS], [DER])
            TS(P, "dve", D8(4 + l), M8(5), 1.0 / ALPHA, None, ALU.mult, None, [MODS], [DER])
            TS(P, "dve", D8(8 + l), M8(4), 1.0, None, ALU.add, None, [MODS], [DER])
            TT(P, "pool", D8(6 + l), D8(8 + l), ln1g, ALU.mult, [DER, smallT], [DER])
            TT(P, "pool", D8(8 + l), D8(8 + l), ln1b, ALU.mult, [DER, smallT], [DER])
            TT(P, "dve", D8(8 + l), D8(8 + l), M8(3), ALU.add, [DER, MODS], [DER])
    if layers == 2:
        for j in range(NJ):
            ln2g = sm[:, SOFF["ln2g"]: SOFF["ln2g"] + 8]
            ln2b = sm[:, SOFF["ln2b"]: SOFF["ln2b"] + 8]
            A1n = DER.ap[:, (1 * NJ + j) * 8:(1 * NJ + j) * 8 + 8]
            SH1n = MODS.ap[:, (1 * NJ + j) * 48: (1 * NJ + j) * 48 + 8]
            g1p = DER.ap[:, (10 * NJ + j) * 8:(10 * NJ + j) * 8 + 8]
            b1p = DER.ap[:, (11 * NJ + j) * 8:(11 * NJ + j) * 8 + 8]
            TT(P, "pool", g1p, A1n, ln2g, ALU.mult, [DER, smallT], [DER])
            TT(P, "pool", b1p, A1n, ln2b, ALU.mult, [DER, smallT], [DER])
            TT(P, "dve", b1p, b1p, SH1n, ALU.add, [DER, MODS], [DER])

    def load_w(nm, m, ncol=1024, buf=None):
        if (nm, m) not in wstate["seen"]:
            wstate["seen"].append((nm, m))
        if order is not None:
            prepass_upto(wpos[(nm, m)] + 1 + LOOK)
        if buf is None:
            buf = WA[wa_i[0] % 3]
            wa_i[0] += 1
        P.dma("sp", buf.ap[:, 0:ncol], wscr[nm][m * 128:(m + 1) * 128, :], reads=[wscrT[nm][m]], writes=[buf])
        return buf

    def proj_fm(wt, rhs3, rhsT, c0, c1, nk=KC):
        ps = nb_()
        w3 = wt.ap[:, 0:nk * 128].rearrange("p (k m) -> p k m", k=nk)
        for kc in range(nk):
            MM(P, ps, ps.ap[:, 0:c1 - c0], w3[:, kc, :], rhs3[:, kc, c0:c1], kc == 0, kc == nk - 1, [wt] + rhsT)
        return ps

    def seqs_of(b, with_ctx):
        s = []
        if with_ctx:
            s.append((0, CT, NB))
        s.append((CT, NT, b))
        return s

    def layer_norm_piece(c0, c1, j, l, which, lnT, nextmod, hskip=0):
        n = c1 - c0
        SQ, MS, RS = lnT
        psm = nb_()
        pse = nb_()
        for mc in range(KC):
            sq = SQ[mc % 2]
            ACT(P, sq.ap[:, 0:n], X3[:, mc, c0:c1], AF.Square, xt(mc, c0, c1), [sq])
            MM(P, psm, psm.ap[:, 0:n], onesD.ap, X3[:, mc, c0:c1], mc == 0, mc == KC - 1, [onesD] + xt(mc, c0, c1))
            MM(P, pse, pse.ap[:, 0:n], onesD.ap, sq.ap[:, 0:n], mc == 0, mc == KC - 1, [onesD, sq])
        ACT(P, MS.ap[:, 0:n], psm.ap[:, 0:n], AF.Square, [psm], [MS])
        TT(P, "dve", MS.ap[:, 0:n], pse.ap[:, 0:n], MS.ap[:, 0:n], ALU.subtract, [pse, MS], [MS])
        ACT(P, MS.ap[:, 0:n], MS.ap[:, 0:n], AF.Sqrt, [MS, cvec], [MS], bias=EPSc, scale=1.0)
        P.op("dve", lambda h: h.reciprocal(RS.ap[:, 0:n], MS.ap[:, 0:n]), reads=[MS], writes=[RS])
        gname = "ln1g" if which == 1 else "ln2g"
        bname = "ln1b" if which == 1 else "ln2b"
        for mc in range(KC):
            xs = X3[:, mc, c0:c1]
            TT(P, "dve", xs, xs, psm.ap[:, 0:n], ALU.subtract, xt(mc, c0, c1) + [psm], xt(mc, c0, c1))
            TT(P, "pool", xs, xs, RS.ap[:, 0:n], ALU.mult, xt(mc, c0, c1) + [RS], xt(mc, c0, c1))
            if nextmod is not None:
                gk, bk = nextmod
                ACT(P, H3[:, mc, c0:c1 - hskip], X3[:, mc, c0:c1 - hskip], AF.Identity, xt(mc, c0, c1) + [DER], ht1(mc, c0, c1), bias=der(bk, j, mc), scale=der(gk, j, mc))
            ACT(P, xs, xs, AF.Identity, xt(mc, c0, c1) + [smallT], xt(mc, c0, c1), bias=scol(bname, l * 8 + mc), scale=scol(gname, l * 8 + mc))

    def proj_resid_ln(b, l, which, wname, nk, rhs3, rhsT, seqs, col_shift, lnT, nextmod, wbufs=None):
        gk = (2 if which == 1 else 4) + l
        for (lo, hi, j) in seqs:
            for (c0, c1) in pieces(lo, hi):
                for mc in range(KC):
                    if wbufs is None:
                        wt = load_w(wname, mc)
                    else:
                        wt = load_w(wname, mc, ncol=nk * 128, buf=wbufs[mc % 2])
                    ps = nb_()
                    w3 = wt.ap[:, 0:nk * 128].rearrange("p (k m) -> p k m", k=nk)
                    for kc in range(nk):
                        MM(P, ps, ps.ap[:, 0:c1 - c0], w3[:, kc, :], rhs3[:, kc, c0 - col_shift:c1 - col_shift],
                           kc == 0, kc == nk - 1, [wt] + rhsT)
                    xs = X3[:, mc, c0:c1]
                    STT(P, "dve", xs, ps.ap[:, 0:c1 - c0], der(gk, j, mc), xs, ALU.mult, ALU.add, [ps, DER] + xt(mc, c0, c1), xt(mc, c0, c1))
                layer_norm_piece(c0, c1, j, l, which, lnT, nextmod)

    def conv_seq(dst, src, dstT, srcT, lo, hi, dlo, slo, wname, wbase, K, padl, bias_ap, seq_lo, seq_hi, eng="dve"):
        ctr = padl
        TS(P, eng, dst[:, lo - dlo:hi - dlo], src[:, lo - slo:hi - slo], scol(wname, wbase + ctr), bias_ap, ALU.mult, ALU.add,
           [srcT, smallT], [dstT])
        for k in range(K):
            if k == ctr:
                continue
            o = k - padl
            t0 = max(lo, seq_lo - o)
            t1 = min(hi, seq_hi - o)
            if t1 <= t0:
                continue
            STT(P, eng, dst[:, t0 - dlo:t1 - dlo], src[:, t0 + o - slo:t1 + o - slo], scol(wname, wbase + k), dst[:, t0 - dlo:t1 - dlo],
                ALU.mult, ALU.add, [srcT, smallT, dstT], [dstT])

    def ffn_stage(b, l, seqs_super, nextmod):
        specs = [(fc * 2048, 1024, BF16) for fc in range(FC)]
        specs += [(45056, 1056, F32), (49280, 1024, F32), (53376, DFF, BF16), (59008, DFF, BF16)]
        specs += [(64640 + i * 2048, 512, F32) for i in range(4)]
        ts = uphase(specs)
        A = ts[0:FC]
        G, CV = ts[FC], ts[FC + 1]
        WD = ts[FC + 2:FC + 4]
        lnT = (ts[FC + 4:FC + 6], ts[FC + 6], ts[FC + 7])
        wg, wu, wd = "wg%d" % l, "wu%d" % l, "wd%d" % l
        patch = []
        for (lo, hi, j, seq_lo, seq_hi) in seqs_super:
            glo = max(seq_lo, lo - 1)
            ghi = min(seq_hi, hi + 1)
            for fc in range(FC):
                wtg = load_w(wg, fc)
                for (c0, c1) in pieces(glo, ghi):
                    ps = proj_fm(wtg, H3, ht(c0, c1), c0, c1)
                    CP(P, "act", G.ap[:, c0 - glo:c1 - glo], ps.ap[:, 0:c1 - c0], [ps], [G])
                conv_seq(CV.ap, G.ap, CV, G, lo, hi, lo, glo, "fcw", (l * FC + fc) * 3, 3, 1, scol("fcb", l * FC + fc), seq_lo, seq_hi)
                ACT(P, CV.ap[:, 0:hi - lo], CV.ap[:, 0:hi - lo], AF.Silu, [CV], [CV])
                wtu = load_w(wu, fc)
                for (c0, c1) in pieces(lo, hi):
                    ps = proj_fm(wtu, H3, ht(c0, c1), c0, c1)
                    TT(P, "dve", A[fc].ap[:, c0 - lo:c1 - lo], ps.ap[:, 0:c1 - c0], CV.ap[:, c0 - lo:c1 - lo], ALU.mult, [ps, CV], [A[fc]])
            A3 = AP(A[0].ap.tensor, A[0].ap.offset, [list(A[0].ap.ap[0]), [1024, FC], [1, 1024]])
            gk = 4 + l
            for (c0, c1) in pieces(lo, hi):
                for mc in range(KC):
                    wt = load_w(wd, mc, ncol=DFF, buf=WD[mc % 2])
                    ps = nb_()
                    w3 = wt.ap.rearrange("p (k m) -> p k m", k=FC)
                    for kc in range(FC):
                        MM(P, ps, ps.ap[:, 0:c1 - c0], w3[:, kc, :], A3[:, kc, c0 - lo:c1 - lo], kc == 0, kc == FC - 1, [wt, A[kc]])
                    xs = X3[:, mc, c0:c1]
                    STT(P, "dve", xs, ps.ap[:, 0:c1 - c0], der(gk, j, mc), xs, ALU.mult, ALU.add, [ps, DER] + xt(mc, c0, c1), xt(mc, c0, c1))
                more = any((s2[3] == seq_lo and s2[0] == hi) for s2 in seqs_super)
                hs = 1 if (nextmod is not None and c1 == hi and more) else 0
                layer_norm_piece(c0, c1, j, l, 2, lnT, nextmod, hskip=hs)
                if l == 1:
                    for mc in range(KC):
                        P.dma("sp" if mc % 2 == 0 else "pool", yT[b, mc][:, c0 - CT:c1 - CT], X3[:, mc, c0:c1], reads=xt(mc, c0, c1))
                if hs:
                    patch.append((hi - 1, j))
        if patch:
            patch_h(patch, l + 1)

    def patch_h(patch, lnext):
        for (col, j) in patch:
            for mc in range(KC):
                ACT(P, H3[:, mc, col:col + 1], X3[:, mc, col:col + 1], AF.Identity, xt(mc, col, col + 1) + [DER, MODS], ht1(mc, col, col + 1),
                    bias=mods(lnext, j, mc), scale=der(0 + lnext, j, mc))

    def even_mixer(b):
        specs = [(c * 4608, NT, BF16) for c in range(KC)]
        specs += [(36864 + i * 9216, NT, F32) for i in range(3)]
        specs += [(64512, NT, BF16)]
        specs += [(69120 + i * 2048, 512, F32) for i in range(4)]
        ts = uphase(specs)
        YC = ts[0:KC]
        setA = (ts[KC], ts[KC + 1], ts[KC + 2], ts[KC + 3], ts[KC + 4:KC + 7], sttA)
        TM = ts[KC + 4:KC + 8]
        def xcarve(off, n, dt):
            if dt == F32:
                return Xs[:, off // 4: off // 4 + n]
            return Xs[:, off // 4: off // 4 + (n + 1) // 2].bitcast(BF16)[:, :n]
        xs_specs = [(i * 9216, NT, F32) for i in range(3)] + [(27648, NT, BF16)] + [(32256 + i * 2048, 512, F32) for i in range(2)]
        xs_specs += [(36352, NT, F32), (45568, NT, F32)]
        xsT = [T(xcarve(o, n, dt)) for (o, n, dt) in xs_specs]
        borrowed = [t for c in range(6) for t in XR[c]]
        P.realias(xsT, borrowed)
        setB = (xsT[0], xsT[1], xsT[2], xsT[3], xsT[4:6], sttB)
        S0, S1 = xsT[6], xsT[7]
        seqs = seqs_of(b, True)
        allp = [pc for (lo, hi, j) in seqs for pc in pieces(lo, hi)]

        def rec_chunk(c, bufs, wbuf):
            R0, R1, R2, XCb, TMs, st = bufs
            tm_i = [0]

            def ntmp():
                t = TMs[tm_i[0] % len(TMs)]
                tm_i[0] += 1
                return t
            wt = load_w("win", c, buf=wbuf)
            for (c0, c1) in allp:
                ps = proj_fm(wt, H3, ht(c0, c1), c0, c1)
                CP(P, "act", R0.ap[:, c0:c1], ps.ap[:, 0:c1 - c0], [ps], [R0])
                yield
            for (lo, hi, j) in seqs:
                conv_seq(R1.ap, R0.ap, R1, R0, lo, hi, 0, 0, "rcw", c * 4, 4, 2, scol("rcb", c), lo, hi)
            yield
            CP(P, "pool", XCb.ap, R1.ap, [R1], [XCb])
            yield
            for d in range(2):
                Bd = R2 if d == 0 else R1
                for (c0, c1) in allp:
                    n = c1 - c0
                    psr = nb_()
                    MM(P, psr, psr.ap[:, 0:n], GW.ap[:, ((d * 2 + 0) * 4 + c) * 128:((d * 2 + 0) * 4 + c + 1) * 128], XCb.ap[:, c0:c1], True, True, [GW, XCb])
                    psi = nb_()
                    MM(P, psi, psi.ap[:, 0:n], GW.ap[:, ((d * 2 + 1) * 4 + c) * 128:((d * 2 + 1) * 4 + c + 1) * 128], XCb.ap[:, c0:c1], True, True, [GW, XCb])
                    ACT(P, R0.ap[:, c0:c1], psr.ap[:, 0:n], AF.Sigmoid, [psr, smallT], [R0], bias=scol("rba", d * 4 + c))
                    t1 = ntmp()
                    ACT(P, t1.ap[:, 0:n], psi.ap[:, 0:n], AF.Sigmoid, [psi, smallT], [t1], bias=scol("rbx", d * 4 + c))
                    if d == 0:
                        TT(P, "dve", Bd.ap[:, c0:c1], t1.ap[:, 0:n], R1.ap[:, c0:c1], ALU.mult, [t1, R1], [Bd])
                    else:
                        TT(P, "dve", Bd.ap[:, c0:c1], R1.ap[:, c0:c1], t1.ap[:, 0:n], ALU.mult, [t1, R1], [Bd])
                    yield
                ACT(P, R0.ap, R0.ap, AF.Exp, [R0, CLAM], [R0], scale=CLAM.ap[:, d * 4 + c:d * 4 + c + 1])
                yield
                for (c0, c1) in allp:
                    n = c1 - c0
                    a_ = R0.ap[:, c0:c1]
                    t2 = ntmp()
                    TT(P, "pool", t2.ap[:, 0:n], a_, a_, ALU.mult, [R0], [t2])
                    ACT(P, t2.ap[:, 0:n], t2.ap[:, 0:n], AF.Sqrt, [t2, cvec], [t2], bias=ONE, scale=-1.0)
                    TT(P, "dve", Bd.ap[:, c0:c1], Bd.ap[:, c0:c1], t2.ap[:, 0:n], ALU.mult, [Bd, t2], [Bd])
                    yield
                if d == 0:
                    SCAN(P, Bd.ap[:, 0:CT], R0.ap[:, 0:CT], Bd.ap[:, 0:CT], 0.0, [R0, Bd], [Bd])
                    CP(P, "act", st.ap[:, 0:1], Bd.ap[:, CT - 1:CT], [Bd], [st])
                    yield
                    SCAN(P, Bd.ap[:, CT:NT], R0.ap[:, CT:NT], Bd.ap[:, CT:NT], st.ap[:, 0:1], [R0, Bd, st], [Bd])
                else:
                    SCAN(P, rev(Bd.ap[:, 0:CT]), rev(R0.ap[:, 0:CT]), rev(Bd.ap[:, 0:CT]), 0.0, [R0, Bd], [Bd])
                    CP(P, "act", st.ap[:, 1:2], Bd.ap[:, 0:1], [Bd], [st])
                    yield
                    SCAN(P, rev(Bd.ap[:, CT:NT]), rev(R0.ap[:, CT:NT]), rev(Bd.ap[:, CT:NT]), st.ap[:, 1:2], [R0, Bd, st], [Bd])
                yield
            TT(P, "pool", R2.ap, R2.ap, R1.ap, ALU.add, [R2, R1], [R2])
            yield
            wt = load_w("win", 4 + c, buf=wbuf)
            for (c0, c1) in allp:
                n = c1 - c0
                ps = proj_fm(wt, H3, ht(c0, c1), c0, c1)
                t1 = ntmp()
                ACT(P, t1.ap[:, 0:n], ps.ap[:, 0:n], AF.Gelu_apprx_tanh, [ps], [t1])
                TT(P, "dve", YC[c].ap[:, c0:c1], t1.ap[:, 0:n], R2.ap[:, c0:c1], ALU.mult, [t1, R2], [YC[c]])
                yield

        def sc_chunk(c, wbuf):
            wt = load_w("win", 12 + c, buf=wbuf)
            for (c0, c1) in allp:
                ps = proj_fm(wt, H3, ht(c0, c1), c0, c1)
                CP(P, "act", S0.ap[:, c0:c1], ps.ap[:, 0:c1 - c0], [ps], [S0])
                yield
            wt = load_w("win", 16 + c, buf=wbuf)
            for (c0, c1) in allp:
                ps = proj_fm(wt, H3, ht(c0, c1), c0, c1)
                TT(P, "dve", S0.ap[:, c0:c1], ps.ap[:, 0:c1 - c0], S0.ap[:, c0:c1], ALU.mult, [ps, S0], [S0])
                yield
            for (lo, hi, j) in seqs:
                conv_seq(S1.ap, S0.ap, S1, S0, lo, hi, 0, 0, "scw", c * 3, 3, 1, scol("scb", c), lo, hi)
            yield
            wt = load_w("win", 8 + c, buf=wbuf)
            for (c0, c1) in allp:
                ps = proj_fm(wt, H3, ht(c0, c1), c0, c1)
                TT(P, "dve", YC[4 + c].ap[:, c0:c1], ps.ap[:, 0:c1 - c0], S1.ap[:, c0:c1], ALU.mult, [ps, S1], [YC[4 + c]])
                yield

        def chain(gs):
            for g in gs:
                yield from g
        gens = [chain([rec_chunk(0, setA, WA[0]), rec_chunk(2, setA, WA[0])]),
                chain([rec_chunk(1, setB, WA[1]), rec_chunk(3, setB, WA[1])]),
                chain([sc_chunk(c, WA[2]) for c in range(4)])]
        while gens:
            for g in list(gens):
                try:
                    next(g)
                except StopIteration:
                    gens.remove(g)
        P.realias(borrowed, xsT)
        for r, (lo, hi) in enumerate(REG):
            for c in range(6):
                q = "sp" if c % 2 == 0 else "pool"
                if r == 0:
                    P.dma(q, X3[:, c, 0:CT], ctxT[b, c], writes=[XR[c][0]])
                else:
                    P.dma(q, X3[:, c, lo:hi], xT[b, c][:, lo - CT:hi - CT], writes=[XR[c][r]])
        YC3 = AP(YC[0].ap.tensor, YC[0].ap.offset, [list(YC[0].ap.ap[0]), [NT, KC], [1, NT]])
        lnT = ([TM[0], TM[1]], TM[2], TM[3])
        proj_resid_ln(b, 0, 1, "wout0", KC, YC3, YC, seqs, 0, lnT, (6, 8))

    def odd_mixer(b):
        specs = [(c * 4096, SEQ, BF16) for c in range(KC)]
        specs += [(32768, SEQ, BF16), (36864, SEQ, BF16), (40960, NT, BF16), (45568, 18 * 192, BF16)]
        specs += [(52480 + i * 3584, 896, F32) for i in range(4)]
        specs += [(66816 + i * 2048, 512, F32) for i in range(2)]
        specs += [(70912 + i * 1024, 512, BF16) for i in range(8)]
        specs += [(79104 + i * 1024, 256, F32) for i in range(2)]
        ts = uphase(specs)
        OT = ts[0:KC]
        QZ0, QZ1, KT, V = ts[KC:KC + 4]
        TB = ts[KC + 4:KC + 8]
        E = ts[KC + 8:KC + 10]
        PT = ts[KC + 10:KC + 18]
        RC = [ts[KC + 18], ts[KC + 18]]
        OS = ts[KC + 19]
        P.op("pool", lambda h: h.memset(QZ0.ap[64:128, :], 0.0), writes=[QZ0])
        P.op("pool", lambda h: h.memset(QZ1.ap[0:64, :], 0.0), writes=[QZ1])
        V3 = V.ap.rearrange("p (t m) -> p t m", t=18)
        P.op("pool", lambda h: h.memset(V3[:, :, 64:128], 1.0), writes=[V])
        cnt = dict(e=0, p=0, r=0)
        j = b
        for hp in range(KC):
            wq = load_w("wqkv", hp)
            for (c0, c1) in pieces(CT, NT):
                ps = proj_fm(wq, H3, ht(c0, c1), c0, c1)
                CP(P, "act", QZ0.ap[0:64, c0 - CT:c1 - CT], ps.ap[0:64, 0:c1 - c0], [ps], [QZ0])
                CP(P, "dve", QZ1.ap[64:128, c0 - CT:c1 - CT], ps.ap[64:128, 0:c1 - c0], [ps], [QZ1])
            wk = load_w("wqkv", 8 + hp)
            for (c0, c1) in pieces(0, NT):
                ps = proj_fm(wk, H3, ht(c0, c1), c0, c1)
                CP(P, "dve", KT.ap[:, c0:c1], ps.ap[:, 0:c1 - c0], [ps], [KT])
            wv = load_w("wqkv", 16 + hp)
            wv3 = wv.ap.rearrange("p (k m) -> p k m", k=KC)
            for t0 in range(0, 18, 4):
                nt_ = min(4, 18 - t0)
                ps = nb_()
                for ti in range(nt_):
                    tt = t0 + ti
                    for kc in range(KC):
                        MM(P, ps, ps.ap[:, ti * 128:(ti + 1) * 128], H3[:, kc, tt * 128:(tt + 1) * 128], wv3[:, kc, :],
                           kc == 0, kc == KC - 1, [wv] + ht(tt * 128, (tt + 1) * 128))
                ps3 = ps.ap[:, 0:nt_ * 128].rearrange("p (t m) -> p t m", t=nt_)
                CP(P, "act", V3[:, t0:t0 + nt_, 0:64], ps3[:, :, 0:64], [ps], [V])
                CP(P, "dve", V3[:, t0:t0 + nt_, 128:192], ps3[:, :, 64:128], [ps], [V])
            for par in range(2):
                h_ = 2 * hp + par
                P.dma("sp", TB[par * 2].ap, tmid[h_ * 128:(h_ + 1) * 128, :], writes=[TB[par * 2]])
                P.dma("sp", TB[par * 2 + 1].ap, tfull[h_ * 128:(h_ + 1) * 128, :], writes=[TB[par * 2 + 1]])
                ACT(P, TB[par * 2].ap, TB[par * 2].ap, AF.Exp, [TB[par * 2]], [TB[par * 2]])
                ACT(P, TB[par * 2 + 1].ap, TB[par * 2 + 1].ap, AF.Exp, [TB[par * 2 + 1]], [TB[par * 2 + 1]])
            items = [(par, r0) for par in range(2) for r0 in range(0, 32, 4)]

            def s_phase(par, r0):
                pb = par * 64
                q0 = r0 * 64
                if r0 == 0:
                    chunks, kind = [0, 1, 2, 3], 1
                elif r0 == 28:
                    chunks, kind = [12, 13, 14, 15], 1
                else:
                    a0 = (r0 - 4) // 2
                    chunks, kind = list(range(a0, a0 + 6)), 0
                tb = TB[par * 2 + kind]
                QZ = QZ0 if par == 0 else QZ1
                pts = []
                for i in range(0, len(chunks), 2):
                    ps = nb_()
                    for k in range(2):
                        a = chunks[i + k]
                        MM(P, ps, ps.ap[:, k * 256:(k + 1) * 256], KT.ap[:, CT + a * 128:CT + (a + 1) * 128],
                           QZ.ap[:, q0:q0 + 256], True, True, [KT, QZ])
                    e = E[cnt["e"] % 2]
                    cnt["e"] += 1
                    ACT(P, e.ap, ps.ap, AF.Exp, [ps], [e], scale=0.125)
                    pt = PT[cnt["p"] % 8]
                    cnt["p"] += 1
                    ei0 = r0 - 2 * chunks[i] + 6
                    tap = AP(tb.ap.tensor, tb.ap[:, ei0 * 64:ei0 * 64 + 1].offset, [list(tb.ap.ap[0]), [-128, 2], [1, 256]])
                    TT(P, "pool", pt.ap.rearrange("p (a b) -> p a b", a=2), e.ap.rearrange("p (a b) -> p a b", a=2), tap, ALU.mult, [e, tb], [pt])
                    pts.append((pt, chunks[i], chunks[i + 1]))
                ps = nb_()
                for k in range(2):
                    MM(P, ps, ps.ap[:, k * 256:(k + 1) * 256], KT.ap[:, k * 128:(k + 1) * 128],
                       QZ.ap[:, q0:q0 + 256], True, True, [KT, QZ])
                ptc = PT[cnt["p"] % 8]
                cnt["p"] += 1
                ACT(P, ptc.ap, ps.ap, AF.Exp, [ps], [ptc], scale=0.125)
                return (par, r0, pts, ptc)

            def pv_phase(st):
                par, r0, pts, ptc = st
                pb = par * 64
                q0 = r0 * 64
                pso = nb_()
                ob = 64 - pb
                mms = []
                for (pt, a, a2) in pts:
                    mms.append((V3[:, 2 + a, pb:pb + 128], pt.ap[:, 0:256], pt))
                    mms.append((V3[:, 2 + a2, pb:pb + 128], pt.ap[:, 256:512], pt))
                mms.append((V3[:, 0, pb:pb + 128], ptc.ap[:, 0:256], ptc))
                mms.append((V3[:, 1, pb:pb + 128], ptc.ap[:, 256:512], ptc))
                nm_ = len(mms)
                for i, (l_, r_, pt) in enumerate(mms):
                    MM(P, pso, pso.ap[:, 0:256], l_, r_, i == 0, i == nm_ - 1, [V, pt])
                rc = RC[cnt["r"] % 2]
                cnt["r"] += 1
                P.op("dve", lambda h: h.reciprocal(rc.ap[pb:pb + 64, :], pso.ap[ob:ob + 64, 0:256]), reads=[pso], writes=[rc])
                CP(P, "act", OS.ap[pb:pb + 64, :], pso.ap[pb:pb + 64, 0:256], [pso], [OS])
                TT(P, "pool", OT[hp].ap[pb:pb + 64, q0:q0 + 256], OS.ap[pb:pb + 64, :], rc.ap[pb:pb + 64, :], ALU.mult, [OS, rc], [OT[hp]])

            prev = None
            for (par, r0) in items:
                st = s_phase(par, r0)
                if prev is not None:
                    pv_phase(prev)
                prev = st
            pv_phase(prev)
        OT3 = AP(OT[0].ap.tensor, OT[0].ap.offset, [list(OT[0].ap.ap[0]), [SEQ, KC], [1, SEQ]])
        lnA, lnB = T(ucarve(70912, 512, F32)), T(ucarve(72960, 512, F32))
        lnT = ([E[0], E[1]], lnA, lnB)
        P.realias([lnA, lnB], PT)
        u_live.extend([lnA, lnB])
        proj_resid_ln(b, 1, 1, "wout1", KC, OT3, OT, seqs_of(b, False), CT, lnT, (6 + 1, 8 + 1))

    for b in range(NB):
        for c in range(KC):
            q = "sp" if c % 2 == 0 else "pool"
            P.dma(q, X3[:, c, 0:CT], ctxT[b, c], writes=[XR[c][0]])
            P.dma(q, X3[:, c, CT:CT + 1024], xT[b, c][:, 0:1024], writes=[XR[c][1]])
            P.dma(q, X3[:, c, CT + 1024:NT], xT[b, c][:, 1024:2048], writes=[XR[c][2]])
        for c in range(KC):
            for r, (lo, hi) in enumerate(REG):
                j = NB if r == 0 else b
                ACT(P, H3[:, c, lo:hi], X3[:, c, lo:hi], AF.Identity, [XR[c][r], DER, MODS], [HR[c][r]], bias=mods(0, j, c), scale=der(0, j, c))
        if stop != "load":
            even_mixer(b)
        nm0 = (10, 11) if layers == 2 else None
        if stop is None:
            ffn_stage(b, 0, [(0, CT, NB, 0, CT), (CT, CT + 1024, b, CT, NT), (CT + 1024, NT, b, CT, NT)], nm0)

        if layers == 2:
            odd_mixer(b)
            ffn_stage(b, 1, [(CT, CT + 1024, b, CT, NT), (CT + 1024, NT, b, CT, NT)], None)
        if not (layers == 2 and stop is None):
            for c in range(KC):
                q = "sp" if c % 2 == 0 else "pool"
                P.dma(q, yT[b, c], X3[:, c, CT:NT], reads=[XR[c][1], XR[c][2]])
    allx = [t for c in range(KC) for t in XR[c]]
    P.wait_all("sp", allx)
    P.worder = wstate["seen"]
    P.finish()
    return nc, P


def _wt(W):
    K, M = W.shape
    return np.ascontiguousarray(W.reshape(K // 128, 128, M // 128, 128).transpose(2, 1, 0, 3)).reshape(M, K)


def _pc(v, nchunk):
    v = np.asarray(v, np.float32)
    lead = v.shape[:-1]
    return np.moveaxis(v.reshape(lead + (nchunk, 128)), -1, 0)


def _tables(rpb):
    kr2 = np.arange(2)[:, None, None, None]
    kcol = np.arange(64)[None, :, None, None]
    e = (np.arange(14) - 6)[None, None, :, None]
    qcol = np.arange(64)[None, None, None, :]
    dr = kr2 - e + 7 + 0 * kcol + 0 * qcol
    dc = kcol - qcol + 15 + 0 * kr2 + 0 * e
    cs = np.clip(qcol - 8, 0, 48)
    colok = (kcol >= cs) & (kcol < cs + 16)
    drok = (dr >= 0) & (dr <= 14)
    rowmid = (kr2 - e >= -4) & (kr2 - e <= 3)
    g = rpb[:, np.clip(dr, 0, 14), np.clip(dc, 0, 30)]
    negs = np.full(g.shape, NEG, np.float32)
    tfull = np.where((colok & drok)[None], g, negs).reshape(16 * 128, 896)
    tmid = np.where((colok & drok & rowmid)[None], g, negs).reshape(16 * 128, 896)
    return np.ascontiguousarray(tmid, np.float32), np.ascontiguousarray(tfull, np.float32)


def prep_shared(mod_w, mod_b, ln1_g, ln1_b, ln2_g, ln2_b, ev_w_in, ev_w_out, rec_conv_w, rec_conv_b, rec_wa, rec_ba,
                rec_wx, rec_bx, rec_lam, sc_conv_w, sc_conv_b, na_w_qkv, na_w_out, na_rpb, ffn_w_gate, ffn_w_up,
                ffn_conv_w, ffn_conv_b, ffn_w_down):
    sh = {}
    sh["win"] = _wt(ev_w_in[0])
    sh["wout0"] = _wt(ev_w_out[0])
    sh["wqkv"] = _wt(na_w_qkv[0])
    sh["wout1"] = _wt(na_w_out[0])
    for l in range(2):
        sh["wg%d" % l] = _wt(ffn_w_gate[l])
        sh["wu%d" % l] = _wt(ffn_w_up[l])
        sh["wd%d" % l] = _wt(ffn_w_down[l])
    sh["modw"] = np.concatenate([_wt(mod_w[l]) for l in range(2)], axis=0)
    small = np.zeros((128, NS), np.float32)

    def put(nm, arr):
        arr = np.asarray(arr, np.float32).reshape(128, -1)
        small[:, SOFF[nm]:SOFF[nm] + arr.shape[1]] = arr
    put("modb", _pc(mod_b, 48))
    put("ln1g", _pc(ln1_g, 8)); put("ln1b", _pc(ln1_b, 8)); put("ln2g", _pc(ln2_g, 8)); put("ln2b", _pc(ln2_b, 8))
    put("rcw", np.transpose(_pc(rec_conv_w[0], 4), (0, 2, 1)))
    put("rcb", _pc(rec_conv_b[0], 4))
    put("rba", _pc(rec_ba[0], 4)); put("rbx", _pc(rec_bx[0], 4)); put("rlam", _pc(rec_lam[0], 4))
    put("scw", np.transpose(_pc(sc_conv_w[0], 4), (0, 2, 1)))
    put("scb", _pc(sc_conv_b[0], 4))
    put("fcw", np.transpose(_pc(ffn_conv_w, FC), (0, 1, 3, 2)))
    put("fcb", _pc(ffn_conv_b, FC))
    sh["smallp"] = small
    gw = np.zeros((128, 16, 128), np.float32)
    for d in range(2):
        for wi, W in enumerate((rec_wa[0], rec_wx[0])):
            for c in range(4):
                g = (d * 2 + wi) * 4 + c
                gw[0:64, g, 0:64] = W[d, 2 * c]
                gw[64:128, g, 64:128] = W[d, 2 * c + 1]
    sh["gw"] = gw.reshape(128, 16 * 128)
    sh["tmid"], sh["tfull"] = _tables(np.asarray(na_rpb[0], np.float32))
    return sh


def prep_core(x, c, ctx, c_ctx, b0, NB):
    m = {}
    m["xT"] = np.ascontiguousarray(x[b0:b0 + NB].transpose(0, 2, 1)).reshape(NB, KC, 128, SEQ)
    m["ctxT"] = np.ascontiguousarray(ctx[b0:b0 + NB].transpose(0, 2, 1)).reshape(NB, KC, 128, CT)
    cc = np.concatenate([c[b0:b0 + NB], c_ctx[None, :]], axis=0)
    m["cT"] = np.ascontiguousarray(np.transpose(cc.reshape(NB + 1, KC, 128), (2, 1, 0))).reshape(128, KC * (NB + 1))
    return m


_CACHE = {}


def kernel(x, c, ctx, c_ctx, mod_w, mod_b, ln1_g, ln1_b, ln2_g, ln2_b, ev_w_in, ev_w_out, rec_conv_w, rec_conv_b,
           rec_wa, rec_ba, rec_wx, rec_bx, rec_lam, sc_conv_w, sc_conv_b, na_w_qkv, na_w_out, na_rpb, ffn_w_gate,
           ffn_w_up, ffn_conv_w, ffn_conv_b, ffn_w_down):
    f = lambda a: np.asarray(a, np.float32)
    x, c, ctx, c_ctx = f(x), f(c), f(ctx), f(c_ctx)
    B = x.shape[0]
    NB = B // NCORES
    sh = prep_shared(f(mod_w), f(mod_b), f(ln1_g), f(ln1_b), f(ln2_g), f(ln2_b), f(ev_w_in), f(ev_w_out), f(rec_conv_w),
                     f(rec_conv_b), f(rec_wa), f(rec_ba), f(rec_wx), f(rec_bx), f(rec_lam), f(sc_conv_w), f(sc_conv_b),
                     f(na_w_qkv), f(na_w_out), f(na_rpb), f(ffn_w_gate), f(ffn_w_up), f(ffn_conv_w), f(ffn_conv_b), f(ffn_w_down))
    if NB not in _CACHE:
        _, p0 = build_program(1)
        _CACHE[NB] = build_program(NB, order=p0.worder)[0]
    nc = _CACHE[NB]
    in_maps = []
    for i in range(NCORES):
        m = dict(sh)
        m.update(prep_core(x, c, ctx, c_ctx, i * NB, NB))
        in_maps.append(m)
    res = run_bass_kernel_spmd(nc, in_maps, core_ids=list(range(NCORES)))
    out = np.empty((B, SEQ, D), np.float32)
    for i in range(NCORES):
        y = np.asarray(res.results[i]["yT"]).reshape(NB, D, SEQ)
        out[i * NB:(i + 1) * NB] = y.transpose(0, 2, 1)
    return out
```

```python
import numpy as np
import concourse.bass as bass
import concourse.mybir as mybir
from concourse.bass_utils import run_bass_kernel_spmd
from concourse.ap import AP

F32 = mybir.dt.float32
BF16 = mybir.dt.bfloat16
ALU = mybir.AluOpType
AF = mybir.ActivationFunctionType

NCORES = 8
D = 1024
KC = 8
SEQ = 2048
CT = 256
NT = SEQ + CT
DFF = 2816
FC = 22
ALPHA = 4.0 ** 0.25
EPS2 = 1e-6 / (ALPHA * ALPHA)
NEG = -30000.0


class T:
    __slots__ = ("ap", "w", "r")

    def __init__(self, ap):
        self.ap = ap
        self.w = None
        self.r = {}


class Eng:
    def __init__(self, name, h, sem):
        self.name = name
        self.h = h
        self.sem = sem
        self.count = 0
        self.waited = {}
        self.q = []


class Prog:
    def __init__(self, nc, n_dma_sems=8):
        self.nc = nc
        self._ctx = []
        self.engs = {}
        for nm, h in (("pe", nc.tensor), ("act", nc.scalar), ("dve", nc.vector), ("pool", nc.gpsimd), ("sp", nc.sync)):
            self.engs[nm] = Eng(nm, h, self._sem("e_" + nm))
        self.n_dma_sems = n_dma_sems
        self.dma_pool = {}
        for nm in ("sp", "pool"):
            self.dma_pool[nm] = dict(sems=[self._sem("d_%s%d" % (nm, i)) for i in range(n_dma_sems)], j=0)
        self.ninst = 0
        self.force_own = False

    def _sem(self, name):
        cm = self.nc.semaphore(name)
        s = cm.__enter__()
        self._ctx.append(cm)
        return (name, s)

    def sbuf(self, name, shape, dt):
        cm = self.nc.sbuf_tensor(name, shape, dt)
        t = cm.__enter__()
        self._ctx.append(cm)
        return t

    def psum(self, name, shape, dt):
        cm = self.nc.psum_tensor(name, shape, dt)
        t = cm.__enter__()
        self._ctx.append(cm)
        return t

    @staticmethod
    def _deps(reads, writes):
        deps = {}

        def add(k, v):
            if deps.get(k, 0) < v:
                deps[k] = v
        for t in reads:
            if t.w is not None:
                add(*t.w)
        for t in writes:
            if t.w is not None:
                add(*t.w)
            for k, v in t.r.items():
                add(k, v)
        return deps

    def _waits(self, e, deps, own_too):
        for (name, sh), v in deps.items():
            if (not own_too) and name == e.sem[0]:
                continue
            if e.waited.get(name, 0) >= v:
                continue
            e.waited[name] = v
            e.q.append(lambda h, sh=sh, v=v: h.wait_ge(sh, v))
            self.ninst += 1

    def op(self, eng, fn, reads=(), writes=(), sig=True, own=False):
        e = self.engs[eng]
        self._waits(e, self._deps(reads, writes), own or self.force_own)
        if sig:
            e.count += 1
            val = e.count
            sh = e.sem[1]
            e.q.append(lambda h, fn=fn, sh=sh: fn(h).then_inc(sh, 1))
        else:
            val = e.count + 1
            e.q.append(lambda h, fn=fn: fn(h))
        self.ninst += 1
        for t in reads:
            if t.r.get(e.sem, 0) < val:
                t.r[e.sem] = val
        for t in writes:
            t.w = (e.sem, val)
            t.r = {}

    def dma(self, q, out_ap, in_ap, reads=(), writes=()):
        e = self.engs[q]
        pool = self.dma_pool[q]
        j = pool["j"]
        pool["j"] += 1
        s = pool["sems"][j % self.n_dma_sems]
        gen = j // self.n_dma_sems
        deps = self._deps(reads, writes)
        if gen > 0 and deps.get(s, 0) < 16 * gen:
            deps[s] = 16 * gen
        self._waits(e, deps, True)
        val = 16 * (gen + 1)
        sh = s[1]
        e.q.append(lambda h, o=out_ap, i=in_ap, sh=sh: h.dma_start(out=o, in_=i).then_inc(sh, 16))
        self.ninst += 1
        for t in reads:
            if t.r.get(s, 0) < val:
                t.r[s] = val
        for t in writes:
            t.w = (s, val)
            t.r = {}

    def realias(self, new_ts, old_ts):
        merged = {}
        for t in old_ts:
            if t.w is not None and merged.get(t.w[0], 0) < t.w[1]:
                merged[t.w[0]] = t.w[1]
            for k, v in t.r.items():
                if merged.get(k, 0) < v:
                    merged[k] = v
        for t in new_ts:
            t.w = None
            t.r = dict(merged)

    def wait_all(self, eng, ts):
        e = self.engs[eng]
        self._waits(e, self._deps(ts, ts), True)

    def finish(self):
        nc = self.nc
        with nc.Block() as block:
            def mk(e):
                def f(h):
                    for c in e.q:
                        c(h)
                return f
            block.tensor(mk(self.engs["pe"]))
            block.scalar(mk(self.engs["act"]))
            block.vector(mk(self.engs["dve"]))
            block.gpsimd(mk(self.engs["pool"]))
            block.sync(mk(self.engs["sp"]))
        for cm in reversed(self._ctx):
            cm.__exit__(None, None, None)
        self._ctx = []


def MM(P, psT, out, lhsT, rhs, start, stop, reads):
    P.op("pe", lambda h: h.matmul(out, lhsT, rhs, start=start, stop=stop), reads=reads, writes=[psT], sig=True)


def ACT(P, out, in_, func, reads, writes, bias=None, scale=None):
    kw = {}
    if bias is not None:
        kw["bias"] = bias
    if scale is not None:
        kw["scale"] = scale
    P.op("act", lambda h: h.activation(out, in_, func, **kw), reads=reads, writes=writes)


def TT(P, eng, out, in0, in1, op, reads, writes):
    P.op(eng, lambda h: h.tensor_tensor(out, in0, in1, op), reads=reads, writes=writes)


def TS(P, eng, out, in0, s1, s2, op0, op1, reads, writes):
    if s2 is None:
        P.op(eng, lambda h: h.tensor_scalar(out, in0, s1, None, op0), reads=reads, writes=writes)
    else:
        P.op(eng, lambda h: h.tensor_scalar(out, in0, s1, s2, op0, op1), reads=reads, writes=writes)


def STT(P, eng, out, in0, sc, in1, op0, op1, reads, writes):
    P.op(eng, lambda h: h.scalar_tensor_tensor(out, in0, sc, in1, op0, op1), reads=reads, writes=writes)


def CP(P, eng, out, in_, reads, writes):
    if eng == "act":
        P.op("act", lambda h: h.copy(out, in_), reads=reads, writes=writes)
    else:
        P.op(eng, lambda h: h.tensor_copy(out, in_), reads=reads, writes=writes)


def SCAN(P, out, d0, d1, init, reads, writes):
    P.op("dve", lambda h: h.tensor_tensor_scan(out, d0, d1, init, ALU.mult, ALU.add), reads=reads, writes=writes)


def rev(ap2d):
    n = ap2d.shape[1]
    last = ap2d[:, n - 1:n]
    return AP(last.tensor, last.offset, [list(last.ap[0]), [-1, n]])


def pieces(lo, hi, step=512):
    n = -(-(hi - lo) // step)
    size = -(-(hi - lo) // n)
    size += size % 2
    out = []
    c = lo
    while c < hi:
        out.append((c, min(hi, c + size)))
        c += size
    return out


def small_layout():
    off = {}
    o = 0
    for nm, n in (("modb", 96), ("ln1g", 16), ("ln1b", 16), ("ln2g", 16), ("ln2b", 16), ("rcw", 16), ("rcb", 4),
                  ("rba", 8), ("rbx", 8), ("rlam", 8), ("scw", 12), ("scb", 4), ("fcw", 132), ("fcb", 44)):
        off[nm] = o
        o += n
    return off, o


SOFF, NS = small_layout()


def build_program(NB, layers=2, stop=None, order=None):
    nc = bass.Bass("TRN2", target_bir_lowering=False)
    NJ = NB + 1

    def din(name, shape, dt=F32):
        return nc.dram_tensor(name, list(shape), dt, kind="ExternalInput").ap()

    xT = din("xT", [NB, KC, 128, SEQ])
    ctxT = din("ctxT", [NB, KC, 128, CT])
    cT = din("cT", [128, KC * NJ])
    smallp = din("smallp", [128, NS])
    modw = din("modw", [2 * 48 * 128, 1024])
    gw_in = din("gw", [128, 16 * 128])
    tmid = din("tmid", [16 * 128, 896])
    tfull = din("tfull", [16 * 128, 896])
    wspec = [("win", 20, 1024), ("wout0", 8, 1024), ("wg0", FC, 1024), ("wu0", FC, 1024), ("wd0", 8, DFF),
             ("wqkv", 24, 1024), ("wout1", 8, 1024), ("wg1", FC, 1024), ("wu1", FC, 1024), ("wd1", 8, DFF)]
    wsrc = {}
    wscr = {}
    wscrT = {}
    for nm, nmc, ncol in wspec:
        wsrc[nm] = din(nm, [nmc * 128, ncol])
        wscr[nm] = nc.dram_tensor(nm + "_s", [nmc * 128, ncol], BF16, kind="Internal").ap()
        wscrT[nm] = [T(wscr[nm][m * 128:(m + 1) * 128, :]) for m in range(nmc)]
    yT = nc.dram_tensor("yT", [NB, KC, 128, SEQ], F32, kind="ExternalOutput").ap()

    P = Prog(nc)
    Xs = P.sbuf("X", [128, KC * NT], F32)
    Hs = P.sbuf("H", [128, KC * NT], BF16)
    UBYTES = 81920
    Us = P.sbuf("U", [128, UBYTES // 4], F32)
    X3 = Xs[:].rearrange("p (c t) -> p c t", c=KC)
    H3 = Hs[:].rearrange("p (c t) -> p c t", c=KC)
    REG = [(0, CT), (CT, CT + 1024), (CT + 1024, NT)]
    XR = [[T(X3[:, c, lo:hi]) for (lo, hi) in REG] for c in range(KC)]
    HR = [[T(H3[:, c, lo:hi]) for (lo, hi) in REG] for c in range(KC)]

    def xt(mc, c0, c1):
        return [XR[mc][r] for r, (lo, hi) in enumerate(REG) if c0 < hi and c1 > lo]

    def ht1(mc, c0, c1):
        return [HR[mc][r] for r, (lo, hi) in enumerate(REG) if c0 < hi and c1 > lo]

    def ht(c0, c1):
        return [t for mc in range(KC) for t in ht1(mc, c0, c1)]
    WA = [T(P.sbuf("wa%d" % i, [128, 1024], BF16)[:]) for i in range(3)]
    wa_i = [0]
    smallT = T(P.sbuf("smallp_sb", [128, NS], F32)[:])
    cvec = T(P.sbuf("cvec", [128, 4], F32)[:])
    sttA = T(P.sbuf("sttA", [128, 2], F32)[:])
    sttB = T(P.sbuf("sttB", [128, 2], F32)[:])
    onesD = T(P.sbuf("onesD", [128, 128], F32)[:])
    onesB = T(P.sbuf("onesB", [128, 128], BF16)[:])
    scT = T(P.sbuf("scT", [128, KC * NJ], F32)[:])
    MODS = T(P.sbuf("MODS", [128, 2 * NJ * 48], F32)[:])
    DER = T(P.sbuf("DER", [128, 12 * NJ * 8], F32)[:])
    CLAM = T(P.sbuf("CLAM", [128, 8], F32)[:])
    GW = T(P.sbuf("GW", [128, 16 * 128], BF16)[:])
    banks = [T(P.psum("ps%d" % i, [128, 512], F32)[:]) for i in range(8)]
    bank_i = [0]

    def nb_():
        t = banks[bank_i[0] % 8]
        bank_i[0] += 1
        return t

    def ucarve(off_bytes, n, dt):
        if dt == F32:
            return Us[:, off_bytes // 4: off_bytes // 4 + n]
        return Us[:, off_bytes // 4: off_bytes // 4 + (n + 1) // 2].bitcast(BF16)[:, :n]

    u_live = []

    def uphase(specs):
        new = [T(ucarve(o, n, dt)) for (o, n, dt) in specs]
        P.realias(new, u_live)
        u_live[:] = new
        return new

    sm = smallT.ap

    def scol(nm, i):
        o = SOFF[nm] + i
        return sm[:, o:o + 1]

    def mods(l, j, q):
        o = (l * NJ + j) * 48 + q
        return MODS.ap[:, o:o + 1]

    def der(k, j, c):
        o = (k * NJ + j) * 8 + c
        return DER.ap[:, o:o + 1]

    worder = list(order) if order is not None else []
    wpos = {k: i for i, k in enumerate(worder)}
    wstate = dict(done=0, seen=[])
    LOOK = 12

    def prepass_upto(k):
        while wstate["done"] < min(k, len(worder)):
            nm, m = worder[wstate["done"]]
            P.dma("pool", wscr[nm][m * 128:(m + 1) * 128, :], wsrc[nm][m * 128:(m + 1) * 128, :], writes=[wscrT[nm][m]])
            wstate["done"] += 1
    if order is None:
        for nm, nmc, ncol in wspec:
            if layers == 1 and nm in ("wqkv", "wout1", "wg1", "wu1", "wd1"):
                continue
            for m in range(nmc):
                P.dma("pool", wscr[nm][m * 128:(m + 1) * 128, :], wsrc[nm][m * 128:(m + 1) * 128, :], writes=[wscrT[nm][m]])
    P.dma("sp", smallT.ap, smallp, writes=[smallT])
    P.dma("sp", scT.ap, cT, writes=[scT])
    P.dma("pool", GW.ap, gw_in, writes=[GW])
    prepass_upto(LOOK)
    P.op("dve", lambda h: h.memset(cvec.ap[:, 0:1], 1.0), writes=[cvec])
    P.op("dve", lambda h: h.memset(cvec.ap[:, 1:2], EPS2), writes=[cvec])
    P.op("dve", lambda h: h.memset(cvec.ap[:, 2:3], 0.0), writes=[cvec])
    P.op("dve", lambda h: h.memset(onesD.ap, 1.0 / D), writes=[onesD])
    P.op("dve", lambda h: h.memset(onesB.ap, 1.0), writes=[onesB])
    ONE = cvec.ap[:, 0:1]
    EPSc = cvec.ap[:, 1:2]
    ACT(P, scT.ap, scT.ap, AF.Silu, [scT], [scT])
    ACT(P, CLAM.ap, sm[:, SOFF["rlam"]:SOFF["rlam"] + 8], AF.Exp, [smallT], [CLAM], scale=-1.0)
    TS(P, "dve", CLAM.ap, CLAM.ap, 1.0, None, ALU.add, None, [CLAM], [CLAM])
    ACT(P, CLAM.ap, CLAM.ap, AF.Ln, [CLAM], [CLAM])
    TS(P, "dve", CLAM.ap, CLAM.ap, -8.0, None, ALU.mult, None, [CLAM], [CLAM])
    mwb = uphase([(i * 2048, 1024, BF16) for i in range(6)])
    scTb = T(P.sbuf("scTb", [128, KC * NJ], BF16)[:])
    CP(P, "dve", scTb.ap, scT.ap, [scT], [scTb])
    sc3 = scTb.ap.rearrange("p (c j) -> p c j", c=KC)
    for l in range(layers):
        for q in range(48):
            wt = mwb[(l * 48 + q) % 6]
            r0 = (l * 48 + q) * 128
            P.dma("pool", wt.ap, modw[r0:r0 + 128, :], writes=[wt])
            ps = nb_()
            w3 = wt.ap.rearrange("p (k m) -> p k m", k=KC)
            for kc in range(KC):
                MM(P, ps, ps.ap[:, 0:NJ], w3[:, kc, :], sc3[:, kc, :], kc == 0, kc == KC - 1, [wt, scTb])
            o0 = l * NJ * 48 + q
            outap = AP(MODS.ap.tensor, MODS.ap[:, o0:o0 + 1].offset, [list(MODS.ap.ap[0]), [48, NJ]])
            TS(P, "dve", outap, ps.ap[:, 0:NJ], scol("modb", l * 48 + q), None, ALU.add, None, [ps, smallT], [MODS])
    for l in range(layers):
        for j in range(NJ):
            def M8(which):
                o = (l * NJ + j) * 48 + which * 8
                return MODS.ap[:, o:o + 8]

            def D8(k):
                o = (k * NJ + j) * 8
                return DER.ap[:, o:o + 8]
            ln1g = sm[:, SOFF["ln1g"] + l * 8: SOFF["ln1g"] + l * 8 + 8]
            ln1b = sm[:, SOFF["ln1b"] + l * 8: SOFF["ln1b"] + l * 8 + 8]
            TS(P, "dve", D8(0 + l), M8(1), 1.0, None, ALU.add, None, [MODS], [DER])
            TS(P, "dve", D8(2 + l), M8(2), 1.0 / ALPHA, None, ALU.mult, None, [MODS], [DER])
            TS(P, "dve", D8(4 + l), M8(5), 1.0 / ALPHA, None, ALU.mult, None, [MODS], [DER])
            TS(P, "dve", D8(8 + l), M8(4), 1.0, None, ALU.add, None, [MODS], [DER])
            TT(P, "pool", D8(6 + l), D8(8 + l), ln1g, ALU.mult, [DER, smallT], [DER])
            TT(P, "pool", D8(8 + l), D8(8 + l), ln1b, ALU.mult, [DER, smallT], [DER])
            TT(P, "dve", D8(8 + l), D8(8 + l), M8(3), ALU.add, [DER, MODS], [DER])
    if layers == 2:
        for j in range(NJ):
            ln2g = sm[:, SOFF["ln2g"]: SOFF["ln2g"] + 8]
            ln2b = sm[:, SOFF["ln2b"]: SOFF["ln2b"] + 8]
            A1n = DER.ap[:, (1 * NJ + j) * 8:(1 * NJ + j) * 8 + 8]
            SH1n = MODS.ap[:, (1 * NJ + j) * 48: (1 * NJ + j) * 48 + 8]
            g1p = DER.ap[:, (10 * NJ + j) * 8:(10 * NJ + j) * 8 + 8]
            b1p = DER.ap[:, (11 * NJ + j) * 8:(11 * NJ + j) * 8 + 8]
            TT(P, "pool", g1p, A1n, ln2g, ALU.mult, [DER, smallT], [DER])
            TT(P, "pool", b1p, A1n, ln2b, ALU.mult, [DER, smallT], [DER])
            TT(P, "dve", b1p, b1p, SH1n, ALU.add, [DER, MODS], [DER])

    def load_w(nm, m, ncol=1024, buf=None):
        if (nm, m) not in wstate["seen"]:
            wstate["seen"].append((nm, m))
        if order is not None:
            prepass_upto(wpos[(nm, m)] + 1 + LOOK)
        if buf is None:
            buf = WA[wa_i[0] % 3]
            wa_i[0] += 1
        P.dma("sp", buf.ap[:, 0:ncol], wscr[nm][m * 128:(m + 1) * 128, :], reads=[wscrT[nm][m]], writes=[buf])
        return buf

    def proj_fm(wt, rhs3, rhsT, c0, c1, nk=KC):
        ps = nb_()
        w3 = wt.ap[:, 0:nk * 128].rearrange("p (k m) -> p k m", k=nk)
        for kc in range(nk):
            MM(P, ps, ps.ap[:, 0:c1 - c0], w3[:, kc, :], rhs3[:, kc, c0:c1], kc == 0, kc == nk - 1, [wt] + rhsT)
        return ps

    def seqs_of(b, with_ctx):
        s = []
        if with_ctx:
            s.append((0, CT, NB))
        s.append((CT, NT, b))
        return s

    def layer_norm_piece(c0, c1, j, l, which, lnT, nextmod, hskip=0):
        n = c1 - c0
        SQ, MS, RS = lnT
        psm = nb_()
        pse = nb_()
        for mc in range(KC):
            sq = SQ[mc % 2]
            ACT(P, sq.ap[:, 0:n], X3[:, mc, c0:c1], AF.Square, xt(mc, c0, c1), [sq])
            MM(P, psm, psm.ap[:, 0:n], onesD.ap, X3[:, mc, c0:c1], mc == 0, mc == KC - 1, [onesD] + xt(mc, c0, c1))
            MM(P, pse, pse.ap[:, 0:n], onesD.ap, sq.ap[:, 0:n], mc == 0, mc == KC - 1, [onesD, sq])
        ACT(P, MS.ap[:, 0:n], psm.ap[:, 0:n], AF.Square, [psm], [MS])
        TT(P, "dve", MS.ap[:, 0:n], pse.ap[:, 0:n], MS.ap[:, 0:n], ALU.subtract, [pse, MS], [MS])
        ACT(P, MS.ap[:, 0:n], MS.ap[:, 0:n], AF.Sqrt, [MS, cvec], [MS], bias=EPSc, scale=1.0)
        P.op("dve", lambda h: h.reciprocal(RS.ap[:, 0:n], MS.ap[:, 0:n]), reads=[MS], writes=[RS])
        gname = "ln1g" if which == 1 else "ln2g"
        bname = "ln1b" if which == 1 else "ln2b"
        for mc in range(KC):
            xs = X3[:, mc, c0:c1]
            TT(P, "dve", xs, xs, psm.ap[:, 0:n], ALU.subtract, xt(mc, c0, c1) + [psm], xt(mc, c0, c1))
            TT(P, "pool", xs, xs, RS.ap[:, 0:n], ALU.mult, xt(mc, c0, c1) + [RS], xt(mc, c0, c1))
            if nextmod is not None:
                gk, bk = nextmod
                ACT(P, H3[:, mc, c0:c1 - hskip], X3[:, mc, c0:c1 - hskip], AF.Identity, xt(mc, c0, c1) + [DER], ht1(mc, c0, c1), bias=der(bk, j, mc), scale=der(gk, j, mc))
            ACT(P, xs, xs, AF.Identity, xt(mc, c0, c1) + [smallT], xt(mc, c0, c1), bias=scol(bname, l * 8 + mc), scale=scol(gname, l * 8 + mc))

    def proj_resid_ln(b, l, which, wname, nk, rhs3, rhsT, seqs, col_shift, lnT, nextmod, wbufs=None):
        gk = (2 if which == 1 else 4) + l
        for (lo, hi, j) in seqs:
            for (c0, c1) in pieces(lo, hi):
                for mc in range(KC):
                    if wbufs is None:
                        wt = load_w(wname, mc)
                    else:
                        wt = load_w(wname, mc, ncol=nk * 128, buf=wbufs[mc % 2])
                    ps = nb_()
                    w3 = wt.ap[:, 0:nk * 128].rearrange("p (k m) -> p k m", k=nk)
                    for kc in range(nk):
                        MM(P, ps, ps.ap[:, 0:c1 - c0], w3[:, kc, :], rhs3[:, kc, c0 - col_shift:c1 - col_shift],
                           kc == 0, kc == nk - 1, [wt] + rhsT)
                    xs = X3[:, mc, c0:c1]
                    STT(P, "dve", xs, ps.ap[:, 0:c1 - c0], der(gk, j, mc), xs, ALU.mult, ALU.add, [ps, DER] + xt(mc, c0, c1), xt(mc, c0, c1))
                layer_norm_piece(c0, c1, j, l, which, lnT, nextmod)

    def conv_seq(dst, src, dstT, srcT, lo, hi, dlo, slo, wname, wbase, K, padl, bias_ap, seq_lo, seq_hi, eng="dve"):
        ctr = padl
        TS(P, eng, dst[:, lo - dlo:hi - dlo], src[:, lo - slo:hi - slo], scol(wname, wbase + ctr), bias_ap, ALU.mult, ALU.add,
           [srcT, smallT], [dstT])
        for k in range(K):
            if k == ctr:
                continue
            o = k - padl
            t0 = max(lo, seq_lo - o)
            t1 = min(hi, seq_hi - o)
            if t1 <= t0:
                continue
            STT(P, eng, dst[:, t0 - dlo:t1 - dlo], src[:, t0 + o - slo:t1 + o - slo], scol(wname, wbase + k), dst[:, t0 - dlo:t1 - dlo],
                ALU.mult, ALU.add, [srcT, smallT, dstT], [dstT])

    def ffn_stage(b, l, seqs_super, nextmod):
        specs = [(fc * 2048, 1024, BF16) for fc in range(FC)]
        specs += [(45056, 1056, F32), (49280, 1024, F32), (53376, DFF, BF16), (59008, DFF, BF16)]
        specs += [(64640 + i * 2048, 512, F32) for i in range(4)]
        ts = uphase(specs)
        A = ts[0:FC]
        G, CV = ts[FC], ts[FC + 1]
        WD = ts[FC + 2:FC + 4]
        lnT = (ts[FC + 4:FC + 6], ts[FC + 6], ts[FC + 7])
        wg, wu, wd = "wg%d" % l, "wu%d" % l, "wd%d" % l
        patch = []
        for (lo, hi, j, seq_lo, seq_hi) in seqs_super:
            glo = max(seq_lo, lo - 1)
            ghi = min(seq_hi, hi + 1)
            for fc in range(FC):
                wtg = load_w(wg, fc)
                for (c0, c1) in pieces(glo, ghi):
                    ps = proj_fm(wtg, H3, ht(c0, c1), c0, c1)
                    CP(P, "act", G.ap[:, c0 - glo:c1 - glo], ps.ap[:, 0:c1 - c0], [ps], [G])
                conv_seq(CV.ap, G.ap, CV, G, lo, hi, lo, glo, "fcw", (l * FC + fc) * 3, 3, 1, scol("fcb", l * FC + fc), seq_lo, seq_hi)
                ACT(P, CV.ap[:, 0:hi - lo], CV.ap[:, 0:hi - lo], AF.Silu, [CV], [CV])
                wtu = load_w(wu, fc)
                for (c0, c1) in pieces(lo, hi):
                    ps = proj_fm(wtu, H3, ht(c0, c1), c0, c1)
                    TT(P, "dve", A[fc].ap[:, c0 - lo:c1 - lo], ps.ap[:, 0:c1 - c0], CV.ap[:, c0 - lo:c1 - lo], ALU.mult, [ps, CV], [A[fc]])
            A3 = AP(A[0].ap.tensor, A[0].ap.offset, [list(A[0].ap.ap[0]), [1024, FC], [1, 1024]])
            gk = 4 + l
            for (c0, c1) in pieces(lo, hi):
                for mc in range(KC):
                    wt = load_w(wd, mc, ncol=DFF, buf=WD[mc % 2])
                    ps = nb_()
                    w3 = wt.ap.rearrange("p (k m) -> p k m", k=FC)
                    for kc in range(FC):
                        MM(P, ps, ps.ap[:, 0:c1 - c0], w3[:, kc, :], A3[:, kc, c0 - lo:c1 - lo], kc == 0, kc == FC - 1, [wt, A[kc]])
                    xs = X3[:, mc, c0:c1]
                    STT(P, "dve", xs, ps.ap[:, 0:c1 - c0], der(gk, j, mc), xs, ALU.mult, ALU.add, [ps, DER] + xt(mc, c0, c1), xt(mc, c0, c1))
                more = any((s2[3] == seq_lo and s2[0] == hi) for s2 in seqs_super)
                hs = 1 if (nextmod is not None and c1 == hi and more) else 0
                layer_norm_piece(c0, c1, j, l, 2, lnT, nextmod, hskip=hs)
                if l == 1:
                    for mc in range(KC):
                        P.dma("sp" if mc % 2 == 0 else "pool", yT[b, mc][:, c0 - CT:c1 - CT], X3[:, mc, c0:c1], reads=xt(mc, c0, c1))
                if hs:
                    patch.append((hi - 1, j))
        if patch:
            patch_h(patch, l + 1)

    def patch_h(patch, lnext):
        for (col, j) in patch:
            for mc in range(KC):
                ACT(P, H3[:, mc, col:col + 1], X3[:, mc, col:col + 1], AF.Identity, xt(mc, col, col + 1) + [DER, MODS], ht1(mc, col, col + 1),
                    bias=mods(lnext, j, mc), scale=der(0 + lnext, j, mc))

    def even_mixer(b):
        specs = [(c * 4608, NT, BF16) for c in range(KC)]
        specs += [(36864 + i * 9216, NT, F32) for i in range(3)]
        specs += [(64512, NT, BF16)]
        specs += [(69120 + i * 2048, 512, F32) for i in range(4)]
        ts = uphase(specs)
        YC = ts[0:KC]
        setA = (ts[KC], ts[KC + 1], ts[KC + 2], ts[KC + 3], ts[KC + 4:KC + 7], sttA)
        TM = ts[KC + 4:KC + 8]
        def xcarve(off, n, dt):
            if dt == F32:
                return Xs[:, off // 4: off // 4 + n]
            return Xs[:, off // 4: off // 4 + (n + 1) // 2].bitcast(BF16)[:, :n]
        xs_specs = [(i * 9216, NT, F32) for i in range(3)] + [(27648, NT, BF16)] + [(32256 + i * 2048, 512, F32) for i in range(2)]
        xs_specs += [(36352, NT, F32), (45568, NT, F32)]
        xsT = [T(xcarve(o, n, dt)) for (o, n, dt) in xs_specs]
        borrowed = [t for c in range(6) for t in XR[c]]
        P.realias(xsT, borrowed)
        setB = (xsT[0], xsT[1], xsT[2], xsT[3], xsT[4:6], sttB)
        S0, S1 = xsT[6], xsT[7]
        seqs = seqs_of(b, True)
        allp = [pc for (lo, hi, j) in seqs for pc in pieces(lo, hi)]

        def rec_chunk(c, bufs, wbuf):
            R0, R1, R2, XCb, TMs, st = bufs
            tm_i = [0]

            def ntmp():
                t = TMs[tm_i[0] % len(TMs)]
                tm_i[0] += 1
                return t
            wt = load_w("win", c, buf=wbuf)
            for (c0, c1) in allp:
                ps = proj_fm(wt, H3, ht(c0, c1), c0, c1)
                CP(P, "act", R0.ap[:, c0:c1], ps.ap[:, 0:c1 - c0], [ps], [R0])
                yield
            for (lo, hi, j) in seqs:
                conv_seq(R1.ap, R0.ap, R1, R0, lo, hi, 0, 0, "rcw", c * 4, 4, 2, scol("rcb", c), lo, hi)
            yield
            CP(P, "pool", XCb.ap, R1.ap, [R1], [XCb])
            yield
            for d in range(2):
                Bd = R2 if d == 0 else R1
                for (c0, c1) in allp:
                    n = c1 - c0
                    psr = nb_()
                    MM(P, psr, psr.ap[:, 0:n], GW.ap[:, ((d * 2 + 0) * 4 + c) * 128:((d * 2 + 0) * 4 + c + 1) * 128], XCb.ap[:, c0:c1], True, True, [GW, XCb])
                    psi = nb_()
                    MM(P, psi, psi.ap[:, 0:n], GW.ap[:, ((d * 2 + 1) * 4 + c) * 128:((d * 2 + 1) * 4 + c + 1) * 128], XCb.ap[:, c0:c1], True, True, [GW, XCb])
                    ACT(P, R0.ap[:, c0:c1], psr.ap[:, 0:n], AF.Sigmoid, [psr, smallT], [R0], bias=scol("rba", d * 4 + c))
                    t1 = ntmp()
                    ACT(P, t1.ap[:, 0:n], psi.ap[:, 0:n], AF.Sigmoid, [psi, smallT], [t1], bias=scol("rbx", d * 4 + c))
                    if d == 0:
                        TT(P, "dve", Bd.ap[:, c0:c1], t1.ap[:, 0:n], R1.ap[:, c0:c1], ALU.mult, [t1, R1], [Bd])
                    else:
                        TT(P, "dve", Bd.ap[:, c0:c1], R1.ap[:, c0:c1], t1.ap[:, 0:n], ALU.mult, [t1, R1], [Bd])
                    yield
                ACT(P, R0.ap, R0.ap, AF.Exp, [R0, CLAM], [R0], scale=CLAM.ap[:, d * 4 + c:d * 4 + c + 1])
                yield
                for (c0, c1) in allp:
                    n = c1 - c0
                    a_ = R0.ap[:, c0:c1]
                    t2 = ntmp()
                    TT(P, "pool", t2.ap[:, 0:n], a_, a_, ALU.mult, [R0], [t2])
                    ACT(P, t2.ap[:, 0:n], t2.ap[:, 0:n], AF.Sqrt, [t2, cvec], [t2], bias=ONE, scale=-1.0)
                    TT(P, "dve", Bd.ap[:, c0:c1], Bd.ap[:, c0:c1], t2.ap[:, 0:n], ALU.mult, [Bd, t2], [Bd])
                    yield
                if d == 0:
                    SCAN(P, Bd.ap[:, 0:CT], R0.ap[:, 0:CT], Bd.ap[:, 0:CT], 0.0, [R0, Bd], [Bd])
                    CP(P, "act", st.ap[:, 0:1], Bd.ap[:, CT - 1:CT], [Bd], [st])
                    yield
                    SCAN(P, Bd.ap[:, CT:NT], R0.ap[:, CT:NT], Bd.ap[:, CT:NT], st.ap[:, 0:1], [R0, Bd, st], [Bd])
                else:
                    SCAN(P, rev(Bd.ap[:, 0:CT]), rev(R0.ap[:, 0:CT]), rev(Bd.ap[:, 0:CT]), 0.0, [R0, Bd], [Bd])
                    CP(P, "act", st.ap[:, 1:2], Bd.ap[:, 0:1], [Bd], [st])
                    yield
                    SCAN(P, rev(Bd.ap[:, CT:NT]), rev(R0.ap[:, CT:NT]), rev(Bd.ap[:, CT:NT]), st.ap[:, 1:2], [R0, Bd, st], [Bd])
                yield
            TT(P, "pool", R2.ap, R2.ap, R1.ap, ALU.add, [R2, R1], [R2])
            yield
            wt = load_w("win", 4 + c, buf=wbuf)
            for (c0, c1) in allp:
                n = c1 - c0
                ps = proj_fm(wt, H3, ht(c0, c1), c0, c1)
                t1 = ntmp()
                ACT(P, t1.ap[:, 0:n], ps.ap[:, 0:n], AF.Gelu_apprx_tanh, [ps], [t1])
                TT(P, "dve", YC[c].ap[:, c0:c1], t1.ap[:, 0:n], R2.ap[:, c0:c1], ALU.mult, [t1, R2], [YC[c]])
                yield

        def sc_chunk(c, wbuf):
            wt = load_w("win", 12 + c, buf=wbuf)
            for (c0, c1) in allp:
                ps = proj_fm(wt, H3, ht(c0, c1), c0, c1)
                CP(P, "act", S0.ap[:, c0:c1], ps.ap[:, 0:c1 - c0], [ps], [S0])
                yield
            wt = load_w("win", 16 + c, buf=wbuf)
            for (c0, c1) in allp:
                ps = proj_fm(wt, H3, ht(c0, c1), c0, c1)
                TT(P, "dve", S0.ap[:, c0:c1], ps.ap[:, 0:c1 - c0], S0.ap[:, c0:c1], ALU.mult, [ps, S0], [S0])
                yield
            for (lo, hi, j) in seqs:
                conv_seq(S1.ap, S0.ap, S1, S0, lo, hi, 0, 0, "scw", c * 3, 3, 1, scol("scb", c), lo, hi)
            yield
            wt = load_w("win", 8 + c, buf=wbuf)
            for (c0, c1) in allp:
                ps = proj_fm(wt, H3, ht(c0, c1), c0, c1)
                TT(P, "dve", YC[4 + c].ap[:, c0:c1], ps.ap[:, 0:c1 - c0], S1.ap[:, c0:c1], ALU.mult, [ps, S1], [YC[4 + c]])
                yield

        def chain(gs):
            for g in gs:
                yield from g
        gens = [chain([rec_chunk(0, setA, WA[0]), rec_chunk(2, setA, WA[0])]),
                chain([rec_chunk(1, setB, WA[1]), rec_chunk(3, setB, WA[1])]),
                chain([sc_chunk(c, WA[2]) for c in range(4)])]
        while gens:
            for g in list(gens):
                try:
                    next(g)
                except StopIteration:
                    gens.remove(g)
        P.realias(borrowed, xsT)
        for r, (lo, hi) in enumerate(REG):
            for c in range(6):
                q = "sp" if c % 2 == 0 else "pool"
                if r == 0:
                    P.dma(q, X3[:, c, 0:CT], ctxT[b, c], writes=[XR[c][0]])
                else:
                    P.dma(q, X3[:, c, lo:hi], xT[b, c][:, lo - CT:hi - CT], writes=[XR[c][r]])
        YC3 = AP(YC[0].ap.tensor, YC[0].ap.offset, [list(YC[0].ap.ap[0]), [NT, KC], [1, NT]])
        lnT = ([TM[0], TM[1]], TM[2], TM[3])
        proj_resid_ln(b, 0, 1, "wout0", KC, YC3, YC, seqs, 0, lnT, (6, 8))

    def odd_mixer(b):
        specs = [(c * 4096, SEQ, BF16) for c in range(KC)]
        specs += [(32768, SEQ, BF16), (36864, SEQ, BF16), (40960, NT, BF16), (45568, 18 * 192, BF16)]
        specs += [(52480 + i * 3584, 896, F32) for i in range(4)]
        specs += [(66816 + i * 2048, 512, F32) for i in range(2)]
        specs += [(70912 + i * 1024, 512, BF16) for i in range(8)]
        specs += [(79104 + i * 1024, 256, F32) for i in range(2)]
        ts = uphase(specs)
        OT = ts[0:KC]
        QZ0, QZ1, KT, V = ts[KC:KC + 4]
        TB = ts[KC + 4:KC + 8]
        E = ts[KC + 8:KC + 10]
        PT = ts[KC + 10:KC + 18]
        RC = [ts[KC + 18], ts[KC + 18]]
        OS = ts[KC + 19]
        P.op("pool", lambda h: h.memset(QZ0.ap[64:128, :], 0.0), writes=[QZ0])
        P.op("pool", lambda h: h.memset(QZ1.ap[0:64, :], 0.0), writes=[QZ1])
        V3 = V.ap.rearrange("p (t m) -> p t m", t=18)
        P.op("pool", lambda h: h.memset(V3[:, :, 64:128], 1.0), writes=[V])
        cnt = dict(e=0, p=0, r=0)
        j = b
        for hp in range(KC):
            wq = load_w("wqkv", hp)
            for (c0, c1) in pieces(CT, NT):
                ps = proj_fm(wq, H3, ht(c0, c1), c0, c1)
                CP(P, "act", QZ0.ap[0:64, c0 - CT:c1 - CT], ps.ap[0:64, 0:c1 - c0], [ps], [QZ0])
                CP(P, "dve", QZ1.ap[64:128, c0 - CT:c1 - CT], ps.ap[64:128, 0:c1 - c0], [ps], [QZ1])
            wk = load_w("wqkv", 8 + hp)
            for (c0, c1) in pieces(0, NT):
                ps = proj_fm(wk, H3, ht(c0, c1), c0, c1)
                CP(P, "dve", KT.ap[:, c0:c1], ps.ap[:, 0:c1 - c0], [ps], [KT])
            wv = load_w("wqkv", 16 + hp)
            wv3 = wv.ap.rearrange("p (k m) -> p k m", k=KC)
            for t0 in range(0, 18, 4):
                nt_ = min(4, 18 - t0)
                ps = nb_()
                for ti in range(nt_):
                    tt = t0 + ti
                    for kc in range(KC):
                        MM(P, ps, ps.ap[:, ti * 128:(ti + 1) * 128], H3[:, kc, tt * 128:(tt + 1) * 128], wv3[:, kc, :],
                           kc == 0, kc == KC - 1, [wv] + ht(tt * 128, (tt + 1) * 128))
                ps3 = ps.ap[:, 0:nt_ * 128].rearrange("p (t m) -> p t m", t=nt_)
                CP(P, "act", V3[:, t0:t0 + nt_, 0:64], ps3[:, :, 0:64], [ps], [V])
                CP(P, "dve", V3[:, t0:t0 + nt_, 128:192], ps3[:, :, 64:128], [ps], [V])
            for par in range(2):
                h_ = 2 * hp + par
                P.dma("sp", TB[par * 2].ap, tmid[h_ * 128:(h_ + 1) * 128, :], writes=[TB[par * 2]])
                P.dma("sp", TB[par * 2 + 1].ap, tfull[h_ * 128:(h_ + 1) * 128, :], writes=[TB[par * 2 + 1]])
                ACT(P, TB[par * 2].ap, TB[par * 2].ap, AF.Exp, [TB[par * 2]], [TB[par * 2]])
                ACT(P, TB[par * 2 + 1].ap, TB[par * 2 + 1].ap, AF.Exp, [TB[par * 2 + 1]], [TB[par * 2 + 1]])
            items = [(par, r0) for par in range(2) for r0 in range(0, 32, 4)]

            def s_phase(par, r0):
                pb = par * 64
                q0 = r0 * 64
                if r0 == 0:
                    chunks, kind = [0, 1, 2, 3], 1
                elif r0 == 28:
                    chunks, kind = [12, 13, 14, 15], 1
                else:
                    a0 = (r0 - 4) // 2
                    chunks, kind = list(range(a0, a0 + 6)), 0
                tb = TB[par * 2 + kind]
                QZ = QZ0 if par == 0 else QZ1
                pts = []
                for i in range(0, len(chunks), 2):
                    ps = nb_()
                    for k in range(2):
                        a = chunks[i + k]
                        MM(P, ps, ps.ap[:, k * 256:(k + 1) * 256], KT.ap[:, CT + a * 128:CT + (a + 1) * 128],
                           QZ.ap[:, q0:q0 + 256], True, True, [KT, QZ])
                    e = E[cnt["e"] % 2]
                    cnt["e"] += 1
                    ACT(P, e.ap, ps.ap, AF.Exp, [ps], [e], scale=0.125)
                    pt = PT[cnt["p"] % 8]
                    cnt["p"] += 1
                    ei0 = r0 - 2 * chunks[i] + 6
                    tap = AP(tb.ap.tensor, tb.ap[:, ei0 * 64:ei0 * 64 + 1].offset, [list(tb.ap.ap[0]), [-128, 2], [1, 256]])
                    TT(P, "pool" if (i // 2) == 1 else "dve", pt.ap.rearrange("p (a b) -> p a b", a=2), e.ap.rearrange("p (a b) -> p a b", a=2), tap, ALU.mult, [e, tb], [pt])
                    pts.append((pt, chunks[i], chunks[i + 1]))
                ps = nb_()
                for k in range(2):
                    MM(P, ps, ps.ap[:, k * 256:(k + 1) * 256], KT.ap[:, k * 128:(k + 1) * 128],
                       QZ.ap[:, q0:q0 + 256], True, True, [KT, QZ])
                ptc = PT[cnt["p"] % 8]
                cnt["p"] += 1
                ACT(P, ptc.ap, ps.ap, AF.Exp, [ps], [ptc], scale=0.125)
                return (par, r0, pts, ptc)

            def pv_phase(st):
                par, r0, pts, ptc = st
                pb = par * 64
                q0 = r0 * 64
                pso = nb_()
                ob = 64 - pb
                mms = []
                for (pt, a, a2) in pts:
                    mms.append((V3[:, 2 + a, pb:pb + 128], pt.ap[:, 0:256], pt))
                    mms.append((V3[:, 2 + a2, pb:pb + 128], pt.ap[:, 256:512], pt))
                mms.append((V3[:, 0, pb:pb + 128], ptc.ap[:, 0:256], ptc))
                mms.append((V3[:, 1, pb:pb + 128], ptc.ap[:, 256:512], ptc))
                nm_ = len(mms)
                for i, (l_, r_, pt) in enumerate(mms):
                    MM(P, pso, pso.ap[:, 0:256], l_, r_, i == 0, i == nm_ - 1, [V, pt])
                rl, rc = OS, RC[0]
                ACT(P, rl.ap[pb:pb + 64, :], pso.ap[ob:ob + 64, 0:256], AF.Ln, [pso], [rl])
                CP(P, "dve", rc.ap[pb:pb + 64, :], rl.ap[pb:pb + 64, :], [rl], [rc])
                ACT(P, rc.ap[pb:pb + 64, :], rc.ap[pb:pb + 64, :], AF.Exp, [rc], [rc], scale=-1.0)
                TT(P, "dve", OT[hp].ap[pb:pb + 64, q0:q0 + 256], pso.ap[pb:pb + 64, 0:256], rc.ap[pb:pb + 64, :], ALU.mult, [pso, rc], [OT[hp]])

            prev = None
            for (par, r0) in items:
                st = s_phase(par, r0)
                if prev is not None:
                    pv_phase(prev)
                prev = st
            pv_phase(prev)
        OT3 = AP(OT[0].ap.tensor, OT[0].ap.offset, [list(OT[0].ap.ap[0]), [SEQ, KC], [1, SEQ]])
        lnA, lnB = T(ucarve(70912, 512, F32)), T(ucarve(72960, 512, F32))
        lnT = ([E[0], E[1]], lnA, lnB)
        P.realias([lnA, lnB], PT)
        u_live.extend([lnA, lnB])
        proj_resid_ln(b, 1, 1, "wout1", KC, OT3, OT, seqs_of(b, False), CT, lnT, (6 + 1, 8 + 1))

    for b in range(NB):
        for c in range(KC):
            q = "sp" if c % 2 == 0 else "pool"
            P.dma(q, X3[:, c, 0:CT], ctxT[b, c], writes=[XR[c][0]])
            P.dma(q, X3[:, c, CT:CT + 1024], xT[b, c][:, 0:1024], writes=[XR[c][1]])
            P.dma(q, X3[:, c, CT + 1024:NT], xT[b, c][:, 1024:2048], writes=[XR[c][2]])
        for c in range(KC):
            for r, (lo, hi) in enumerate(REG):
                j = NB if r == 0 else b
                ACT(P, H3[:, c, lo:hi], X3[:, c, lo:hi], AF.Identity, [XR[c][r], DER, MODS], [HR[c][r]], bias=mods(0, j, c), scale=der(0, j, c))
        if stop != "load":
            even_mixer(b)
        nm0 = (10, 11) if layers == 2 else None
        if stop is None:
            ffn_stage(b, 0, [(0, CT, NB, 0, CT), (CT, CT + 1024, b, CT, NT), (CT + 1024, NT, b, CT, NT)], nm0)

        if layers == 2:
            odd_mixer(b)
            ffn_stage(b, 1, [(CT, CT + 1024, b, CT, NT), (CT + 1024, NT, b, CT, NT)], None)
        if not (layers == 2 and stop is None):
            for c in range(KC):
                q = "sp" if c % 2 == 0 else "pool"
                P.dma(q, yT[b, c], X3[:, c, CT:NT], reads=[XR[c][1], XR[c][2]])
    allx = [t for c in range(KC) for t in XR[c]]
    P.wait_all("sp", allx)
    P.worder = wstate["seen"]
    P.finish()
    return nc, P


def _wt(W):
    K, M = W.shape
    return np.ascontiguousarray(W.reshape(K // 128, 128, M // 128, 128).transpose(2, 1, 0, 3)).reshape(M, K)


def _pc(v, nchunk):
    v = np.asarray(v, np.float32)
    lead = v.shape[:-1]
    return np.moveaxis(v.reshape(lead + (nchunk, 128)), -1, 0)


def _tables(rpb):
    kr2 = np.arange(2)[:, None, None, None]
    kcol = np.arange(64)[None, :, None, None]
    e = (np.arange(14) - 6)[None, None, :, None]
    qcol = np.arange(64)[None, None, None, :]
    dr = kr2 - e + 7 + 0 * kcol + 0 * qcol
    dc = kcol - qcol + 15 + 0 * kr2 + 0 * e
    cs = np.clip(qcol - 8, 0, 48)
    colok = (kcol >= cs) & (kcol < cs + 16)
    drok = (dr >= 0) & (dr <= 14)
    rowmid = (kr2 - e >= -4) & (kr2 - e <= 3)
    g = rpb[:, np.clip(dr, 0, 14), np.clip(dc, 0, 30)]
    negs = np.full(g.shape, NEG, np.float32)
    tfull = np.where((colok & drok)[None], g, negs).reshape(16 * 128, 896)
    tmid = np.where((colok & drok & rowmid)[None], g, negs).reshape(16 * 128, 896)
    return np.ascontiguousarray(tmid, np.float32), np.ascontiguousarray(tfull, np.float32)


def prep_shared(mod_w, mod_b, ln1_g, ln1_b, ln2_g, ln2_b, ev_w_in, ev_w_out, rec_conv_w, rec_conv_b, rec_wa, rec_ba,
                rec_wx, rec_bx, rec_lam, sc_conv_w, sc_conv_b, na_w_qkv, na_w_out, na_rpb, ffn_w_gate, ffn_w_up,
                ffn_conv_w, ffn_conv_b, ffn_w_down):
    sh = {}
    sh["win"] = _wt(ev_w_in[0])
    sh["wout0"] = _wt(ev_w_out[0])
    sh["wqkv"] = _wt(na_w_qkv[0])
    sh["wout1"] = _wt(na_w_out[0])
    for l in range(2):
        sh["wg%d" % l] = _wt(ffn_w_gate[l])
        sh["wu%d" % l] = _wt(ffn_w_up[l])
        sh["wd%d" % l] = _wt(ffn_w_down[l])
    sh["modw"] = np.concatenate([_wt(mod_w[l]) for l in range(2)], axis=0)
    small = np.zeros((128, NS), np.float32)

    def put(nm, arr):
        arr = np.asarray(arr, np.float32).reshape(128, -1)
        small[:, SOFF[nm]:SOFF[nm] + arr.shape[1]] = arr
    put("modb", _pc(mod_b, 48))
    put("ln1g", _pc(ln1_g, 8)); put("ln1b", _pc(ln1_b, 8)); put("ln2g", _pc(ln2_g, 8)); put("ln2b", _pc(ln2_b, 8))
    put("rcw", np.transpose(_pc(rec_conv_w[0], 4), (0, 2, 1)))
    put("rcb", _pc(rec_conv_b[0], 4))
    put("rba", _pc(rec_ba[0], 4)); put("rbx", _pc(rec_bx[0], 4)); put("rlam", _pc(rec_lam[0], 4))
    put("scw", np.transpose(_pc(sc_conv_w[0], 4), (0, 2, 1)))
    put("scb", _pc(sc_conv_b[0], 4))
    put("fcw", np.transpose(_pc(ffn_conv_w, FC), (0, 1, 3, 2)))
    put("fcb", _pc(ffn_conv_b, FC))
    sh["smallp"] = small
    gw = np.zeros((128, 16, 128), np.float32)
    for d in range(2):
        for wi, W in enumerate((rec_wa[0], rec_wx[0])):
            for c in range(4):
                g = (d * 2 + wi) * 4 + c
                gw[0:64, g, 0:64] = W[d, 2 * c]
                gw[64:128, g, 64:128] = W[d, 2 * c + 1]
    sh["gw"] = gw.reshape(128, 16 * 128)
    sh["tmid"], sh["tfull"] = _tables(np.asarray(na_rpb[0], np.float32))
    return sh


def prep_core(x, c, ctx, c_ctx, b0, NB):
    m = {}
    m["xT"] = np.ascontiguousarray(x[b0:b0 + NB].transpose(0, 2, 1)).reshape(NB, KC, 128, SEQ)
    m["ctxT"] = np.ascontiguousarray(ctx[b0:b0 + NB].transpose(0, 2, 1)).reshape(NB, KC, 128, CT)
    cc = np.concatenate([c[b0:b0 + NB], c_ctx[None, :]], axis=0)
    m["cT"] = np.ascontiguousarray(np.transpose(cc.reshape(NB + 1, KC, 128), (2, 1, 0))).reshape(128, KC * (NB + 1))
    return m


_CACHE = {}


def kernel(x, c, ctx, c_ctx, mod_w, mod_b, ln1_g, ln1_b, ln2_g, ln2_b, ev_w_in, ev_w_out, rec_conv_w, rec_conv_b,
           rec_wa, rec_ba, rec_wx, rec_bx, rec_lam, sc_conv_w, sc_conv_b, na_w_qkv, na_w_out, na_rpb, ffn_w_gate,
           ffn_w_up, ffn_conv_w, ffn_conv_b, ffn_w_down):
    f = lambda a: np.asarray(a, np.float32)
    x, c, ctx, c_ctx = f(x), f(c), f(ctx), f(c_ctx)
    B = x.shape[0]
    NB = B // NCORES
    sh = prep_shared(f(mod_w), f(mod_b), f(ln1_g), f(ln1_b), f(ln2_g), f(ln2_b), f(ev_w_in), f(ev_w_out), f(rec_conv_w),
                     f(rec_conv_b), f(rec_wa), f(rec_ba), f(rec_wx), f(rec_bx), f(rec_lam), f(sc_conv_w), f(sc_conv_b),
                     f(na_w_qkv), f(na_w_out), f(na_rpb), f(ffn_w_gate), f(ffn_w_up), f(ffn_conv_w), f(ffn_conv_b), f(ffn_w_down))
    if NB not in _CACHE:
        _, p0 = build_program(1)
        _CACHE[NB] = build_program(NB, order=p0.worder)[0]
    nc = _CACHE[NB]
    in_maps = []
    for i in range(NCORES):
        m = dict(sh)
        m.update(prep_core(x, c, ctx, c_ctx, i * NB, NB))
        in_maps.append(m)
    res = run_bass_kernel_spmd(nc, in_maps, core_ids=list(range(NCORES)))
    out = np.empty((B, SEQ, D), np.float32)
    for i in range(NCORES):
        y = np.asarray(res.results[i]["yT"]).reshape(NB, D, SEQ)
        out[i * NB:(i + 1) * NB] = y.transpose(0, 2, 1)
    return out
```

```python
import numpy as np
import concourse.bass as bass
import concourse.mybir as mybir
from concourse.bass_utils import run_bass_kernel_spmd
from concourse.ap import AP

F32 = mybir.dt.float32
BF16 = mybir.dt.bfloat16
ALU = mybir.AluOpType
AF = mybir.ActivationFunctionType

NCORES = 8
D = 1024
KC = 8
SEQ = 2048
CT = 256
NT = SEQ + CT
DFF = 2816
FC = 22
ALPHA = 4.0 ** 0.25
EPS2 = 1e-6 / (ALPHA * ALPHA)
NEG = -30000.0


class T:
    __slots__ = ("ap", "w", "r")

    def __init__(self, ap):
        self.ap = ap
        self.w = None
        self.r = {}


class Eng:
    def __init__(self, name, h, sem):
        self.name = name
        self.h = h
        self.sem = sem
        self.count = 0
        self.waited = {}
        self.q = []


class Prog:
    def __init__(self, nc, n_dma_sems=8):
        self.nc = nc
        self._ctx = []
        self.engs = {}
        for nm, h in (("pe", nc.tensor), ("act", nc.scalar), ("dve", nc.vector), ("pool", nc.gpsimd), ("sp", nc.sync)):
            self.engs[nm] = Eng(nm, h, self._sem("e_" + nm))
        self.n_dma_sems = n_dma_sems
        self.dma_pool = {}
        for nm in ("sp", "pool"):
            self.dma_pool[nm] = dict(sems=[self._sem("d_%s%d" % (nm, i)) for i in range(n_dma_sems)], j=0)
        self.ninst = 0
        self.force_own = False

    def _sem(self, name):
        cm = self.nc.semaphore(name)
        s = cm.__enter__()
        self._ctx.append(cm)
        return (name, s)

    def sbuf(self, name, shape, dt):
        cm = self.nc.sbuf_tensor(name, shape, dt)
        t = cm.__enter__()
        self._ctx.append(cm)
        return t

    def psum(self, name, shape, dt):
        cm = self.nc.psum_tensor(name, shape, dt)
        t = cm.__enter__()
        self._ctx.append(cm)
        return t

    @staticmethod
    def _deps(reads, writes):
        deps = {}

        def add(k, v):
            if deps.get(k, 0) < v:
                deps[k] = v
        for t in reads:
            if t.w is not None:
                add(*t.w)
        for t in writes:
            if t.w is not None:
                add(*t.w)
            for k, v in t.r.items():
                add(k, v)
        return deps

    def _waits(self, e, deps, own_too):
        for (name, sh), v in deps.items():
            if (not own_too) and name == e.sem[0]:
                continue
            if e.waited.get(name, 0) >= v:
                continue
            e.waited[name] = v
            e.q.append(lambda h, sh=sh, v=v: h.wait_ge(sh, v))
            self.ninst += 1

    def op(self, eng, fn, reads=(), writes=(), sig=True, own=False):
        e = self.engs[eng]
        self._waits(e, self._deps(reads, writes), own or self.force_own)
        if sig:
            e.count += 1
            val = e.count
            sh = e.sem[1]
            e.q.append(lambda h, fn=fn, sh=sh: fn(h).then_inc(sh, 1))
        else:
            val = e.count + 1
            e.q.append(lambda h, fn=fn: fn(h))
        self.ninst += 1
        for t in reads:
            if t.r.get(e.sem, 0) < val:
                t.r[e.sem] = val
        for t in writes:
            t.w = (e.sem, val)
            t.r = {}

    def dma(self, q, out_ap, in_ap, reads=(), writes=()):
        e = self.engs[q]
        pool = self.dma_pool[q]
        j = pool["j"]
        pool["j"] += 1
        s = pool["sems"][j % self.n_dma_sems]
        gen = j // self.n_dma_sems
        deps = self._deps(reads, writes)
        if gen > 0 and deps.get(s, 0) < 16 * gen:
            deps[s] = 16 * gen
        self._waits(e, deps, True)
        val = 16 * (gen + 1)
        sh = s[1]
        e.q.append(lambda h, o=out_ap, i=in_ap, sh=sh: h.dma_start(out=o, in_=i).then_inc(sh, 16))
        self.ninst += 1
        for t in reads:
            if t.r.get(s, 0) < val:
                t.r[s] = val
        for t in writes:
            t.w = (s, val)
            t.r = {}

    def realias(self, new_ts, old_ts):
        merged = {}
        for t in old_ts:
            if t.w is not None and merged.get(t.w[0], 0) < t.w[1]:
                merged[t.w[0]] = t.w[1]
            for k, v in t.r.items():
                if merged.get(k, 0) < v:
                    merged[k] = v
        for t in new_ts:
            t.w = None
            t.r = dict(merged)

    def wait_all(self, eng, ts):
        e = self.engs[eng]
        self._waits(e, self._deps(ts, ts), True)

    def finish(self):
        nc = self.nc
        with nc.Block() as block:
            def mk(e):
                def f(h):
                    for c in e.q:
                        c(h)
                return f
            block.tensor(mk(self.engs["pe"]))
            block.scalar(mk(self.engs["act"]))
            block.vector(mk(self.engs["dve"]))
            block.gpsimd(mk(self.engs["pool"]))
            block.sync(mk(self.engs["sp"]))
        for cm in reversed(self._ctx):
            cm.__exit__(None, None, None)
        self._ctx = []


def MM(P, psT, out, lhsT, rhs, start, stop, reads):
    P.op("pe", lambda h: h.matmul(out, lhsT, rhs, start=start, stop=stop), reads=reads, writes=[psT], sig=True)


def ACT(P, out, in_, func, reads, writes, bias=None, scale=None):
    kw = {}
    if bias is not None:
        kw["bias"] = bias
    if scale is not None:
        kw["scale"] = scale
    P.op("act", lambda h: h.activation(out, in_, func, **kw), reads=reads, writes=writes)


def TT(P, eng, out, in0, in1, op, reads, writes):
    P.op(eng, lambda h: h.tensor_tensor(out, in0, in1, op), reads=reads, writes=writes)


def TS(P, eng, out, in0, s1, s2, op0, op1, reads, writes):
    if s2 is None:
        P.op(eng, lambda h: h.tensor_scalar(out, in0, s1, None, op0), reads=reads, writes=writes)
    else:
        P.op(eng, lambda h: h.tensor_scalar(out, in0, s1, s2, op0, op1), reads=reads, writes=writes)


def STT(P, eng, out, in0, sc, in1, op0, op1, reads, writes):
    P.op(eng, lambda h: h.scalar_tensor_tensor(out, in0, sc, in1, op0, op1), reads=reads, writes=writes)


def CP(P, eng, out, in_, reads, writes):
    if eng == "act":
        P.op("act", lambda h: h.copy(out, in_), reads=reads, writes=writes)
    else:
        P.op(eng, lambda h: h.tensor_copy(out, in_), reads=reads, writes=writes)


def SCAN(P, out, d0, d1, init, reads, writes):
    P.op("dve", lambda h: h.tensor_tensor_scan(out, d0, d1, init, ALU.mult, ALU.add), reads=reads, writes=writes)


def rev(ap2d):
    n = ap2d.shape[1]
    last = ap2d[:, n - 1:n]
    return AP(last.tensor, last.offset, [list(last.ap[0]), [-1, n]])


def pieces(lo, hi, step=512):
    n = -(-(hi - lo) // step)
    size = -(-(hi - lo) // n)
    size += size % 2
    out = []
    c = lo
    while c < hi:
        out.append((c, min(hi, c + size)))
        c += size
    return out


def small_layout():
    off = {}
    o = 0
    for nm, n in (("modb", 96), ("ln1g", 16), ("ln1b", 16), ("ln2g", 16), ("ln2b", 16), ("rcw", 16), ("rcb", 4),
                  ("rba", 8), ("rbx", 8), ("rlam", 8), ("scw", 12), ("scb", 4), ("fcw", 132), ("fcb", 44)):
        off[nm] = o
        o += n
    return off, o


SOFF, NS = small_layout()


def build_program(NB, layers=2, stop=None, order=None):
    nc = bass.Bass("TRN2", target_bir_lowering=False)
    NJ = NB + 1

    def din(name, shape, dt=F32):
        return nc.dram_tensor(name, list(shape), dt, kind="ExternalInput").ap()

    xT = din("xT", [NB, KC, 128, SEQ])
    ctxT = din("ctxT", [NB, KC, 128, CT])
    cT = din("cT", [128, KC * NJ])
    smallp = din("smallp", [128, NS])
    modw = din("modw", [2 * 48 * 128, 1024])
    gw_in = din("gw", [128, 16 * 128])
    tmid = din("tmid", [16 * 128, 896])
    tfull = din("tfull", [16 * 128, 896])
    wspec = [("win", 20, 1024), ("wout0", 8, 1024), ("wg0", FC, 1024), ("wu0", FC, 1024), ("wd0", 8, DFF),
             ("wqkv", 24, 1024), ("wout1", 8, 1024), ("wg1", FC, 1024), ("wu1", FC, 1024), ("wd1", 8, DFF)]
    wsrc = {}
    wscr = {}
    wscrT = {}
    for nm, nmc, ncol in wspec:
        wsrc[nm] = din(nm, [nmc * 128, ncol])
        wscr[nm] = nc.dram_tensor(nm + "_s", [nmc * 128, ncol], BF16, kind="Internal").ap()
        wscrT[nm] = [T(wscr[nm][m * 128:(m + 1) * 128, :]) for m in range(nmc)]
    yT = nc.dram_tensor("yT", [NB, KC, 128, SEQ], F32, kind="ExternalOutput").ap()

    P = Prog(nc)
    Xs = P.sbuf("X", [128, KC * NT], F32)
    Hs = P.sbuf("H", [128, KC * NT], BF16)
    UBYTES = 81920
    Us = P.sbuf("U", [128, UBYTES // 4], F32)
    X3 = Xs[:].rearrange("p (c t) -> p c t", c=KC)
    H3 = Hs[:].rearrange("p (c t) -> p c t", c=KC)
    REG = [(0, CT), (CT, CT + 1024), (CT + 1024, NT)]
    XR = [[T(X3[:, c, lo:hi]) for (lo, hi) in REG] for c in range(KC)]
    HR = [[T(H3[:, c, lo:hi]) for (lo, hi) in REG] for c in range(KC)]

    def xt(mc, c0, c1):
        return [XR[mc][r] for r, (lo, hi) in enumerate(REG) if c0 < hi and c1 > lo]

    def ht1(mc, c0, c1):
        return [HR[mc][r] for r, (lo, hi) in enumerate(REG) if c0 < hi and c1 > lo]

    def ht(c0, c1):
        return [t for mc in range(KC) for t in ht1(mc, c0, c1)]
    WA = [T(P.sbuf("wa%d" % i, [128, 1024], BF16)[:]) for i in range(3)]
    wa_i = [0]
    smallT = T(P.sbuf("smallp_sb", [128, NS], F32)[:])
    cvec = T(P.sbuf("cvec", [128, 4], F32)[:])
    sttA = T(P.sbuf("sttA", [128, 2], F32)[:])
    sttB = T(P.sbuf("sttB", [128, 2], F32)[:])
    onesD = T(P.sbuf("onesD", [128, 128], F32)[:])
    onesB = T(P.sbuf("onesB", [128, 128], BF16)[:])
    scT = T(P.sbuf("scT", [128, KC * NJ], F32)[:])
    MODS = T(P.sbuf("MODS", [128, 2 * NJ * 48], F32)[:])
    DER = T(P.sbuf("DER", [128, 12 * NJ * 8], F32)[:])
    CLAM = T(P.sbuf("CLAM", [128, 8], F32)[:])
    GW = T(P.sbuf("GW", [128, 16 * 128], BF16)[:])
    banks = [T(P.psum("ps%d" % i, [128, 512], F32)[:]) for i in range(8)]
    bank_i = [0]

    def nb_():
        t = banks[bank_i[0] % 8]
        bank_i[0] += 1
        return t

    def ucarve(off_bytes, n, dt):
        if dt == F32:
            return Us[:, off_bytes // 4: off_bytes // 4 + n]
        return Us[:, off_bytes // 4: off_bytes // 4 + (n + 1) // 2].bitcast(BF16)[:, :n]

    u_live = []

    def uphase(specs):
        new = [T(ucarve(o, n, dt)) for (o, n, dt) in specs]
        P.realias(new, u_live)
        u_live[:] = new
        return new

    sm = smallT.ap

    def scol(nm, i):
        o = SOFF[nm] + i
        return sm[:, o:o + 1]

    def mods(l, j, q):
        o = (l * NJ + j) * 48 + q
        return MODS.ap[:, o:o + 1]

    def der(k, j, c):
        o = (k * NJ + j) * 8 + c
        return DER.ap[:, o:o + 1]

    worder = list(order) if order is not None else []
    wpos = {k: i for i, k in enumerate(worder)}
    wstate = dict(done=0, seen=[])
    LOOK = 12

    def prepass_upto(k):
        while wstate["done"] < min(k, len(worder)):
            nm, m = worder[wstate["done"]]
            P.dma("pool", wscr[nm][m * 128:(m + 1) * 128, :], wsrc[nm][m * 128:(m + 1) * 128, :], writes=[wscrT[nm][m]])
            wstate["done"] += 1
    if order is None:
        for nm, nmc, ncol in wspec:
            if layers == 1 and nm in ("wqkv", "wout1", "wg1", "wu1", "wd1"):
                continue
            for m in range(nmc):
                P.dma("pool", wscr[nm][m * 128:(m + 1) * 128, :], wsrc[nm][m * 128:(m + 1) * 128, :], writes=[wscrT[nm][m]])
    P.dma("sp", smallT.ap, smallp, writes=[smallT])
    P.dma("sp", scT.ap, cT, writes=[scT])
    P.dma("pool", GW.ap, gw_in, writes=[GW])
    prepass_upto(LOOK)
    P.op("dve", lambda h: h.memset(cvec.ap[:, 0:1], 1.0), writes=[cvec])
    P.op("dve", lambda h: h.memset(cvec.ap[:, 1:2], EPS2), writes=[cvec])
    P.op("dve", lambda h: h.memset(cvec.ap[:, 2:3], 0.0), writes=[cvec])
    P.op("dve", lambda h: h.memset(onesD.ap, 1.0 / D), writes=[onesD])
    P.op("dve", lambda h: h.memset(onesB.ap, 1.0), writes=[onesB])
    ONE = cvec.ap[:, 0:1]
    EPSc = cvec.ap[:, 1:2]
    ACT(P, scT.ap, scT.ap, AF.Silu, [scT], [scT])
    ACT(P, CLAM.ap, sm[:, SOFF["rlam"]:SOFF["rlam"] + 8], AF.Exp, [smallT], [CLAM], scale=-1.0)
    TS(P, "dve", CLAM.ap, CLAM.ap, 1.0, None, ALU.add, None, [CLAM], [CLAM])
    ACT(P, CLAM.ap, CLAM.ap, AF.Ln, [CLAM], [CLAM])
    TS(P, "dve", CLAM.ap, CLAM.ap, -8.0, None, ALU.mult, None, [CLAM], [CLAM])
    mwb = uphase([(i * 2048, 1024, BF16) for i in range(6)])
    scTb = T(P.sbuf("scTb", [128, KC * NJ], BF16)[:])
    CP(P, "dve", scTb.ap, scT.ap, [scT], [scTb])
    sc3 = scTb.ap.rearrange("p (c j) -> p c j", c=KC)
    for l in range(layers):
        for q in range(48):
            wt = mwb[(l * 48 + q) % 6]
            r0 = (l * 48 + q) * 128
            P.dma("pool", wt.ap, modw[r0:r0 + 128, :], writes=[wt])
            ps = nb_()
            w3 = wt.ap.rearrange("p (k m) -> p k m", k=KC)
            for kc in range(KC):
                MM(P, ps, ps.ap[:, 0:NJ], w3[:, kc, :], sc3[:, kc, :], kc == 0, kc == KC - 1, [wt, scTb])
            o0 = l * NJ * 48 + q
            outap = AP(MODS.ap.tensor, MODS.ap[:, o0:o0 + 1].offset, [list(MODS.ap.ap[0]), [48, NJ]])
            TS(P, "dve", outap, ps.ap[:, 0:NJ], scol("modb", l * 48 + q), None, ALU.add, None, [ps, smallT], [MODS])
    for l in range(layers):
        for j in range(NJ):
            def M8(which):
                o = (l * NJ + j) * 48 + which * 8
                return MODS.ap[:, o:o + 8]

            def D8(k):
                o = (k * NJ + j) * 8
                return DER.ap[:, o:o + 8]
            ln1g = sm[:, SOFF["ln1g"] + l * 8: SOFF["ln1g"] + l * 8 + 8]
            ln1b = sm[:, SOFF["ln1b"] + l * 8: SOFF["ln1b"] + l * 8 + 8]
            TS(P, "dve", D8(0 + l), M8(1), 1.0, None, ALU.add, None, [MODS], [DER])
            TS(P, "dve", D8(2 + l), M8(2), 1.0 / ALPHA, None, ALU.mult, None, [MODS], [DER])
            TS(P, "dve", D8(4 + l), M8(5), 1.0 / ALPHA, None, ALU.mult, None, [MODS], [DER])
            TS(P, "dve", D8(8 + l), M8(4), 1.0, None, ALU.add, None, [MODS], [DER])
            TT(P, "pool", D8(6 + l), D8(8 + l), ln1g, ALU.mult, [DER, smallT], [DER])
            TT(P, "pool", D8(8 + l), D8(8 + l), ln1b, ALU.mult, [DER, smallT], [DER])
            TT(P, "dve", D8(8 + l), D8(8 + l), M8(3), ALU.add, [DER, MODS], [DER])
    if layers == 2:
        for j in range(NJ):
            ln2g = sm[:, SOFF["ln2g"]: SOFF["ln2g"] + 8]
            ln2b = sm[:, SOFF["ln2b"]: SOFF["ln2b"] + 8]
            A1n = DER.ap[:, (1 * NJ + j) * 8:(1 * NJ + j) * 8 + 8]
            SH1n = MODS.ap[:, (1 * NJ + j) * 48: (1 * NJ + j) * 48 + 8]
            g1p = DER.ap[:, (10 * NJ + j) * 8:(10 * NJ + j) * 8 + 8]
            b1p = DER.ap[:, (11 * NJ + j) * 8:(11 * NJ + j) * 8 + 8]
            TT(P, "pool", g1p, A1n, ln2g, ALU.mult, [DER, smallT], [DER])
            TT(P, "pool", b1p, A1n, ln2b, ALU.mult, [DER, smallT], [DER])
            TT(P, "dve", b1p, b1p, SH1n, ALU.add, [DER, MODS], [DER])

    def load_w(nm, m, ncol=1024, buf=None):
        if (nm, m) not in wstate["seen"]:
            wstate["seen"].append((nm, m))
        if order is not None:
            prepass_upto(wpos[(nm, m)] + 1 + LOOK)
        if buf is None:
            buf = WA[wa_i[0] % 3]
            wa_i[0] += 1
        P.dma("sp", buf.ap[:, 0:ncol], wscr[nm][m * 128:(m + 1) * 128, :], reads=[wscrT[nm][m]], writes=[buf])
        return buf

    def proj_fm(wt, rhs3, rhsT, c0, c1, nk=KC):
        ps = nb_()
        w3 = wt.ap[:, 0:nk * 128].rearrange("p (k m) -> p k m", k=nk)
        for kc in range(nk):
            MM(P, ps, ps.ap[:, 0:c1 - c0], w3[:, kc, :], rhs3[:, kc, c0:c1], kc == 0, kc == nk - 1, [wt] + rhsT)
        return ps

    def seqs_of(b, with_ctx):
        s = []
        if with_ctx:
            s.append((0, CT, NB))
        s.append((CT, NT, b))
        return s

    def layer_norm_piece(c0, c1, j, l, which, lnT, nextmod, hskip=0):
        n = c1 - c0
        SQ, MS, RS = lnT
        psm = nb_()
        pse = nb_()
        for mc in range(KC):
            sq = SQ[mc % 2]
            ACT(P, sq.ap[:, 0:n], X3[:, mc, c0:c1], AF.Square, xt(mc, c0, c1), [sq])
            MM(P, psm, psm.ap[:, 0:n], onesD.ap, X3[:, mc, c0:c1], mc == 0, mc == KC - 1, [onesD] + xt(mc, c0, c1))
            MM(P, pse, pse.ap[:, 0:n], onesD.ap, sq.ap[:, 0:n], mc == 0, mc == KC - 1, [onesD, sq])
        ACT(P, MS.ap[:, 0:n], psm.ap[:, 0:n], AF.Square, [psm], [MS])
        TT(P, "dve", MS.ap[:, 0:n], pse.ap[:, 0:n], MS.ap[:, 0:n], ALU.subtract, [pse, MS], [MS])
        ACT(P, MS.ap[:, 0:n], MS.ap[:, 0:n], AF.Sqrt, [MS, cvec], [MS], bias=EPSc, scale=1.0)
        P.op("dve", lambda h: h.reciprocal(RS.ap[:, 0:n], MS.ap[:, 0:n]), reads=[MS], writes=[RS])
        gname = "ln1g" if which == 1 else "ln2g"
        bname = "ln1b" if which == 1 else "ln2b"
        for mc in range(KC):
            xs = X3[:, mc, c0:c1]
            TT(P, "dve", xs, xs, psm.ap[:, 0:n], ALU.subtract, xt(mc, c0, c1) + [psm], xt(mc, c0, c1))
            TT(P, "pool", xs, xs, RS.ap[:, 0:n], ALU.mult, xt(mc, c0, c1) + [RS], xt(mc, c0, c1))
            if nextmod is not None:
                gk, bk = nextmod
                ACT(P, H3[:, mc, c0:c1 - hskip], X3[:, mc, c0:c1 - hskip], AF.Identity, xt(mc, c0, c1) + [DER], ht1(mc, c0, c1), bias=der(bk, j, mc), scale=der(gk, j, mc))
            ACT(P, xs, xs, AF.Identity, xt(mc, c0, c1) + [smallT], xt(mc, c0, c1), bias=scol(bname, l * 8 + mc), scale=scol(gname, l * 8 + mc))

    def proj_resid_ln(b, l, which, wname, nk, rhs3, rhsT, seqs, col_shift, lnT, nextmod, wbufs=None, mc_order=None):
        gk = (2 if which == 1 else 4) + l
        for (lo, hi, j) in seqs:
            for (c0, c1) in pieces(lo, hi):
                for mc in (mc_order or range(KC)):
                    if wbufs is None:
                        wt = load_w(wname, mc)
                    else:
                        wt = load_w(wname, mc, ncol=nk * 128, buf=wbufs[mc % 2])
                    ps = nb_()
                    w3 = wt.ap[:, 0:nk * 128].rearrange("p (k m) -> p k m", k=nk)
                    for kc in range(nk):
                        MM(P, ps, ps.ap[:, 0:c1 - c0], w3[:, kc, :], rhs3[:, kc, c0 - col_shift:c1 - col_shift],
                           kc == 0, kc == nk - 1, [wt] + rhsT)
                    xs = X3[:, mc, c0:c1]
                    STT(P, "dve", xs, ps.ap[:, 0:c1 - c0], der(gk, j, mc), xs, ALU.mult, ALU.add, [ps, DER] + xt(mc, c0, c1), xt(mc, c0, c1))
                layer_norm_piece(c0, c1, j, l, which, lnT, nextmod)

    def conv_seq(dst, src, dstT, srcT, lo, hi, dlo, slo, wname, wbase, K, padl, bias_ap, seq_lo, seq_hi, eng="dve"):
        ctr = padl
        TS(P, eng, dst[:, lo - dlo:hi - dlo], src[:, lo - slo:hi - slo], scol(wname, wbase + ctr), bias_ap, ALU.mult, ALU.add,
           [srcT, smallT], [dstT])
        for k in range(K):
            if k == ctr:
                continue
            o = k - padl
            t0 = max(lo, seq_lo - o)
            t1 = min(hi, seq_hi - o)
            if t1 <= t0:
                continue
            STT(P, eng, dst[:, t0 - dlo:t1 - dlo], src[:, t0 + o - slo:t1 + o - slo], scol(wname, wbase + k), dst[:, t0 - dlo:t1 - dlo],
                ALU.mult, ALU.add, [srcT, smallT, dstT], [dstT])

    def ffn_stage(b, l, seqs_super, nextmod):
        specs = [(fc * 2048, 1024, BF16) for fc in range(FC)]
        specs += [(45056, 1056, F32), (49280, 1024, F32), (53376, DFF, BF16), (59008, DFF, BF16)]
        specs += [(64640 + i * 2048, 512, F32) for i in range(4)]
        ts = uphase(specs)
        A = ts[0:FC]
        G, CV = ts[FC], ts[FC + 1]
        WD = ts[FC + 2:FC + 4]
        lnT = (ts[FC + 4:FC + 6], ts[FC + 6], ts[FC + 7])
        wg, wu, wd = "wg%d" % l, "wu%d" % l, "wd%d" % l
        patch = []
        for (lo, hi, j, seq_lo, seq_hi) in seqs_super:
            glo = max(seq_lo, lo - 1)
            ghi = min(seq_hi, hi + 1)
            for fc in range(FC):
                wtg = load_w(wg, fc)
                for (c0, c1) in pieces(glo, ghi):
                    ps = proj_fm(wtg, H3, ht(c0, c1), c0, c1)
                    CP(P, "act", G.ap[:, c0 - glo:c1 - glo], ps.ap[:, 0:c1 - c0], [ps], [G])
                conv_seq(CV.ap, G.ap, CV, G, lo, hi, lo, glo, "fcw", (l * FC + fc) * 3, 3, 1, scol("fcb", l * FC + fc), seq_lo, seq_hi)
                ACT(P, CV.ap[:, 0:hi - lo], CV.ap[:, 0:hi - lo], AF.Silu, [CV], [CV])
                wtu = load_w(wu, fc)
                for (c0, c1) in pieces(lo, hi):
                    ps = proj_fm(wtu, H3, ht(c0, c1), c0, c1)
                    TT(P, "dve", A[fc].ap[:, c0 - lo:c1 - lo], ps.ap[:, 0:c1 - c0], CV.ap[:, c0 - lo:c1 - lo], ALU.mult, [ps, CV], [A[fc]])
            A3 = AP(A[0].ap.tensor, A[0].ap.offset, [list(A[0].ap.ap[0]), [1024, FC], [1, 1024]])
            gk = 4 + l
            for (c0, c1) in pieces(lo, hi):
                for mc in range(KC):
                    wt = load_w(wd, mc, ncol=DFF, buf=WD[mc % 2])
                    ps = nb_()
                    w3 = wt.ap.rearrange("p (k m) -> p k m", k=FC)
                    for kc in range(FC):
                        MM(P, ps, ps.ap[:, 0:c1 - c0], w3[:, kc, :], A3[:, kc, c0 - lo:c1 - lo], kc == 0, kc == FC - 1, [wt, A[kc]])
                    xs = X3[:, mc, c0:c1]
                    STT(P, "dve", xs, ps.ap[:, 0:c1 - c0], der(gk, j, mc), xs, ALU.mult, ALU.add, [ps, DER] + xt(mc, c0, c1), xt(mc, c0, c1))
                more = any((s2[3] == seq_lo and s2[0] == hi) for s2 in seqs_super)
                hs = 1 if (nextmod is not None and c1 == hi and more) else 0
                layer_norm_piece(c0, c1, j, l, 2, lnT, nextmod, hskip=hs)
                if l == 1:
                    for mc in range(KC):
                        P.dma("sp" if mc % 2 == 0 else "pool", yT[b, mc][:, c0 - CT:c1 - CT], X3[:, mc, c0:c1], reads=xt(mc, c0, c1))
                if hs:
                    patch.append((hi - 1, j))
        if patch:
            patch_h(patch, l + 1)

    def patch_h(patch, lnext):
        for (col, j) in patch:
            for mc in range(KC):
                ACT(P, H3[:, mc, col:col + 1], X3[:, mc, col:col + 1], AF.Identity, xt(mc, col, col + 1) + [DER, MODS], ht1(mc, col, col + 1),
                    bias=mods(lnext, j, mc), scale=der(0 + lnext, j, mc))

    def even_mixer(b):
        specs = [(c * 4608, NT, BF16) for c in range(KC)]
        specs += [(36864 + i * 9216, NT, F32) for i in range(3)]
        specs += [(64512, NT, BF16)]
        specs += [(69120 + i * 2048, 512, F32) for i in range(4)]
        ts = uphase(specs)
        YC = ts[0:KC]
        setA = (ts[KC], ts[KC + 1], ts[KC + 2], ts[KC + 3], ts[KC + 4:KC + 7], sttA)
        TM = ts[KC + 4:KC + 8]
        def xcarve(off, n, dt):
            if dt == F32:
                return Xs[:, off // 4: off // 4 + n]
            return Xs[:, off // 4: off // 4 + (n + 1) // 2].bitcast(BF16)[:, :n]
        xs_specs = [(i * 9216, NT, F32) for i in range(3)] + [(27648, NT, BF16)] + [(32256 + i * 2048, 512, F32) for i in range(2)]
        xs_specs += [(36864, NT, F32), (46080, NT, F32)]
        xsT = [T(xcarve(o, n, dt)) for (o, n, dt) in xs_specs]
        borrowB = [t for c in range(4) for t in XR[c]]
        borrowS = [t for c in (4, 5) for t in XR[c]]
        P.realias(xsT[0:6], borrowB)
        P.realias(xsT[6:8], borrowS)
        setB = (xsT[0], xsT[1], xsT[2], xsT[3], xsT[4:6], sttB)
        S0, S1 = xsT[6], xsT[7]

        def reload_x(chunks):
            for r, (lo, hi) in enumerate(REG):
                for c in chunks:
                    q = "sp" if c % 2 == 0 else "pool"
                    if r == 0:
                        P.dma(q, X3[:, c, 0:CT], ctxT[b, c], writes=[XR[c][0]])
                    else:
                        P.dma(q, X3[:, c, lo:hi], xT[b, c][:, lo - CT:hi - CT], writes=[XR[c][r]])
        seqs = seqs_of(b, True)
        allp = [pc for (lo, hi, j) in seqs for pc in pieces(lo, hi)]

        def rec_chunk(c, bufs, wbuf):
            R0, R1, R2, XCb, TMs, st = bufs
            tm_i = [0]

            def ntmp():
                t = TMs[tm_i[0] % len(TMs)]
                tm_i[0] += 1
                return t
            wt = load_w("win", c, buf=wbuf)
            for (c0, c1) in allp:
                ps = proj_fm(wt, H3, ht(c0, c1), c0, c1)
                CP(P, "act", R0.ap[:, c0:c1], ps.ap[:, 0:c1 - c0], [ps], [R0])
                yield
            for (lo, hi, j) in seqs:
                conv_seq(R1.ap, R0.ap, R1, R0, lo, hi, 0, 0, "rcw", c * 4, 4, 2, scol("rcb", c), lo, hi)
            yield
            CP(P, "pool", XCb.ap, R1.ap, [R1], [XCb])
            yield
            for d in range(2):
                Bd = R2 if d == 0 else R1
                for (c0, c1) in allp:
                    n = c1 - c0
                    psr = nb_()
                    MM(P, psr, psr.ap[:, 0:n], GW.ap[:, ((d * 2 + 0) * 4 + c) * 128:((d * 2 + 0) * 4 + c + 1) * 128], XCb.ap[:, c0:c1], True, True, [GW, XCb])
                    psi = nb_()
                    MM(P, psi, psi.ap[:, 0:n], GW.ap[:, ((d * 2 + 1) * 4 + c) * 128:((d * 2 + 1) * 4 + c + 1) * 128], XCb.ap[:, c0:c1], True, True, [GW, XCb])
                    ACT(P, R0.ap[:, c0:c1], psr.ap[:, 0:n], AF.Sigmoid, [psr, smallT], [R0], bias=scol("rba", d * 4 + c))
                    t1 = ntmp()
                    ACT(P, t1.ap[:, 0:n], psi.ap[:, 0:n], AF.Sigmoid, [psi, smallT], [t1], bias=scol("rbx", d * 4 + c))
                    if d == 0:
                        TT(P, "dve", Bd.ap[:, c0:c1], t1.ap[:, 0:n], R1.ap[:, c0:c1], ALU.mult, [t1, R1], [Bd])
                    else:
                        TT(P, "dve", Bd.ap[:, c0:c1], R1.ap[:, c0:c1], t1.ap[:, 0:n], ALU.mult, [t1, R1], [Bd])
                    yield
                ACT(P, R0.ap, R0.ap, AF.Exp, [R0, CLAM], [R0], scale=CLAM.ap[:, d * 4 + c:d * 4 + c + 1])
                yield
                for (c0, c1) in allp:
                    n = c1 - c0
                    a_ = R0.ap[:, c0:c1]
                    t2 = ntmp()
                    TT(P, "pool", t2.ap[:, 0:n], a_, a_, ALU.mult, [R0], [t2])
                    ACT(P, t2.ap[:, 0:n], t2.ap[:, 0:n], AF.Sqrt, [t2, cvec], [t2], bias=ONE, scale=-1.0)
                    TT(P, "dve", Bd.ap[:, c0:c1], Bd.ap[:, c0:c1], t2.ap[:, 0:n], ALU.mult, [Bd, t2], [Bd])
                    yield
                if d == 0:
                    SCAN(P, Bd.ap[:, 0:CT], R0.ap[:, 0:CT], Bd.ap[:, 0:CT], 0.0, [R0, Bd], [Bd])
                    CP(P, "act", st.ap[:, 0:1], Bd.ap[:, CT - 1:CT], [Bd], [st])
                    yield
                    SCAN(P, Bd.ap[:, CT:NT], R0.ap[:, CT:NT], Bd.ap[:, CT:NT], st.ap[:, 0:1], [R0, Bd, st], [Bd])
                else:
                    SCAN(P, rev(Bd.ap[:, 0:CT]), rev(R0.ap[:, 0:CT]), rev(Bd.ap[:, 0:CT]), 0.0, [R0, Bd], [Bd])
                    CP(P, "act", st.ap[:, 1:2], Bd.ap[:, 0:1], [Bd], [st])
                    yield
                    SCAN(P, rev(Bd.ap[:, CT:NT]), rev(R0.ap[:, CT:NT]), rev(Bd.ap[:, CT:NT]), st.ap[:, 1:2], [R0, Bd, st], [Bd])
                yield
            TT(P, "pool", R2.ap, R2.ap, R1.ap, ALU.add, [R2, R1], [R2])
            yield
            wt = load_w("win", 4 + c, buf=wbuf)
            for (c0, c1) in allp:
                n = c1 - c0
                ps = proj_fm(wt, H3, ht(c0, c1), c0, c1)
                t1 = ntmp()
                ACT(P, t1.ap[:, 0:n], ps.ap[:, 0:n], AF.Gelu_apprx_tanh, [ps], [t1])
                TT(P, "dve", YC[c].ap[:, c0:c1], t1.ap[:, 0:n], R2.ap[:, c0:c1], ALU.mult, [t1, R2], [YC[c]])
                yield

        def sc_chunk(c, wbuf):
            wt = load_w("win", 12 + c, buf=wbuf)
            for (c0, c1) in allp:
                ps = proj_fm(wt, H3, ht(c0, c1), c0, c1)
                CP(P, "act", S0.ap[:, c0:c1], ps.ap[:, 0:c1 - c0], [ps], [S0])
                yield
            wt = load_w("win", 16 + c, buf=wbuf)
            for (c0, c1) in allp:
                ps = proj_fm(wt, H3, ht(c0, c1), c0, c1)
                TT(P, "dve", S0.ap[:, c0:c1], ps.ap[:, 0:c1 - c0], S0.ap[:, c0:c1], ALU.mult, [ps, S0], [S0])
                yield
            for (lo, hi, j) in seqs:
                conv_seq(S1.ap, S0.ap, S1, S0, lo, hi, 0, 0, "scw", c * 3, 3, 1, scol("scb", c), lo, hi)
            yield
            wt = load_w("win", 8 + c, buf=wbuf)
            for (c0, c1) in allp:
                ps = proj_fm(wt, H3, ht(c0, c1), c0, c1)
                TT(P, "dve", YC[4 + c].ap[:, c0:c1], ps.ap[:, 0:c1 - c0], S1.ap[:, c0:c1], ALU.mult, [ps, S1], [YC[4 + c]])
                yield

        def chain(gs):
            for g in gs:
                yield from g
        gens = [("a", chain([rec_chunk(0, setA, WA[0]), rec_chunk(2, setA, WA[0])])),
                ("b", chain([rec_chunk(1, setB, WA[1]), rec_chunk(3, setB, WA[1])])),
                ("s", chain([sc_chunk(c, WA[2]) for c in range(4)]))]
        while gens:
            for item in list(gens):
                try:
                    next(item[1])
                except StopIteration:
                    gens.remove(item)
                    if item[0] == "s":
                        P.realias(borrowS, xsT[6:8])
                        reload_x([4, 5])
                    elif item[0] == "b":
                        P.realias(borrowB, xsT[0:6])
                        reload_x([0, 1, 2, 3])
        YC3 = AP(YC[0].ap.tensor, YC[0].ap.offset, [list(YC[0].ap.ap[0]), [NT, KC], [1, NT]])
        lnT = ([TM[0], TM[1]], TM[2], TM[3])
        proj_resid_ln(b, 0, 1, "wout0", KC, YC3, YC, seqs, 0, lnT, (6, 8), mc_order=[7, 6, 5, 4, 3, 2, 1, 0])

    def odd_mixer(b):
        specs = [(c * 4096, SEQ, BF16) for c in range(KC)]
        specs += [(32768, SEQ, BF16), (36864, SEQ, BF16), (40960, NT, BF16), (45568, 18 * 192, BF16)]
        specs += [(52480 + i * 3584, 896, F32) for i in range(4)]
        specs += [(66816 + i * 2048, 512, F32) for i in range(2)]
        specs += [(70912 + i * 1024, 512, BF16) for i in range(8)]
        specs += [(79104 + i * 1024, 256, F32) for i in range(2)]
        ts = uphase(specs)
        OT = ts[0:KC]
        QZ0, QZ1, KT, V = ts[KC:KC + 4]
        TB = ts[KC + 4:KC + 8]
        E = ts[KC + 8:KC + 10]
        PT = ts[KC + 10:KC + 18]
        RC = [ts[KC + 18], ts[KC + 18]]
        OS = ts[KC + 19]
        P.op("pool", lambda h: h.memset(QZ0.ap[64:128, :], 0.0), writes=[QZ0])
        P.op("pool", lambda h: h.memset(QZ1.ap[0:64, :], 0.0), writes=[QZ1])
        V3 = V.ap.rearrange("p (t m) -> p t m", t=18)
        P.op("pool", lambda h: h.memset(V3[:, :, 64:128], 1.0), writes=[V])
        cnt = dict(e=0, p=0, r=0)
        j = b
        for hp in range(KC):
            wq = load_w("wqkv", hp)
            for (c0, c1) in pieces(CT, NT):
                ps = proj_fm(wq, H3, ht(c0, c1), c0, c1)
                CP(P, "act", QZ0.ap[0:64, c0 - CT:c1 - CT], ps.ap[0:64, 0:c1 - c0], [ps], [QZ0])
                CP(P, "dve", QZ1.ap[64:128, c0 - CT:c1 - CT], ps.ap[64:128, 0:c1 - c0], [ps], [QZ1])
            wk = load_w("wqkv", 8 + hp)
            for (c0, c1) in pieces(0, NT):
                ps = proj_fm(wk, H3, ht(c0, c1), c0, c1)
                CP(P, "dve", KT.ap[:, c0:c1], ps.ap[:, 0:c1 - c0], [ps], [KT])
            wv = load_w("wqkv", 16 + hp)
            wv3 = wv.ap.rearrange("p (k m) -> p k m", k=KC)
            for t0 in range(0, 18, 4):
                nt_ = min(4, 18 - t0)
                ps = nb_()
                for ti in range(nt_):
                    tt = t0 + ti
                    for kc in range(KC):
                        MM(P, ps, ps.ap[:, ti * 128:(ti + 1) * 128], H3[:, kc, tt * 128:(tt + 1) * 128], wv3[:, kc, :],
                           kc == 0, kc == KC - 1, [wv] + ht(tt * 128, (tt + 1) * 128))
                ps3 = ps.ap[:, 0:nt_ * 128].rearrange("p (t m) -> p t m", t=nt_)
                CP(P, "act", V3[:, t0:t0 + nt_, 0:64], ps3[:, :, 0:64], [ps], [V])
                CP(P, "dve", V3[:, t0:t0 + nt_, 128:192], ps3[:, :, 64:128], [ps], [V])
            for par in range(2):
                h_ = 2 * hp + par
                P.dma("sp", TB[par * 2].ap, tmid[h_ * 128:(h_ + 1) * 128, :], writes=[TB[par * 2]])
                P.dma("sp", TB[par * 2 + 1].ap, tfull[h_ * 128:(h_ + 1) * 128, :], writes=[TB[par * 2 + 1]])
                ACT(P, TB[par * 2].ap, TB[par * 2].ap, AF.Exp, [TB[par * 2]], [TB[par * 2]])
                ACT(P, TB[par * 2 + 1].ap, TB[par * 2 + 1].ap, AF.Exp, [TB[par * 2 + 1]], [TB[par * 2 + 1]])
            items = [(par, r0) for par in range(2) for r0 in range(0, 32, 4)]

            def s_phase(par, r0):
                pb = par * 64
                q0 = r0 * 64
                if r0 == 0:
                    chunks, kind = [0, 1, 2, 3], 1
                elif r0 == 28:
                    chunks, kind = [12, 13, 14, 15], 1
                else:
                    a0 = (r0 - 4) // 2
                    chunks, kind = list(range(a0, a0 + 6)), 0
                tb = TB[par * 2 + kind]
                QZ = QZ0 if par == 0 else QZ1
                pts = []
                for i in range(0, len(chunks), 2):
                    ps = nb_()
                    for k in range(2):
                        a = chunks[i + k]
                        MM(P, ps, ps.ap[:, k * 256:(k + 1) * 256], KT.ap[:, CT + a * 128:CT + (a + 1) * 128],
                           QZ.ap[:, q0:q0 + 256], True, True, [KT, QZ])
                    e = E[cnt["e"] % 2]
                    cnt["e"] += 1
                    ACT(P, e.ap, ps.ap, AF.Exp, [ps], [e], scale=0.125)
                    pt = PT[cnt["p"] % 8]
                    cnt["p"] += 1
                    ei0 = r0 - 2 * chunks[i] + 6
                    tap = AP(tb.ap.tensor, tb.ap[:, ei0 * 64:ei0 * 64 + 1].offset, [list(tb.ap.ap[0]), [-128, 2], [1, 256]])
                    TT(P, "pool" if (i // 2) == 1 else "dve", pt.ap.rearrange("p (a b) -> p a b", a=2), e.ap.rearrange("p (a b) -> p a b", a=2), tap, ALU.mult, [e, tb], [pt])
                    pts.append((pt, chunks[i], chunks[i + 1]))
                ps = nb_()
                for k in range(2):
                    MM(P, ps, ps.ap[:, k * 256:(k + 1) * 256], KT.ap[:, k * 128:(k + 1) * 128],
                       QZ.ap[:, q0:q0 + 256], True, True, [KT, QZ])
                ptc = PT[cnt["p"] % 8]
                cnt["p"] += 1
                ACT(P, ptc.ap, ps.ap, AF.Exp, [ps], [ptc], scale=0.125)
                return (par, r0, pts, ptc)

            def pv_phase(st):
                par, r0, pts, ptc = st
                pb = par * 64
                q0 = r0 * 64
                pso = nb_()
                ob = 64 - pb
                mms = []
                for (pt, a, a2) in pts:
                    mms.append((V3[:, 2 + a, pb:pb + 128], pt.ap[:, 0:256], pt))
                    mms.append((V3[:, 2 + a2, pb:pb + 128], pt.ap[:, 256:512], pt))
                mms.append((V3[:, 0, pb:pb + 128], ptc.ap[:, 0:256], ptc))
                mms.append((V3[:, 1, pb:pb + 128], ptc.ap[:, 256:512], ptc))
                nm_ = len(mms)
                for i, (l_, r_, pt) in enumerate(mms):
                    MM(P, pso, pso.ap[:, 0:256], l_, r_, i == 0, i == nm_ - 1, [V, pt])
                rl, rc = OS, RC[0]
                ACT(P, rl.ap[pb:pb + 64, :], pso.ap[ob:ob + 64, 0:256], AF.Ln, [pso], [rl])
                CP(P, "dve", rc.ap[pb:pb + 64, :], rl.ap[pb:pb + 64, :], [rl], [rc])
                ACT(P, rc.ap[pb:pb + 64, :], rc.ap[pb:pb + 64, :], AF.Exp, [rc], [rc], scale=-1.0)
                TT(P, "dve", OT[hp].ap[pb:pb + 64, q0:q0 + 256], pso.ap[pb:pb + 64, 0:256], rc.ap[pb:pb + 64, :], ALU.mult, [pso, rc], [OT[hp]])

            prev = None
            for (par, r0) in items:
                st = s_phase(par, r0)
                if prev is not None:
                    pv_phase(prev)
                prev = st
            pv_phase(prev)
        OT3 = AP(OT[0].ap.tensor, OT[0].ap.offset, [list(OT[0].ap.ap[0]), [SEQ, KC], [1, SEQ]])
        lnA, lnB = T(ucarve(70912, 512, F32)), T(ucarve(72960, 512, F32))
        lnT = ([E[0], E[1]], lnA, lnB)
        P.realias([lnA, lnB], PT)
        u_live.extend([lnA, lnB])
        proj_resid_ln(b, 1, 1, "wout1", KC, OT3, OT, seqs_of(b, False), CT, lnT, (6 + 1, 8 + 1))

    for b in range(NB):
        for c in range(KC):
            q = "sp" if c % 2 == 0 else "pool"
            P.dma(q, X3[:, c, 0:CT], ctxT[b, c], writes=[XR[c][0]])
            P.dma(q, X3[:, c, CT:CT + 1024], xT[b, c][:, 0:1024], writes=[XR[c][1]])
            P.dma(q, X3[:, c, CT + 1024:NT], xT[b, c][:, 1024:2048], writes=[XR[c][2]])
        for c in range(KC):
            for r, (lo, hi) in enumerate(REG):
                j = NB if r == 0 else b
                ACT(P, H3[:, c, lo:hi], X3[:, c, lo:hi], AF.Identity, [XR[c][r], DER, MODS], [HR[c][r]], bias=mods(0, j, c), scale=der(0, j, c))
        if stop != "load":
            even_mixer(b)
        nm0 = (10, 11) if layers == 2 else None
        if stop is None:
            ffn_stage(b, 0, [(0, CT, NB, 0, CT), (CT, CT + 1024, b, CT, NT), (CT + 1024, NT, b, CT, NT)], nm0)

        if layers == 2:
            odd_mixer(b)
            ffn_stage(b, 1, [(CT, CT + 1024, b, CT, NT), (CT + 1024, NT, b, CT, NT)], None)
        if not (layers == 2 and stop is None):
            for c in range(KC):
                q = "sp" if c % 2 == 0 else "pool"
                P.dma(q, yT[b, c], X3[:, c, CT:NT], reads=[XR[c][1], XR[c][2]])
    allx = [t for c in range(KC) for t in XR[c]]
    P.wait_all("sp", allx)
    P.worder = wstate["seen"]
    P.finish()
    return nc, P


def _wt(W):
    K, M = W.shape
    return np.ascontiguousarray(W.reshape(K // 128, 128, M // 128, 128).transpose(2, 1, 0, 3)).reshape(M, K)


def _pc(v, nchunk):
    v = np.asarray(v, np.float32)
    lead = v.shape[:-1]
    return np.moveaxis(v.reshape(lead + (nchunk, 128)), -1, 0)


def _tables(rpb):
    kr2 = np.arange(2)[:, None, None, None]
    kcol = np.arange(64)[None, :, None, None]
    e = (np.arange(14) - 6)[None, None, :, None]
    qcol = np.arange(64)[None, None, None, :]
    dr = kr2 - e + 7 + 0 * kcol + 0 * qcol
    dc = kcol - qcol + 15 + 0 * kr2 + 0 * e
    cs = np.clip(qcol - 8, 0, 48)
    colok = (kcol >= cs) & (kcol < cs + 16)
    drok = (dr >= 0) & (dr <= 14)
    rowmid = (kr2 - e >= -4) & (kr2 - e <= 3)
    g = rpb[:, np.clip(dr, 0, 14), np.clip(dc, 0, 30)]
    negs = np.full(g.shape, NEG, np.float32)
    tfull = np.where((colok & drok)[None], g, negs).reshape(16 * 128, 896)
    tmid = np.where((colok & drok & rowmid)[None], g, negs).reshape(16 * 128, 896)
    return np.ascontiguousarray(tmid, np.float32), np.ascontiguousarray(tfull, np.float32)


def prep_shared(mod_w, mod_b, ln1_g, ln1_b, ln2_g, ln2_b, ev_w_in, ev_w_out, rec_conv_w, rec_conv_b, rec_wa, rec_ba,
                rec_wx, rec_bx, rec_lam, sc_conv_w, sc_conv_b, na_w_qkv, na_w_out, na_rpb, ffn_w_gate, ffn_w_up,
                ffn_conv_w, ffn_conv_b, ffn_w_down):
    sh = {}
    sh["win"] = _wt(ev_w_in[0])
    sh["wout0"] = _wt(ev_w_out[0])
    sh["wqkv"] = _wt(na_w_qkv[0])
    sh["wout1"] = _wt(na_w_out[0])
    for l in range(2):
        sh["wg%d" % l] = _wt(ffn_w_gate[l])
        sh["wu%d" % l] = _wt(ffn_w_up[l])
        sh["wd%d" % l] = _wt(ffn_w_down[l])
    sh["modw"] = np.concatenate([_wt(mod_w[l]) for l in range(2)], axis=0)
    small = np.zeros((128, NS), np.float32)

    def put(nm, arr):
        arr = np.asarray(arr, np.float32).reshape(128, -1)
        small[:, SOFF[nm]:SOFF[nm] + arr.shape[1]] = arr
    put("modb", _pc(mod_b, 48))
    put("ln1g", _pc(ln1_g, 8)); put("ln1b", _pc(ln1_b, 8)); put("ln2g", _pc(ln2_g, 8)); put("ln2b", _pc(ln2_b, 8))
    put("rcw", np.transpose(_pc(rec_conv_w[0], 4), (0, 2, 1)))
    put("rcb", _pc(rec_conv_b[0], 4))
    put("rba", _pc(rec_ba[0], 4)); put("rbx", _pc(rec_bx[0], 4)); put("rlam", _pc(rec_lam[0], 4))
    put("scw", np.transpose(_pc(sc_conv_w[0], 4), (0, 2, 1)))
    put("scb", _pc(sc_conv_b[0], 4))
    put("fcw", np.transpose(_pc(ffn_conv_w, FC), (0, 1, 3, 2)))
    put("fcb", _pc(ffn_conv_b, FC))
    sh["smallp"] = small
    gw = np.zeros((128, 16, 128), np.float32)
    for d in range(2):
        for wi, W in enumerate((rec_wa[0], rec_wx[0])):
            for c in range(4):
                g = (d * 2 + wi) * 4 + c
                gw[0:64, g, 0:64] = W[d, 2 * c]
                gw[64:128, g, 64:128] = W[d, 2 * c + 1]
    sh["gw"] = gw.reshape(128, 16 * 128)
    sh["tmid"], sh["tfull"] = _tables(np.asarray(na_rpb[0], np.float32))
    return sh


def prep_core(x, c, ctx, c_ctx, b0, NB):
    m = {}
    m["xT"] = np.ascontiguousarray(x[b0:b0 + NB].transpose(0, 2, 1)).reshape(NB, KC, 128, SEQ)
    m["ctxT"] = np.ascontiguousarray(ctx[b0:b0 + NB].transpose(0, 2, 1)).reshape(NB, KC, 128, CT)
    cc = np.concatenate([c[b0:b0 + NB], c_ctx[None, :]], axis=0)
    m["cT"] = np.ascontiguousarray(np.transpose(cc.reshape(NB + 1, KC, 128), (2, 1, 0))).reshape(128, KC * (NB + 1))
    return m


_CACHE = {}


def kernel(x, c, ctx, c_ctx, mod_w, mod_b, ln1_g, ln1_b, ln2_g, ln2_b, ev_w_in, ev_w_out, rec_conv_w, rec_conv_b,
           rec_wa, rec_ba, rec_wx, rec_bx, rec_lam, sc_conv_w, sc_conv_b, na_w_qkv, na_w_out, na_rpb, ffn_w_gate,
           ffn_w_up, ffn_conv_w, ffn_conv_b, ffn_w_down):
    f = lambda a: np.asarray(a, np.float32)
    x, c, ctx, c_ctx = f(x), f(c), f(ctx), f(c_ctx)
    B = x.shape[0]
    NB = B // NCORES
    sh = prep_shared(f(mod_w), f(mod_b), f(ln1_g), f(ln1_b), f(ln2_g), f(ln2_b), f(ev_w_in), f(ev_w_out), f(rec_conv_w),
                     f(rec_conv_b), f(rec_wa), f(rec_ba), f(rec_wx), f(rec_bx), f(rec_lam), f(sc_conv_w), f(sc_conv_b),
                     f(na_w_qkv), f(na_w_out), f(na_rpb), f(ffn_w_gate), f(ffn_w_up), f(ffn_conv_w), f(ffn_conv_b), f(ffn_w_down))
    if NB not in _CACHE:
        _, p0 = build_program(1)
        _CACHE[NB] = build_program(NB, order=p0.worder)[0]
    nc = _CACHE[NB]
    in_maps = []
    for i in range(NCORES):
        m = dict(sh)
        m.update(prep_core(x, c, ctx, c_ctx, i * NB, NB))
        in_maps.append(m)
    res = run_bass_kernel_spmd(nc, in_maps, core_ids=list(range(NCORES)))
    out = np.empty((B, SEQ, D), np.float32)
    for i in range(NCORES):
        y = np.asarray(res.results[i]["yT"]).reshape(NB, D, SEQ)
        out[i * NB:(i + 1) * NB] = y.transpose(0, 2, 1)
    return out
```

```python
import numpy as np
import concourse.bass as bass
import concourse.mybir as mybir
from concourse.bass_utils import run_bass_kernel_spmd
from concourse.ap import AP

F32 = mybir.dt.float32
BF16 = mybir.dt.bfloat16
ALU = mybir.AluOpType
AF = mybir.ActivationFunctionType

NCORES = 8
D = 1024
KC = 8
SEQ = 2048
CT = 256
NT = SEQ + CT
DFF = 2816
FC = 22
ALPHA = 4.0 ** 0.25
EPS2 = 1e-6 / (ALPHA * ALPHA)
NEG = -30000.0


class T:
    __slots__ = ("ap", "w", "r")

    def __init__(self, ap):
        self.ap = ap
        self.w = None
        self.r = {}


class Eng:
    def __init__(self, name, h, sem):
        self.name = name
        self.h = h
        self.sem = sem
        self.count = 0
        self.waited = {}
        self.q = []


class Prog:
    def __init__(self, nc, n_dma_sems=8):
        self.nc = nc
        self._ctx = []
        self.engs = {}
        for nm, h in (("pe", nc.tensor), ("act", nc.scalar), ("dve", nc.vector), ("pool", nc.gpsimd), ("sp", nc.sync)):
            self.engs[nm] = Eng(nm, h, self._sem("e_" + nm))
        self.n_dma_sems = n_dma_sems
        self.dma_pool = {}
        for nm in ("sp", "pool"):
            self.dma_pool[nm] = dict(sems=[self._sem("d_%s%d" % (nm, i)) for i in range(n_dma_sems)], j=0)
        self.ninst = 0
        self.force_own = False

    def _sem(self, name):
        cm = self.nc.semaphore(name)
        s = cm.__enter__()
        self._ctx.append(cm)
        return (name, s)

    def sbuf(self, name, shape, dt):
        cm = self.nc.sbuf_tensor(name, shape, dt)
        t = cm.__enter__()
        self._ctx.append(cm)
        return t

    def psum(self, name, shape, dt):
        cm = self.nc.psum_tensor(name, shape, dt)
        t = cm.__enter__()
        self._ctx.append(cm)
        return t

    @staticmethod
    def _deps(reads, writes):
        deps = {}

        def add(k, v):
            if deps.get(k, 0) < v:
                deps[k] = v
        for t in reads:
            if t.w is not None:
                add(*t.w)
        for t in writes:
            if t.w is not None:
                add(*t.w)
            for k, v in t.r.items():
                add(k, v)
        return deps

    def _waits(self, e, deps, own_too):
        for (name, sh), v in deps.items():
            if (not own_too) and name == e.sem[0]:
                continue
            if e.waited.get(name, 0) >= v:
                continue
            e.waited[name] = v
            e.q.append(lambda h, sh=sh, v=v: h.wait_ge(sh, v))
            self.ninst += 1

    def op(self, eng, fn, reads=(), writes=(), sig=True, own=False):
        e = self.engs[eng]
        self._waits(e, self._deps(reads, writes), own or self.force_own)
        if sig:
            e.count += 1
            val = e.count
            sh = e.sem[1]
            e.q.append(lambda h, fn=fn, sh=sh: fn(h).then_inc(sh, 1))
        else:
            val = e.count + 1
            e.q.append(lambda h, fn=fn: fn(h))
        self.ninst += 1
        for t in reads:
            if t.r.get(e.sem, 0) < val:
                t.r[e.sem] = val
        for t in writes:
            t.w = (e.sem, val)
            t.r = {}

    def dma(self, q, out_ap, in_ap, reads=(), writes=()):
        e = self.engs[q]
        pool = self.dma_pool[q]
        j = pool["j"]
        pool["j"] += 1
        s = pool["sems"][j % self.n_dma_sems]
        gen = j // self.n_dma_sems
        deps = self._deps(reads, writes)
        if gen > 0 and deps.get(s, 0) < 16 * gen:
            deps[s] = 16 * gen
        self._waits(e, deps, True)
        val = 16 * (gen + 1)
        sh = s[1]
        e.q.append(lambda h, o=out_ap, i=in_ap, sh=sh: h.dma_start(out=o, in_=i).then_inc(sh, 16))
        self.ninst += 1
        for t in reads:
            if t.r.get(s, 0) < val:
                t.r[s] = val
        for t in writes:
            t.w = (s, val)
            t.r = {}

    def realias(self, new_ts, old_ts):
        merged = {}
        for t in old_ts:
            if t.w is not None and merged.get(t.w[0], 0) < t.w[1]:
                merged[t.w[0]] = t.w[1]
            for k, v in t.r.items():
                if merged.get(k, 0) < v:
                    merged[k] = v
        for t in new_ts:
            t.w = None
            t.r = dict(merged)

    def wait_all(self, eng, ts):
        e = self.engs[eng]
        self._waits(e, self._deps(ts, ts), True)

    def finish(self):
        nc = self.nc
        with nc.Block() as block:
            def mk(e):
                def f(h):
                    for c in e.q:
                        c(h)
                return f
            block.tensor(mk(self.engs["pe"]))
            block.scalar(mk(self.engs["act"]))
            block.vector(mk(self.engs["dve"]))
            block.gpsimd(mk(self.engs["pool"]))
            block.sync(mk(self.engs["sp"]))
        for cm in reversed(self._ctx):
            cm.__exit__(None, None, None)
        self._ctx = []


def MM(P, psT, out, lhsT, rhs, start, stop, reads):
    P.op("pe", lambda h: h.matmul(out, lhsT, rhs, start=start, stop=stop), reads=reads, writes=[psT], sig=True)


def ACT(P, out, in_, func, reads, writes, bias=None, scale=None):
    kw = {}
    if bias is not None:
        kw["bias"] = bias
    if scale is not None:
        kw["scale"] = scale
    P.op("act", lambda h: h.activation(out, in_, func, **kw), reads=reads, writes=writes)


def TT(P, eng, out, in0, in1, op, reads, writes):
    P.op(eng, lambda h: h.tensor_tensor(out, in0, in1, op), reads=reads, writes=writes)


def TS(P, eng, out, in0, s1, s2, op0, op1, reads, writes):
    if s2 is None:
        P.op(eng, lambda h: h.tensor_scalar(out, in0, s1, None, op0), reads=reads, writes=writes)
    else:
        P.op(eng, lambda h: h.tensor_scalar(out, in0, s1, s2, op0, op1), reads=reads, writes=writes)


def STT(P, eng, out, in0, sc, in1, op0, op1, reads, writes):
    P.op(eng, lambda h: h.scalar_tensor_tensor(out, in0, sc, in1, op0, op1), reads=reads, writes=writes)


def CP(P, eng, out, in_, reads, writes):
    if eng == "act":
        P.op("act", lambda h: h.copy(out, in_), reads=reads, writes=writes)
    else:
        P.op(eng, lambda h: h.tensor_copy(out, in_), reads=reads, writes=writes)


def SCAN(P, out, d0, d1, init, reads, writes):
    P.op("dve", lambda h: h.tensor_tensor_scan(out, d0, d1, init, ALU.mult, ALU.add), reads=reads, writes=writes)


def rev(ap2d):
    n = ap2d.shape[1]
    last = ap2d[:, n - 1:n]
    return AP(last.tensor, last.offset, [list(last.ap[0]), [-1, n]])


def pieces(lo, hi, step=512):
    n = -(-(hi - lo) // step)
    size = -(-(hi - lo) // n)
    size += size % 2
    out = []
    c = lo
    while c < hi:
        out.append((c, min(hi, c + size)))
        c += size
    return out


def small_layout():
    off = {}
    o = 0
    for nm, n in (("modb", 96), ("ln1g", 16), ("ln1b", 16), ("ln2g", 16), ("ln2b", 16), ("rcw", 16), ("rcb", 4),
                  ("rba", 8), ("rbx", 8), ("rlam", 8), ("scw", 12), ("scb", 4), ("fcw", 132), ("fcb", 44)):
        off[nm] = o
        o += n
    return off, o


SOFF, NS = small_layout()


def build_program(NB, layers=2, stop=None, order=None):
    nc = bass.Bass("TRN2", target_bir_lowering=False)
    NJ = NB + 1

    def din(name, shape, dt=F32):
        return nc.dram_tensor(name, list(shape), dt, kind="ExternalInput").ap()

    xT = din("xT", [NB, KC, 128, SEQ])
    ctxT = din("ctxT", [NB, KC, 128, CT])
    cT = din("cT", [128, KC * NJ])
    smallp = din("smallp", [128, NS])
    modw = din("modw", [2 * 48 * 128, 1024])
    gw_in = din("gw", [128, 16 * 128])
    tmid = din("tmid", [16 * 128, 896])
    tfull = din("tfull", [16 * 128, 896])
    wspec = [("win", 20, 1024), ("wout0", 8, 1024), ("wg0", FC, 1024), ("wu0", FC, 1024), ("wd0", 8, DFF),
             ("wqkv", 24, 1024), ("wout1", 8, 1024), ("wg1", FC, 1024), ("wu1", FC, 1024), ("wd1", 8, DFF)]
    wsrc = {}
    wscr = {}
    wscrT = {}
    for nm, nmc, ncol in wspec:
        wsrc[nm] = din(nm, [nmc * 128, ncol])
        wscr[nm] = nc.dram_tensor(nm + "_s", [nmc * 128, ncol], BF16, kind="Internal").ap()
        wscrT[nm] = [T(wscr[nm][m * 128:(m + 1) * 128, :]) for m in range(nmc)]
    yT = nc.dram_tensor("yT", [NB, KC, 128, SEQ], F32, kind="ExternalOutput").ap()

    P = Prog(nc)
    Xs = P.sbuf("X", [128, KC * NT], F32)
    Hs = P.sbuf("H", [128, KC * NT], BF16)
    UBYTES = 81920
    Us = P.sbuf("U", [128, UBYTES // 4], F32)
    X3 = Xs[:].rearrange("p (c t) -> p c t", c=KC)
    H3 = Hs[:].rearrange("p (c t) -> p c t", c=KC)
    REG = [(0, CT), (CT, CT + 1024), (CT + 1024, NT)]
    XR = [[T(X3[:, c, lo:hi]) for (lo, hi) in REG] for c in range(KC)]
    HR = [[T(H3[:, c, lo:hi]) for (lo, hi) in REG] for c in range(KC)]

    def xt(mc, c0, c1):
        return [XR[mc][r] for r, (lo, hi) in enumerate(REG) if c0 < hi and c1 > lo]

    def ht1(mc, c0, c1):
        return [HR[mc][r] for r, (lo, hi) in enumerate(REG) if c0 < hi and c1 > lo]

    def ht(c0, c1):
        return [t for mc in range(KC) for t in ht1(mc, c0, c1)]
    WA = [T(P.sbuf("wa%d" % i, [128, 1024], BF16)[:]) for i in range(3)]
    wa_i = [0]
    smallT = T(P.sbuf("smallp_sb", [128, NS], F32)[:])
    cvec = T(P.sbuf("cvec", [128, 4], F32)[:])
    sttA = T(P.sbuf("sttA", [128, 2], F32)[:])
    sttB = T(P.sbuf("sttB", [128, 2], F32)[:])
    onesD = T(P.sbuf("onesD", [128, 128], F32)[:])
    onesB = T(P.sbuf("onesB", [128, 128], BF16)[:])
    scT = T(P.sbuf("scT", [128, KC * NJ], F32)[:])
    MODS = T(P.sbuf("MODS", [128, 2 * NJ * 48], F32)[:])
    DER = T(P.sbuf("DER", [128, 12 * NJ * 8], F32)[:])
    CLAM = T(P.sbuf("CLAM", [128, 8], F32)[:])
    GW = T(P.sbuf("GW", [128, 16 * 128], BF16)[:])
    banks = [T(P.psum("ps%d" % i, [128, 512], F32)[:]) for i in range(8)]
    bank_i = [0]

    def nb_():
        t = banks[bank_i[0] % 8]
        bank_i[0] += 1
        return t

    def ucarve(off_bytes, n, dt):
        if dt == F32:
            return Us[:, off_bytes // 4: off_bytes // 4 + n]
        return Us[:, off_bytes // 4: off_bytes // 4 + (n + 1) // 2].bitcast(BF16)[:, :n]

    u_live = []

    def uphase(specs):
        new = [T(ucarve(o, n, dt)) for (o, n, dt) in specs]
        P.realias(new, u_live)
        u_live[:] = new
        return new

    sm = smallT.ap

    def scol(nm, i):
        o = SOFF[nm] + i
        return sm[:, o:o + 1]

    def mods(l, j, q):
        o = (l * NJ + j) * 48 + q
        return MODS.ap[:, o:o + 1]

    def der(k, j, c):
        o = (k * NJ + j) * 8 + c
        return DER.ap[:, o:o + 1]

    worder = list(order) if order is not None else []
    wpos = {k: i for i, k in enumerate(worder)}
    wstate = dict(done=0, seen=[])
    LOOK = 12

    def prepass_upto(k):
        while wstate["done"] < min(k, len(worder)):
            nm, m = worder[wstate["done"]]
            P.dma("pool", wscr[nm][m * 128:(m + 1) * 128, :], wsrc[nm][m * 128:(m + 1) * 128, :], writes=[wscrT[nm][m]])
            wstate["done"] += 1
    if order is None:
        for nm, nmc, ncol in wspec:
            if layers == 1 and nm in ("wqkv", "wout1", "wg1", "wu1", "wd1"):
                continue
            for m in range(nmc):
                P.dma("pool", wscr[nm][m * 128:(m + 1) * 128, :], wsrc[nm][m * 128:(m + 1) * 128, :], writes=[wscrT[nm][m]])
    P.dma("sp", smallT.ap, smallp, writes=[smallT])
    P.dma("sp", scT.ap, cT, writes=[scT])
    P.dma("pool", GW.ap, gw_in, writes=[GW])
    prepass_upto(LOOK)
    P.op("dve", lambda h: h.memset(cvec.ap[:, 0:1], 1.0), writes=[cvec])
    P.op("dve", lambda h: h.memset(cvec.ap[:, 1:2], EPS2), writes=[cvec])
    P.op("dve", lambda h: h.memset(cvec.ap[:, 2:3], 0.0), writes=[cvec])
    P.op("dve", lambda h: h.memset(onesD.ap, 1.0 / D), writes=[onesD])
    P.op("dve", lambda h: h.memset(onesB.ap, 1.0), writes=[onesB])
    ONE = cvec.ap[:, 0:1]
    EPSc = cvec.ap[:, 1:2]
    ACT(P, scT.ap, scT.ap, AF.Silu, [scT], [scT])
    ACT(P, CLAM.ap, sm[:, SOFF["rlam"]:SOFF["rlam"] + 8], AF.Exp, [smallT], [CLAM], scale=-1.0)
    TS(P, "dve", CLAM.ap, CLAM.ap, 1.0, None, ALU.add, None, [CLAM], [CLAM])
    ACT(P, CLAM.ap, CLAM.ap, AF.Ln, [CLAM], [CLAM])
    TS(P, "dve", CLAM.ap, CLAM.ap, -8.0, None, ALU.mult, None, [CLAM], [CLAM])
    mwb = uphase([(i * 2048, 1024, BF16) for i in range(6)])
    scTb = T(P.sbuf("scTb", [128, KC * NJ], BF16)[:])
    CP(P, "dve", scTb.ap, scT.ap, [scT], [scTb])
    sc3 = scTb.ap.rearrange("p (c j) -> p c j", c=KC)
    for l in range(layers):
        for q in range(48):
            wt = mwb[(l * 48 + q) % 6]
            r0 = (l * 48 + q) * 128
            P.dma("pool", wt.ap, modw[r0:r0 + 128, :], writes=[wt])
            ps = nb_()
            w3 = wt.ap.rearrange("p (k m) -> p k m", k=KC)
            for kc in range(KC):
                MM(P, ps, ps.ap[:, 0:NJ], w3[:, kc, :], sc3[:, kc, :], kc == 0, kc == KC - 1, [wt, scTb])
            o0 = l * NJ * 48 + q
            outap = AP(MODS.ap.tensor, MODS.ap[:, o0:o0 + 1].offset, [list(MODS.ap.ap[0]), [48, NJ]])
            TS(P, "dve", outap, ps.ap[:, 0:NJ], scol("modb", l * 48 + q), None, ALU.add, None, [ps, smallT], [MODS])
    for l in range(layers):
        for j in range(NJ):
            def M8(which):
                o = (l * NJ + j) * 48 + which * 8
                return MODS.ap[:, o:o + 8]

            def D8(k):
                o = (k * NJ + j) * 8
                return DER.ap[:, o:o + 8]
            ln1g = sm[:, SOFF["ln1g"] + l * 8: SOFF["ln1g"] + l * 8 + 8]
            ln1b = sm[:, SOFF["ln1b"] + l * 8: SOFF["ln1b"] + l * 8 + 8]
            TS(P, "dve", D8(0 + l), M8(1), 1.0, None, ALU.add, None, [MODS], [DER])
            TS(P, "dve", D8(2 + l), M8(2), 1.0 / ALPHA, None, ALU.mult, None, [MODS], [DER])
            TS(P, "dve", D8(4 + l), M8(5), 1.0 / ALPHA, None, ALU.mult, None, [MODS], [DER])
            TS(P, "dve", D8(8 + l), M8(4), 1.0, None, ALU.add, None, [MODS], [DER])
            TT(P, "pool", D8(6 + l), D8(8 + l), ln1g, ALU.mult, [DER, smallT], [DER])
            TT(P, "pool", D8(8 + l), D8(8 + l), ln1b, ALU.mult, [DER, smallT], [DER])
            TT(P, "dve", D8(8 + l), D8(8 + l), M8(3), ALU.add, [DER, MODS], [DER])
    if layers == 2:
        for j in range(NJ):
            ln2g = sm[:, SOFF["ln2g"]: SOFF["ln2g"] + 8]
            ln2b = sm[:, SOFF["ln2b"]: SOFF["ln2b"] + 8]
            A1n = DER.ap[:, (1 * NJ + j) * 8:(1 * NJ + j) * 8 + 8]
            SH1n = MODS.ap[:, (1 * NJ + j) * 48: (1 * NJ + j) * 48 + 8]
            g1p = DER.ap[:, (10 * NJ + j) * 8:(10 * NJ + j) * 8 + 8]
            b1p = DER.ap[:, (11 * NJ + j) * 8:(11 * NJ + j) * 8 + 8]
            TT(P, "pool", g1p, A1n, ln2g, ALU.mult, [DER, smallT], [DER])
            TT(P, "pool", b1p, A1n, ln2b, ALU.mult, [DER, smallT], [DER])
            TT(P, "dve", b1p, b1p, SH1n, ALU.add, [DER, MODS], [DER])

    def load_w(nm, m, ncol=1024, buf=None):
        if (nm, m) not in wstate["seen"]:
            wstate["seen"].append((nm, m))
        if order is not None:
            prepass_upto(wpos[(nm, m)] + 1 + LOOK)
        if buf is None:
            buf = WA[wa_i[0] % 3]
            wa_i[0] += 1
        P.dma("sp", buf.ap[:, 0:ncol], wscr[nm][m * 128:(m + 1) * 128, :], reads=[wscrT[nm][m]], writes=[buf])
        return buf

    def proj_fm(wt, rhs3, rhsT, c0, c1, nk=KC):
        ps = nb_()
        w3 = wt.ap[:, 0:nk * 128].rearrange("p (k m) -> p k m", k=nk)
        for kc in range(nk):
            MM(P, ps, ps.ap[:, 0:c1 - c0], w3[:, kc, :], rhs3[:, kc, c0:c1], kc == 0, kc == nk - 1, [wt] + rhsT)
        return ps

    def seqs_of(b, with_ctx):
        s = []
        if with_ctx:
            s.append((0, CT, NB))
        s.append((CT, NT, b))
        return s

    def layer_norm_piece(c0, c1, j, l, which, lnT, nextmod, hskip=0):
        n = c1 - c0
        SQ, MS, RS = lnT
        psm = nb_()
        pse = nb_()
        for mc in range(KC):
            sq = SQ[mc % 2]
            ACT(P, sq.ap[:, 0:n], X3[:, mc, c0:c1], AF.Square, xt(mc, c0, c1), [sq])
            MM(P, psm, psm.ap[:, 0:n], onesD.ap, X3[:, mc, c0:c1], mc == 0, mc == KC - 1, [onesD] + xt(mc, c0, c1))
            MM(P, pse, pse.ap[:, 0:n], onesD.ap, sq.ap[:, 0:n], mc == 0, mc == KC - 1, [onesD, sq])
        ACT(P, MS.ap[:, 0:n], psm.ap[:, 0:n], AF.Square, [psm], [MS])
        TT(P, "dve", MS.ap[:, 0:n], pse.ap[:, 0:n], MS.ap[:, 0:n], ALU.subtract, [pse, MS], [MS])
        ACT(P, MS.ap[:, 0:n], MS.ap[:, 0:n], AF.Sqrt, [MS, cvec], [MS], bias=EPSc, scale=1.0)
        P.op("dve", lambda h: h.reciprocal(RS.ap[:, 0:n], MS.ap[:, 0:n]), reads=[MS], writes=[RS])
        gname = "ln1g" if which == 1 else "ln2g"
        bname = "ln1b" if which == 1 else "ln2b"
        for mc in range(KC):
            xs = X3[:, mc, c0:c1]
            TT(P, "dve", xs, xs, psm.ap[:, 0:n], ALU.subtract, xt(mc, c0, c1) + [psm], xt(mc, c0, c1))
            TT(P, "pool", xs, xs, RS.ap[:, 0:n], ALU.mult, xt(mc, c0, c1) + [RS], xt(mc, c0, c1))
            if nextmod is not None:
                gk, bk = nextmod
                ACT(P, H3[:, mc, c0:c1 - hskip], X3[:, mc, c0:c1 - hskip], AF.Identity, xt(mc, c0, c1) + [DER], ht1(mc, c0, c1), bias=der(bk, j, mc), scale=der(gk, j, mc))
            ACT(P, xs, xs, AF.Identity, xt(mc, c0, c1) + [smallT], xt(mc, c0, c1), bias=scol(bname, l * 8 + mc), scale=scol(gname, l * 8 + mc))

    def proj_resid_ln(b, l, which, wname, nk, rhs3, rhsT, seqs, col_shift, lnT, nextmod, wbufs=None, mc_order=None):
        gk = (2 if which == 1 else 4) + l
        for (lo, hi, j) in seqs:
            for (c0, c1) in pieces(lo, hi):
                for mc in (mc_order or range(KC)):
                    if wbufs is None:
                        wt = load_w(wname, mc)
                    else:
                        wt = load_w(wname, mc, ncol=nk * 128, buf=wbufs[mc % 2])
                    ps = nb_()
                    w3 = wt.ap[:, 0:nk * 128].rearrange("p (k m) -> p k m", k=nk)
                    for kc in range(nk):
                        MM(P, ps, ps.ap[:, 0:c1 - c0], w3[:, kc, :], rhs3[:, kc, c0 - col_shift:c1 - col_shift],
                           kc == 0, kc == nk - 1, [wt] + rhsT)
                    xs = X3[:, mc, c0:c1]
                    STT(P, "dve", xs, ps.ap[:, 0:c1 - c0], der(gk, j, mc), xs, ALU.mult, ALU.add, [ps, DER] + xt(mc, c0, c1), xt(mc, c0, c1))
                layer_norm_piece(c0, c1, j, l, which, lnT, nextmod)

    def conv_seq(dst, src, dstT, srcT, lo, hi, dlo, slo, wname, wbase, K, padl, bias_ap, seq_lo, seq_hi, eng="dve", center="dve"):
        ctr = padl
        if center == "act":
            ACT(P, dst[:, lo - dlo:hi - dlo], src[:, lo - slo:hi - slo], AF.Identity, [srcT, smallT], [dstT], bias=bias_ap, scale=scol(wname, wbase + ctr))
        else:
            TS(P, eng, dst[:, lo - dlo:hi - dlo], src[:, lo - slo:hi - slo], scol(wname, wbase + ctr), bias_ap, ALU.mult, ALU.add,
               [srcT, smallT], [dstT])
        for k in range(K):
            if k == ctr:
                continue
            o = k - padl
            t0 = max(lo, seq_lo - o)
            t1 = min(hi, seq_hi - o)
            if t1 <= t0:
                continue
            STT(P, eng, dst[:, t0 - dlo:t1 - dlo], src[:, t0 + o - slo:t1 + o - slo], scol(wname, wbase + k), dst[:, t0 - dlo:t1 - dlo],
                ALU.mult, ALU.add, [srcT, smallT, dstT], [dstT])

    def ffn_stage(b, l, seqs_super, nextmod):
        specs = [(fc * 2048, 1024, BF16) for fc in range(FC)]
        specs += [(45056, 1056, F32), (49280, 1024, F32), (53376, DFF, BF16), (59008, DFF, BF16)]
        specs += [(64640 + i * 2048, 512, F32) for i in range(4)]
        ts = uphase(specs)
        A = ts[0:FC]
        G, CV = ts[FC], ts[FC + 1]
        WD = ts[FC + 2:FC + 4]
        lnT = (ts[FC + 4:FC + 6], ts[FC + 6], ts[FC + 7])
        wg, wu, wd = "wg%d" % l, "wu%d" % l, "wd%d" % l
        patch = []
        for (lo, hi, j, seq_lo, seq_hi) in seqs_super:
            glo = max(seq_lo, lo - 1)
            ghi = min(seq_hi, hi + 1)
            for fc in range(FC):
                wtg = load_w(wg, fc)
                for (c0, c1) in pieces(glo, ghi):
                    ps = proj_fm(wtg, H3, ht(c0, c1), c0, c1)
                    CP(P, "act", G.ap[:, c0 - glo:c1 - glo], ps.ap[:, 0:c1 - c0], [ps], [G])
                conv_seq(CV.ap, G.ap, CV, G, lo, hi, lo, glo, "fcw", (l * FC + fc) * 3, 3, 1, scol("fcb", l * FC + fc), seq_lo, seq_hi)
                ACT(P, CV.ap[:, 0:hi - lo], CV.ap[:, 0:hi - lo], AF.Silu, [CV], [CV])
                wtu = load_w(wu, fc)
                for (c0, c1) in pieces(lo, hi):
                    ps = proj_fm(wtu, H3, ht(c0, c1), c0, c1)
                    TT(P, "dve", A[fc].ap[:, c0 - lo:c1 - lo], ps.ap[:, 0:c1 - c0], CV.ap[:, c0 - lo:c1 - lo], ALU.mult, [ps, CV], [A[fc]])
            A3 = AP(A[0].ap.tensor, A[0].ap.offset, [list(A[0].ap.ap[0]), [1024, FC], [1, 1024]])
            gk = 4 + l
            for (c0, c1) in pieces(lo, hi):
                for mc in range(KC):
                    wt = load_w(wd, mc, ncol=DFF, buf=WD[mc % 2])
                    ps = nb_()
                    w3 = wt.ap.rearrange("p (k m) -> p k m", k=FC)
                    for kc in range(FC):
                        MM(P, ps, ps.ap[:, 0:c1 - c0], w3[:, kc, :], A3[:, kc, c0 - lo:c1 - lo], kc == 0, kc == FC - 1, [wt, A[kc]])
                    xs = X3[:, mc, c0:c1]
                    STT(P, "dve", xs, ps.ap[:, 0:c1 - c0], der(gk, j, mc), xs, ALU.mult, ALU.add, [ps, DER] + xt(mc, c0, c1), xt(mc, c0, c1))
                more = any((s2[3] == seq_lo and s2[0] == hi) for s2 in seqs_super)
                hs = 1 if (nextmod is not None and c1 == hi and more) else 0
                layer_norm_piece(c0, c1, j, l, 2, lnT, nextmod, hskip=hs)
                if l == 1:
                    for mc in range(KC):
                        P.dma("sp" if mc % 2 == 0 else "pool", yT[b, mc][:, c0 - CT:c1 - CT], X3[:, mc, c0:c1], reads=xt(mc, c0, c1))
                if hs:
                    patch.append((hi - 1, j))
        if patch:
            patch_h(patch, l + 1)

    def patch_h(patch, lnext):
        for (col, j) in patch:
            for mc in range(KC):
                ACT(P, H3[:, mc, col:col + 1], X3[:, mc, col:col + 1], AF.Identity, xt(mc, col, col + 1) + [DER, MODS], ht1(mc, col, col + 1),
                    bias=mods(lnext, j, mc), scale=der(0 + lnext, j, mc))

    def even_mixer(b):
        specs = [(c * 4608, NT, BF16) for c in range(KC)]
        specs += [(36864 + i * 9216, NT, F32) for i in range(3)]
        specs += [(64512, NT, BF16)]
        specs += [(69120 + i * 2048, 512, F32) for i in range(4)]
        ts = uphase(specs)
        YC = ts[0:KC]
        setA = (ts[KC], ts[KC + 1], ts[KC + 2], ts[KC + 3], ts[KC + 4:KC + 7], sttA)
        TM = ts[KC + 4:KC + 8]
        def xcarve(off, n, dt):
            if dt == F32:
                return Xs[:, off // 4: off // 4 + n]
            return Xs[:, off // 4: off // 4 + (n + 1) // 2].bitcast(BF16)[:, :n]
        xs_specs = [(i * 9216, NT, F32) for i in range(3)] + [(27648, NT, BF16)] + [(32256 + i * 2048, 512, F32) for i in range(2)]
        xs_specs += [(36864, NT, F32), (46080, NT, F32)]
        xsT = [T(xcarve(o, n, dt)) for (o, n, dt) in xs_specs]
        borrowB = [t for c in range(4) for t in XR[c]]
        borrowS = [t for c in (4, 5) for t in XR[c]]
        P.realias(xsT[0:6], borrowB)
        P.realias(xsT[6:8], borrowS)
        setB = (xsT[0], xsT[1], xsT[2], xsT[3], xsT[4:6], sttB)
        S0, S1 = xsT[6], xsT[7]

        def reload_x(chunks):
            for r, (lo, hi) in enumerate(REG):
                for c in chunks:
                    q = "sp" if c % 2 == 0 else "pool"
                    if r == 0:
                        P.dma(q, X3[:, c, 0:CT], ctxT[b, c], writes=[XR[c][0]])
                    else:
                        P.dma(q, X3[:, c, lo:hi], xT[b, c][:, lo - CT:hi - CT], writes=[XR[c][r]])
        seqs = seqs_of(b, True)
        allp = [pc for (lo, hi, j) in seqs for pc in pieces(lo, hi)]

        def rec_chunk(c, bufs, wbuf):
            R0, R1, R2, XCb, TMs, st = bufs
            tm_i = [0]

            def ntmp():
                t = TMs[tm_i[0] % len(TMs)]
                tm_i[0] += 1
                return t
            wt = load_w("win", c, buf=wbuf)
            for (c0, c1) in allp:
                ps = proj_fm(wt, H3, ht(c0, c1), c0, c1)
                CP(P, "act", R0.ap[:, c0:c1], ps.ap[:, 0:c1 - c0], [ps], [R0])
                yield
            for (lo, hi, j) in seqs:
                conv_seq(R1.ap, R0.ap, R1, R0, lo, hi, 0, 0, "rcw", c * 4, 4, 2, scol("rcb", c), lo, hi, center="act")
            yield
            CP(P, "pool", XCb.ap, R1.ap, [R1], [XCb])
            yield
            for d in range(2):
                Bd = R2 if d == 0 else R1
                for (c0, c1) in allp:
                    n = c1 - c0
                    psr = nb_()
                    MM(P, psr, psr.ap[:, 0:n], GW.ap[:, ((d * 2 + 0) * 4 + c) * 128:((d * 2 + 0) * 4 + c + 1) * 128], XCb.ap[:, c0:c1], True, True, [GW, XCb])
                    psi = nb_()
                    MM(P, psi, psi.ap[:, 0:n], GW.ap[:, ((d * 2 + 1) * 4 + c) * 128:((d * 2 + 1) * 4 + c + 1) * 128], XCb.ap[:, c0:c1], True, True, [GW, XCb])
                    ACT(P, R0.ap[:, c0:c1], psr.ap[:, 0:n], AF.Sigmoid, [psr, smallT], [R0], bias=scol("rba", d * 4 + c))
                    t1 = ntmp()
                    ACT(P, t1.ap[:, 0:n], psi.ap[:, 0:n], AF.Sigmoid, [psi, smallT], [t1], bias=scol("rbx", d * 4 + c))
                    if d == 0:
                        TT(P, "dve", Bd.ap[:, c0:c1], t1.ap[:, 0:n], R1.ap[:, c0:c1], ALU.mult, [t1, R1], [Bd])
                    else:
                        TT(P, "dve", Bd.ap[:, c0:c1], R1.ap[:, c0:c1], t1.ap[:, 0:n], ALU.mult, [t1, R1], [Bd])
                    yield
                ACT(P, R0.ap, R0.ap, AF.Exp, [R0, CLAM], [R0], scale=CLAM.ap[:, d * 4 + c:d * 4 + c + 1])
                yield
                for (c0, c1) in allp:
                    n = c1 - c0
                    a_ = R0.ap[:, c0:c1]
                    t2 = ntmp()
                    TT(P, "pool", t2.ap[:, 0:n], a_, a_, ALU.mult, [R0], [t2])
                    ACT(P, t2.ap[:, 0:n], t2.ap[:, 0:n], AF.Sqrt, [t2, cvec], [t2], bias=ONE, scale=-1.0)
                    TT(P, "dve", Bd.ap[:, c0:c1], Bd.ap[:, c0:c1], t2.ap[:, 0:n], ALU.mult, [Bd, t2], [Bd])
                    yield
                if d == 0:
                    SCAN(P, Bd.ap[:, 0:CT], R0.ap[:, 0:CT], Bd.ap[:, 0:CT], 0.0, [R0, Bd], [Bd])
                    CP(P, "act", st.ap[:, 0:1], Bd.ap[:, CT - 1:CT], [Bd], [st])
                    yield
                    SCAN(P, Bd.ap[:, CT:NT], R0.ap[:, CT:NT], Bd.ap[:, CT:NT], st.ap[:, 0:1], [R0, Bd, st], [Bd])
                else:
                    SCAN(P, rev(Bd.ap[:, 0:CT]), rev(R0.ap[:, 0:CT]), rev(Bd.ap[:, 0:CT]), 0.0, [R0, Bd], [Bd])
                    CP(P, "act", st.ap[:, 1:2], Bd.ap[:, 0:1], [Bd], [st])
                    yield
                    SCAN(P, rev(Bd.ap[:, CT:NT]), rev(R0.ap[:, CT:NT]), rev(Bd.ap[:, CT:NT]), st.ap[:, 1:2], [R0, Bd, st], [Bd])
                yield
            TT(P, "pool", R2.ap, R2.ap, R1.ap, ALU.add, [R2, R1], [R2])
            yield
            wt = load_w("win", 4 + c, buf=wbuf)
            for (c0, c1) in allp:
                n = c1 - c0
                ps = proj_fm(wt, H3, ht(c0, c1), c0, c1)
                t1 = ntmp()
                ACT(P, t1.ap[:, 0:n], ps.ap[:, 0:n], AF.Gelu_apprx_tanh, [ps], [t1])
                TT(P, "dve", YC[c].ap[:, c0:c1], t1.ap[:, 0:n], R2.ap[:, c0:c1], ALU.mult, [t1, R2], [YC[c]])
                yield

        def sc_chunk(c, wbuf):
            wt = load_w("win", 12 + c, buf=wbuf)
            for (c0, c1) in allp:
                ps = proj_fm(wt, H3, ht(c0, c1), c0, c1)
                CP(P, "act", S0.ap[:, c0:c1], ps.ap[:, 0:c1 - c0], [ps], [S0])
                yield
            wt = load_w("win", 16 + c, buf=wbuf)
            for (c0, c1) in allp:
                ps = proj_fm(wt, H3, ht(c0, c1), c0, c1)
                TT(P, "dve", S0.ap[:, c0:c1], ps.ap[:, 0:c1 - c0], S0.ap[:, c0:c1], ALU.mult, [ps, S0], [S0])
                yield
            for (lo, hi, j) in seqs:
                conv_seq(S1.ap, S0.ap, S1, S0, lo, hi, 0, 0, "scw", c * 3, 3, 1, scol("scb", c), lo, hi, center="act")
            yield
            wt = load_w("win", 8 + c, buf=wbuf)
            for (c0, c1) in allp:
                ps = proj_fm(wt, H3, ht(c0, c1), c0, c1)
                TT(P, "dve", YC[4 + c].ap[:, c0:c1], ps.ap[:, 0:c1 - c0], S1.ap[:, c0:c1], ALU.mult, [ps, S1], [YC[4 + c]])
                yield

        def chain(gs):
            for g in gs:
                yield from g
        gens = [("a", chain([rec_chunk(0, setA, WA[0]), rec_chunk(2, setA, WA[0])])),
                ("b", chain([rec_chunk(1, setB, WA[1]), rec_chunk(3, setB, WA[1])])),
                ("s", chain([sc_chunk(c, WA[2]) for c in range(4)]))]
        while gens:
            for item in list(gens):
                try:
                    next(item[1])
                except StopIteration:
                    gens.remove(item)
                    if item[0] == "s":
                        P.realias(borrowS, xsT[6:8])
                        reload_x([4, 5])
                    elif item[0] == "b":
                        P.realias(borrowB, xsT[0:6])
                        reload_x([0, 1, 2, 3])
        YC3 = AP(YC[0].ap.tensor, YC[0].ap.offset, [list(YC[0].ap.ap[0]), [NT, KC], [1, NT]])
        lnT = ([TM[0], TM[1]], TM[2], TM[3])
        proj_resid_ln(b, 0, 1, "wout0", KC, YC3, YC, seqs, 0, lnT, (6, 8), mc_order=[7, 6, 5, 4, 3, 2, 1, 0])

    def odd_mixer(b):
        specs = [(c * 4096, SEQ, BF16) for c in range(KC)]
        specs += [(32768, SEQ, BF16), (36864, SEQ, BF16), (40960, NT, BF16), (45568, 18 * 192, BF16)]
        specs += [(52480 + i * 3584, 896, F32) for i in range(4)]
        specs += [(66816 + i * 2048, 512, F32) for i in range(2)]
        specs += [(70912 + i * 1024, 512, BF16) for i in range(8)]
        specs += [(79104 + i * 1024, 256, F32) for i in range(2)]
        ts = uphase(specs)
        OT = ts[0:KC]
        QZ0, QZ1, KT, V = ts[KC:KC + 4]
        TB = ts[KC + 4:KC + 8]
        E = ts[KC + 8:KC + 10]
        PT = ts[KC + 10:KC + 18]
        RC = [ts[KC + 18], ts[KC + 18]]
        OS = ts[KC + 19]
        P.op("pool", lambda h: h.memset(QZ0.ap[64:128, :], 0.0), writes=[QZ0])
        P.op("pool", lambda h: h.memset(QZ1.ap[0:64, :], 0.0), writes=[QZ1])
        V3 = V.ap.rearrange("p (t m) -> p t m", t=18)
        P.op("pool", lambda h: h.memset(V3[:, :, 64:128], 1.0), writes=[V])
        cnt = dict(e=0, p=0, r=0)
        j = b
        for hp in range(KC):
            wq = load_w("wqkv", hp)
            for (c0, c1) in pieces(CT, NT):
                ps = proj_fm(wq, H3, ht(c0, c1), c0, c1)
                CP(P, "act", QZ0.ap[0:64, c0 - CT:c1 - CT], ps.ap[0:64, 0:c1 - c0], [ps], [QZ0])
                CP(P, "dve", QZ1.ap[64:128, c0 - CT:c1 - CT], ps.ap[64:128, 0:c1 - c0], [ps], [QZ1])
            wk = load_w("wqkv", 8 + hp)
            for (c0, c1) in pieces(0, NT):
                ps = proj_fm(wk, H3, ht(c0, c1), c0, c1)
                CP(P, "dve", KT.ap[:, c0:c1], ps.ap[:, 0:c1 - c0], [ps], [KT])
            wv = load_w("wqkv", 16 + hp)
            wv3 = wv.ap.rearrange("p (k m) -> p k m", k=KC)
            for t0 in range(0, 18, 4):
                nt_ = min(4, 18 - t0)
                ps = nb_()
                for ti in range(nt_):
                    tt = t0 + ti
                    for kc in range(KC):
                        MM(P, ps, ps.ap[:, ti * 128:(ti + 1) * 128], H3[:, kc, tt * 128:(tt + 1) * 128], wv3[:, kc, :],
                           kc == 0, kc == KC - 1, [wv] + ht(tt * 128, (tt + 1) * 128))
                ps3 = ps.ap[:, 0:nt_ * 128].rearrange("p (t m) -> p t m", t=nt_)
                CP(P, "act", V3[:, t0:t0 + nt_, 0:64], ps3[:, :, 0:64], [ps], [V])
                CP(P, "dve", V3[:, t0:t0 + nt_, 128:192], ps3[:, :, 64:128], [ps], [V])
            for par in range(2):
                h_ = 2 * hp + par
                P.dma("sp", TB[par * 2].ap, tmid[h_ * 128:(h_ + 1) * 128, :], writes=[TB[par * 2]])
                P.dma("sp", TB[par * 2 + 1].ap, tfull[h_ * 128:(h_ + 1) * 128, :], writes=[TB[par * 2 + 1]])
                ACT(P, TB[par * 2].ap, TB[par * 2].ap, AF.Exp, [TB[par * 2]], [TB[par * 2]])
                ACT(P, TB[par * 2 + 1].ap, TB[par * 2 + 1].ap, AF.Exp, [TB[par * 2 + 1]], [TB[par * 2 + 1]])
            items = [(par, r0) for par in range(2) for r0 in range(0, 32, 4)]

            def s_phase(par, r0):
                pb = par * 64
                q0 = r0 * 64
                if r0 == 0:
                    chunks, kind = [0, 1, 2, 3], 1
                elif r0 == 28:
                    chunks, kind = [12, 13, 14, 15], 1
                else:
                    a0 = (r0 - 4) // 2
                    chunks, kind = list(range(a0, a0 + 6)), 0
                tb = TB[par * 2 + kind]
                QZ = QZ0 if par == 0 else QZ1
                pts = []
                for i in range(0, len(chunks), 2):
                    ps = nb_()
                    for k in range(2):
                        a = chunks[i + k]
                        MM(P, ps, ps.ap[:, k * 256:(k + 1) * 256], KT.ap[:, CT + a * 128:CT + (a + 1) * 128],
                           QZ.ap[:, q0:q0 + 256], True, True, [KT, QZ])
                    e = E[cnt["e"] % 2]
                    cnt["e"] += 1
                    ACT(P, e.ap, ps.ap, AF.Exp, [ps], [e], scale=0.125)
                    pt = PT[cnt["p"] % 8]
                    cnt["p"] += 1
                    ei0 = r0 - 2 * chunks[i] + 6
                    tap = AP(tb.ap.tensor, tb.ap[:, ei0 * 64:ei0 * 64 + 1].offset, [list(tb.ap.ap[0]), [-128, 2], [1, 256]])
                    TT(P, "pool" if (i // 2) == 1 else "dve", pt.ap.rearrange("p (a b) -> p a b", a=2), e.ap.rearrange("p (a b) -> p a b", a=2), tap, ALU.mult, [e, tb], [pt])
                    pts.append((pt, chunks[i], chunks[i + 1]))
                ps = nb_()
                for k in range(2):
                    MM(P, ps, ps.ap[:, k * 256:(k + 1) * 256], KT.ap[:, k * 128:(k + 1) * 128],
                       QZ.ap[:, q0:q0 + 256], True, True, [KT, QZ])
                ptc = PT[cnt["p"] % 8]
                cnt["p"] += 1
                ACT(P, ptc.ap, ps.ap, AF.Exp, [ps], [ptc], scale=0.125)
                return (par, r0, pts, ptc)

            def pv_phase(st):
                par, r0, pts, ptc = st
                pb = par * 64
                q0 = r0 * 64
                pso = nb_()
                ob = 64 - pb
                mms = []
                for (pt, a, a2) in pts:
                    mms.append((V3[:, 2 + a, pb:pb + 128], pt.ap[:, 0:256], pt))
                    mms.append((V3[:, 2 + a2, pb:pb + 128], pt.ap[:, 256:512], pt))
                mms.append((V3[:, 0, pb:pb + 128], ptc.ap[:, 0:256], ptc))
                mms.append((V3[:, 1, pb:pb + 128], ptc.ap[:, 256:512], ptc))
                nm_ = len(mms)
                for i, (l_, r_, pt) in enumerate(mms):
                    MM(P, pso, pso.ap[:, 0:256], l_, r_, i == 0, i == nm_ - 1, [V, pt])
                rl, rc = OS, RC[0]
                ACT(P, rl.ap[pb:pb + 64, :], pso.ap[ob:ob + 64, 0:256], AF.Ln, [pso], [rl])
                CP(P, "dve", rc.ap[pb:pb + 64, :], rl.ap[pb:pb + 64, :], [rl], [rc])
                ACT(P, rc.ap[pb:pb + 64, :], rc.ap[pb:pb + 64, :], AF.Exp, [rc], [rc], scale=-1.0)
                TT(P, "dve", OT[hp].ap[pb:pb + 64, q0:q0 + 256], pso.ap[pb:pb + 64, 0:256], rc.ap[pb:pb + 64, :], ALU.mult, [pso, rc], [OT[hp]])

            prev = None
            for (par, r0) in items:
                st = s_phase(par, r0)
                if prev is not None:
                    pv_phase(prev)
                prev = st
            pv_phase(prev)
        OT3 = AP(OT[0].ap.tensor, OT[0].ap.offset, [list(OT[0].ap.ap[0]), [SEQ, KC], [1, SEQ]])
        lnA, lnB = T(ucarve(70912, 512, F32)), T(ucarve(72960, 512, F32))
        lnT = ([E[0], E[1]], lnA, lnB)
        P.realias([lnA, lnB], PT)
        u_live.extend([lnA, lnB])
        proj_resid_ln(b, 1, 1, "wout1", KC, OT3, OT, seqs_of(b, False), CT, lnT, (6 + 1, 8 + 1))

    for b in range(NB):
        for c in range(KC):
            q = "sp" if c % 2 == 0 else "pool"
            P.dma(q, X3[:, c, 0:CT], ctxT[b, c], writes=[XR[c][0]])
            P.dma(q, X3[:, c, CT:CT + 1024], xT[b, c][:, 0:1024], writes=[XR[c][1]])
            P.dma(q, X3[:, c, CT + 1024:NT], xT[b, c][:, 1024:2048], writes=[XR[c][2]])
        for c in range(KC):
            for r, (lo, hi) in enumerate(REG):
                j = NB if r == 0 else b
                ACT(P, H3[:, c, lo:hi], X3[:, c, lo:hi], AF.Identity, [XR[c][r], DER, MODS], [HR[c][r]], bias=mods(0, j, c), scale=der(0, j, c))
        if stop != "load":
            even_mixer(b)
        nm0 = (10, 11) if layers == 2 else None
        if stop is None:
            ffn_stage(b, 0, [(0, CT, NB, 0, CT), (CT, CT + 1024, b, CT, NT), (CT + 1024, NT, b, CT, NT)], nm0)

        if layers == 2:
            odd_mixer(b)
            ffn_stage(b, 1, [(CT, CT + 1024, b, CT, NT), (CT + 1024, NT, b, CT, NT)], None)
        if not (layers == 2 and stop is None):
            for c in range(KC):
                q = "sp" if c % 2 == 0 else "pool"
                P.dma(q, yT[b, c], X3[:, c, CT:NT], reads=[XR[c][1], XR[c][2]])
    allx = [t for c in range(KC) for t in XR[c]]
    P.wait_all("sp", allx)
    P.worder = wstate["seen"]
    P.finish()
    return nc, P


def _wt(W):
    K, M = W.shape
    return np.ascontiguousarray(W.reshape(K // 128, 128, M // 128, 128).transpose(2, 1, 0, 3)).reshape(M, K)


def _pc(v, nchunk):
    v = np.asarray(v, np.float32)
    lead = v.shape[:-1]
    return np.moveaxis(v.reshape(lead + (nchunk, 128)), -1, 0)


def _tables(rpb):
    kr2 = np.arange(2)[:, None, None, None]
    kcol = np.arange(64)[None, :, None, None]
    e = (np.arange(14) - 6)[None, None, :, None]
    qcol = np.arange(64)[None, None, None, :]
    dr = kr2 - e + 7 + 0 * kcol + 0 * qcol
    dc = kcol - qcol + 15 + 0 * kr2 + 0 * e
    cs = np.clip(qcol - 8, 0, 48)
    colok = (kcol >= cs) & (kcol < cs + 16)
    drok = (dr >= 0) & (dr <= 14)
    rowmid = (kr2 - e >= -4) & (kr2 - e <= 3)
    g = rpb[:, np.clip(dr, 0, 14), np.clip(dc, 0, 30)]
    negs = np.full(g.shape, NEG, np.float32)
    tfull = np.where((colok & drok)[None], g, negs).reshape(16 * 128, 896)
    tmid = np.where((colok & drok & rowmid)[None], g, negs).reshape(16 * 128, 896)
    return np.ascontiguousarray(tmid, np.float32), np.ascontiguousarray(tfull, np.float32)


def prep_shared(mod_w, mod_b, ln1_g, ln1_b, ln2_g, ln2_b, ev_w_in, ev_w_out, rec_conv_w, rec_conv_b, rec_wa, rec_ba,
                rec_wx, rec_bx, rec_lam, sc_conv_w, sc_conv_b, na_w_qkv, na_w_out, na_rpb, ffn_w_gate, ffn_w_up,
                ffn_conv_w, ffn_conv_b, ffn_w_down):
    sh = {}
    sh["win"] = _wt(ev_w_in[0])
    sh["wout0"] = _wt(ev_w_out[0])
    sh["wqkv"] = _wt(na_w_qkv[0])
    sh["wout1"] = _wt(na_w_out[0])
    for l in range(2):
        sh["wg%d" % l] = _wt(ffn_w_gate[l])
        sh["wu%d" % l] = _wt(ffn_w_up[l])
        sh["wd%d" % l] = _wt(ffn_w_down[l])
    sh["modw"] = np.concatenate([_wt(mod_w[l]) for l in range(2)], axis=0)
    small = np.zeros((128, NS), np.float32)

    def put(nm, arr):
        arr = np.asarray(arr, np.float32).reshape(128, -1)
        small[:, SOFF[nm]:SOFF[nm] + arr.shape[1]] = arr
    put("modb", _pc(mod_b, 48))
    put("ln1g", _pc(ln1_g, 8)); put("ln1b", _pc(ln1_b, 8)); put("ln2g", _pc(ln2_g, 8)); put("ln2b", _pc(ln2_b, 8))
    put("rcw", np.transpose(_pc(rec_conv_w[0], 4), (0, 2, 1)))
    put("rcb", _pc(rec_conv_b[0], 4))
    put("rba", _pc(rec_ba[0], 4)); put("rbx", _pc(rec_bx[0], 4)); put("rlam", _pc(rec_lam[0], 4))
    put("scw", np.transpose(_pc(sc_conv_w[0], 4), (0, 2, 1)))
    put("scb", _pc(sc_conv_b[0], 4))
    put("fcw", np.transpose(_pc(ffn_conv_w, FC), (0, 1, 3, 2)))
    put("fcb", _pc(ffn_conv_b, FC))
    sh["smallp"] = small
    gw = np.zeros((128, 16, 128), np.float32)
    for d in range(2):
        for wi, W in enumerate((rec_wa[0], rec_wx[0])):
            for c in range(4):
                g = (d * 2 + wi) * 4 + c
                gw[0:64, g, 0:64] = W[d, 2 * c]
                gw[64:128, g, 64:128] = W[d, 2 * c + 1]
    sh["gw"] = gw.reshape(128, 16 * 128)
    sh["tmid"], sh["tfull"] = _tables(np.asarray(na_rpb[0], np.float32))
    return sh


def prep_core(x, c, ctx, c_ctx, b0, NB):
    m = {}
    m["xT"] = np.ascontiguousarray(x[b0:b0 + NB].transpose(0, 2, 1)).reshape(NB, KC, 128, SEQ)
    m["ctxT"] = np.ascontiguousarray(ctx[b0:b0 + NB].transpose(0, 2, 1)).reshape(NB, KC, 128, CT)
    cc = np.concatenate([c[b0:b0 + NB], c_ctx[None, :]], axis=0)
    m["cT"] = np.ascontiguousarray(np.transpose(cc.reshape(NB + 1, KC, 128), (2, 1, 0))).reshape(128, KC * (NB + 1))
    return m


_CACHE = {}


def kernel(x, c, ctx, c_ctx, mod_w, mod_b, ln1_g, ln1_b, ln2_g, ln2_b, ev_w_in, ev_w_out, rec_conv_w, rec_conv_b,
           rec_wa, rec_ba, rec_wx, rec_bx, rec_lam, sc_conv_w, sc_conv_b, na_w_qkv, na_w_out, na_rpb, ffn_w_gate,
           ffn_w_up, ffn_conv_w, ffn_conv_b, ffn_w_down):
    f = lambda a: np.asarray(a, np.float32)
    x, c, ctx, c_ctx = f(x), f(c), f(ctx), f(c_ctx)
    B = x.shape[0]
    NB = B // NCORES
    sh = prep_shared(f(mod_w), f(mod_b), f(ln1_g), f(ln1_b), f(ln2_g), f(ln2_b), f(ev_w_in), f(ev_w_out), f(rec_conv_w),
                     f(rec_conv_b), f(rec_wa), f(rec_ba), f(rec_wx), f(rec_bx), f(rec_lam), f(sc_conv_w), f(sc_conv_b),
                     f(na_w_qkv), f(na_w_out), f(na_rpb), f(ffn_w_gate), f(ffn_w_up), f(ffn_conv_w), f(ffn_conv_b), f(ffn_w_down))
    if NB not in _CACHE:
        _, p0 = build_program(1)
        _CACHE[NB] = build_program(NB, order=p0.worder)[0]
    nc = _CACHE[NB]
    in_maps = []
    for i in range(NCORES):
        m = dict(sh)
        m.update(prep_core(x, c, ctx, c_ctx, i * NB, NB))
        in_maps.append(m)
    res = run_bass_kernel_spmd(nc, in_maps, core_ids=list(range(NCORES)))
    out = np.empty((B, SEQ, D), np.float32)
    for i in range(NCORES):
        y = np.asarray(res.results[i]["yT"]).reshape(NB, D, SEQ)
        out[i * NB:(i + 1) * NB] = y.transpose(0, 2, 1)
    return out
```
